# Optimizing a Trainium2 kernel written in Bass

```python
import jax, jax.numpy as jnp
from jax import lax
import numpy as np

D_MODEL = 2048
BATCH = 8
SEQ = 4096
DEPTH = 4

N_A_LAYERS = DEPTH // 2
N_B_LAYERS = DEPTH - N_A_LAYERS
D_FF = 4 * D_MODEL
NORM_EPS = 1e-6

GDN_DK = 128
GDN_DV = 128
GDN_NK = D_MODEL // 128
GDN_NV = 2 * GDN_NK
GDN_CONV_K = 4
GDN_CHUNK = 64
GDN_QK_DIM = GDN_NK * GDN_DK
GDN_V_DIM = GDN_NV * GDN_DV
GDN_CONV_DIM = 2 * GDN_QK_DIM + GDN_V_DIM
GDN_PROJ = GDN_CONV_DIM + GDN_V_DIM + 2 * GDN_NV

ATT_HD = 128
ATT_HQ = D_MODEL // ATT_HD
ATT_HKV = 4
DILATED_GROUPS = ((128, 1), (512, 4), (2048, 16))
N_GROUPS = len(DILATED_GROUPS)
ATT_Q_PROJ = N_GROUPS * ATT_HQ * ATT_HD
ATT_KV_PROJ = N_GROUPS * 2 * ATT_HKV * ATT_HD
ROPE_THETA = 500000.0
ROT_DIM = ATT_HD // 4

kernel_name = "yoco_gdn_dilated_hybrid"


def rms_norm(x, gain):
    xf = x.astype(jnp.float32)
    y = xf * lax.rsqrt(jnp.mean(xf * xf, axis=-1, keepdims=True) + NORM_EPS)
    return (y * gain.astype(jnp.float32)).astype(x.dtype)


def modulate(x, gain, shift, scale):
    return rms_norm(x, gain) * (1 + scale[:, None, :]) + shift[:, None, :]


def l2_normalize(t):
    return t * lax.rsqrt(jnp.sum(t * t, axis=-1, keepdims=True) + NORM_EPS)


def partial_rotary(t, positions):
    half = ROT_DIM // 2
    inv = ROPE_THETA ** (-jnp.arange(0, ROT_DIM, 2, dtype=jnp.float32) / ROT_DIM)
    ang = positions.astype(jnp.float32)[..., None] * inv
    cos = jnp.cos(ang)[:, :, None, :]
    sin = jnp.sin(ang)[:, :, None, :]
    t1 = t[..., :half].astype(jnp.float32)
    t2 = t[..., half:ROT_DIM].astype(jnp.float32)
    rot = jnp.concatenate([t1 * cos - t2 * sin, t2 * cos + t1 * sin], axis=-1).astype(t.dtype)
    return jnp.concatenate([rot, t[..., ROT_DIM:]], axis=-1)


def causal_depthwise_conv(x, w):
    k_len, ch = w.shape
    return lax.conv_general_dilated(
        x, w[:, None, :].astype(x.dtype), window_strides=(1,), padding=[(k_len - 1, 0)],
        dimension_numbers=("NWC", "WIO", "NWC"), feature_group_count=ch)


def chunk_gated_delta_rule(q, k, v, g, beta):
    B, H, S, DK = q.shape
    DV = v.shape[-1]
    C = GDN_CHUNK
    N = S // C
    q = (q * DK ** -0.5).reshape(B, H, N, C, DK)
    k = k.reshape(B, H, N, C, DK)
    v = v.reshape(B, H, N, C, DV)
    g = jnp.cumsum(g.reshape(B, H, N, C), axis=-1)
    beta = beta.reshape(B, H, N, C)
    idx = jnp.arange(C)
    causal = idx[:, None] >= idx[None, :]
    strict = idx[:, None] > idx[None, :]
    decay = jnp.exp(jnp.where(causal, g[..., :, None] - g[..., None, :], -jnp.inf))
    kb = k * beta[..., None]
    lmat = jnp.einsum('bhnid,bhnjd->bhnij', kb, k) * jnp.where(strict, decay, 0.0)
    tri = lmat + jnp.eye(C, dtype=lmat.dtype)
    rhs = jnp.concatenate([v * beta[..., None], kb * jnp.exp(g)[..., None]], axis=-1)
    sol = lax.linalg.triangular_solve(tri, rhs, left_side=True, lower=True, unit_diagonal=True)
    u = sol[..., :DV]
    w = sol[..., DV:]

    def step(state, inp):
        q_c, k_c, u_c, w_c, g_c, dec_c = inp
        v_new = u_c - jnp.einsum('bhck,bhkv->bhcv', w_c, state)
        intra = jnp.einsum('bhik,bhjk->bhij', q_c, k_c) * dec_c
        o_c = (jnp.einsum('bhck,bhkv->bhcv', q_c * jnp.exp(g_c)[..., None], state)
               + jnp.einsum('bhij,bhjv->bhiv', intra, v_new))
        g_last = g_c[..., -1:]
        state = (state * jnp.exp(g_last)[..., None]
                 + jnp.einsum('bhck,bhcv->bhkv', k_c * jnp.exp(g_last - g_c)[..., None], v_new))
        return state, o_c

    xs = tuple(jnp.moveaxis(t, 2, 0) for t in (q, k, u, w, g, decay))
    state0 = jnp.zeros((B, H, DK, DV), jnp.float32)
    _, o = lax.scan(step, state0, xs)
    return jnp.moveaxis(o, 0, 2).reshape(B, H, S, DV)


def gated_deltanet_mixer(h, w_in, conv_w, a_log, dt_bias, onorm_g, w_out):
    B, S, _ = h.shape
    proj = h @ w_in
    o1 = GDN_CONV_DIM
    o2 = o1 + GDN_V_DIM
    o3 = o2 + GDN_NV
    qkv = jax.nn.silu(causal_depthwise_conv(proj[..., :o1], conv_w)).astype(jnp.float32)
    z = proj[..., o1:o2].astype(jnp.float32).reshape(B, S, GDN_NV, GDN_DV)
    b_in = proj[..., o2:o3].astype(jnp.float32)
    a_in = proj[..., o3:].astype(jnp.float32)
    rep = GDN_NV // GDN_NK
    q = jnp.repeat(l2_normalize(qkv[..., :GDN_QK_DIM].reshape(B, S, GDN_NK, GDN_DK)), rep, axis=2)
    k = jnp.repeat(l2_normalize(qkv[..., GDN_QK_DIM:2 * GDN_QK_DIM].reshape(B, S, GDN_NK, GDN_DK)), rep, axis=2)
    v = qkv[..., 2 * GDN_QK_DIM:].reshape(B, S, GDN_NV, GDN_DV)
    beta = jax.nn.sigmoid(b_in)
    g = -jnp.exp(a_log.astype(jnp.float32)) * jax.nn.softplus(a_in + dt_bias.astype(jnp.float32))
    o = chunk_gated_delta_rule(q.transpose(0, 2, 1, 3), k.transpose(0, 2, 1, 3),
                               v.transpose(0, 2, 1, 3), g.transpose(0, 2, 1),
                               beta.transpose(0, 2, 1))
    o = o.transpose(0, 2, 1, 3)
    o = rms_norm(o, onorm_g) * jax.nn.silu(z)
    return o.reshape(B, S, GDN_V_DIM).astype(h.dtype) @ w_out


def banded_causal_attention(q, k, v, span):
    N, L, HQ, HD = q.shape
    HKV = k.shape[2]
    G = HQ // HKV
    nb = -(-L // span)
    Lp = nb * span
    padw = ((0, 0), (0, Lp - L), (0, 0), (0, 0))
    qb = jnp.pad(q, padw).reshape(N, nb, span, HKV, G, HD)
    kb = jnp.pad(k, padw).reshape(N, nb, span, HKV, HD)
    vb = jnp.pad(v, padw).reshape(N, nb, span, HKV, HD)
    shift = ((0, 0), (1, 0), (0, 0), (0, 0), (0, 0))
    kk = jnp.concatenate([jnp.pad(kb, shift)[:, :-1], kb], axis=2)
    vv = jnp.concatenate([jnp.pad(vb, shift)[:, :-1], vb], axis=2)
    s = jnp.einsum('nbqhgd,nbkhd->nbhgqk', qb, kk,
                   preferred_element_type=jnp.float32) * (ATT_HD ** -0.5)
    qi = jnp.arange(span)[:, None] + span
    ki = jnp.arange(2 * span)[None, :]
    rel = qi - ki
    band = (rel >= 0) & (rel <= span)
    valid = (jnp.arange(nb)[:, None, None] > 0) | (ki[None] >= span)
    mask = band[None] & valid
    s = jnp.where(mask[None, :, None, None], s, -jnp.inf)
    m = jnp.max(s, axis=-1, keepdims=True)
    p = jnp.exp(s - m)
    den = jnp.sum(p, axis=-1, keepdims=True)
    o = jnp.einsum('nbhgqk,nbkhd->nbqhgd', (p / den).astype(vv.dtype), vv,
                   preferred_element_type=jnp.float32)
    lse = (m + jnp.log(den))[..., 0].transpose(0, 1, 4, 2, 3)
    o = o.reshape(N, Lp, HQ, HD)[:, :L]
    lse = lse.reshape(N, Lp, HQ)[:, :L]
    return o, lse


def dilated_window_attention(q, k, v, dilation, span):
    B, S, HQ, HD = q.shape
    L = S // dilation

    def to_sub(t):
        return t.reshape(B, L, dilation, t.shape[2], HD).transpose(0, 2, 1, 3, 4).reshape(B * dilation, L, t.shape[2], HD)

    o, lse = banded_causal_attention(to_sub(q), to_sub(k), to_sub(v), span)
    o = o.reshape(B, dilation, L, HQ, HD).transpose(0, 2, 1, 3, 4).reshape(B, S, HQ, HD)
    lse = lse.reshape(B, dilation, L, HQ).transpose(0, 2, 1, 3).reshape(B, S, HQ)
    return o, lse


def shared_kv(x, c_act, kv_norm_g, kv_ada_w, kv_ada_b, w_kv, k_norm_g, positions):
    B, S, _ = x.shape
    shift, scale = jnp.split(c_act @ kv_ada_w + kv_ada_b, 2, axis=-1)
    h = modulate(x, kv_norm_g, shift, scale)
    kv = (h @ w_kv).reshape(B, S, N_GROUPS, 2, ATT_HKV, ATT_HD)
    keys = [partial_rotary(rms_norm(kv[:, :, gi, 0], k_norm_g[gi]), positions) for gi in range(N_GROUPS)]
    values = [kv[:, :, gi, 1] for gi in range(N_GROUPS)]
    return keys, values


def dilated_attention_mixer(h, w_q, q_norm_g, w_o, keys, values, positions):
    B, S, _ = h.shape
    q_all = (h @ w_q).reshape(B, S, N_GROUPS, ATT_HQ, ATT_HD)
    outs, lses = [], []
    for gi, (window, dilation) in enumerate(DILATED_GROUPS):
        q = partial_rotary(rms_norm(q_all[:, :, gi], q_norm_g[gi]), positions)
        o, lse = dilated_window_attention(q, keys[gi], values[gi], dilation, window // dilation)
        outs.append(o)
        lses.append(lse)
    wts = jax.nn.softmax(jnp.stack(lses, axis=0), axis=0)
    o = jnp.einsum('gbsh,gbshd->bshd', wts, jnp.stack(outs, axis=0))
    return o.reshape(B, S, ATT_HQ * ATT_HD).astype(h.dtype) @ w_o


def squared_relu_mlp(h, w1, w2):
    return jnp.square(jax.nn.relu(h @ w1)) @ w2


def setup_inputs(seed: int = 0) -> dict:
    key = jax.random.key(seed)
    ks = jax.random.split(key, 24)
    f32 = jnp.float32

    def nrm(k, shape, scale):
        return jax.random.normal(k, shape, f32) * scale

    def gain(k, shape):
        return 1.0 + 0.02 * jax.random.normal(k, shape, f32)

    x = nrm(ks[0], (BATCH, SEQ, D_MODEL), 1.0)
    c = nrm(ks[1], (BATCH, D_MODEL), 1.0)
    positions = (jax.random.randint(ks[2], (BATCH, 1), 0, 1024, dtype=jnp.int32)
                 + jnp.arange(SEQ, dtype=jnp.int32)[None, :])
    ada_w = nrm(ks[3], (DEPTH, D_MODEL, 6 * D_MODEL), 0.5 * D_MODEL ** -0.5)
    ada_b = nrm(ks[4], (DEPTH, 6 * D_MODEL), 0.02)
    norm_g = gain(ks[5], (DEPTH, 2, D_MODEL))
    mlp_w1 = nrm(ks[6], (DEPTH, D_MODEL, D_FF), D_MODEL ** -0.5)
    mlp_w2 = nrm(ks[7], (DEPTH, D_FF, D_MODEL), D_FF ** -0.5)
    gdn_w_in = nrm(ks[8], (N_A_LAYERS, D_MODEL, GDN_PROJ), D_MODEL ** -0.5)
    gdn_conv_w = nrm(ks[9], (N_A_LAYERS, GDN_CONV_K, GDN_CONV_DIM), GDN_CONV_K ** -0.5)
    gdn_a_log = jnp.log(jax.random.uniform(ks[10], (N_A_LAYERS, GDN_NV), f32, 1.0, 16.0))
    dt = jnp.exp(jax.random.uniform(ks[11], (N_A_LAYERS, GDN_NV), f32, np.log(1e-3), np.log(1e-1)))
    gdn_dt_bias = dt + jnp.log(-jnp.expm1(-dt))
    gdn_onorm_g = gain(ks[12], (N_A_LAYERS, GDN_DV))
    gdn_w_out = nrm(ks[13], (N_A_LAYERS, GDN_V_DIM, D_MODEL), GDN_V_DIM ** -0.5)
    kv_norm_g = gain(ks[14], (D_MODEL,))
    kv_ada_w = nrm(ks[15], (D_MODEL, 2 * D_MODEL), 0.5 * D_MODEL ** -0.5)
    kv_ada_b = nrm(ks[16], (2 * D_MODEL,), 0.02)
    w_kv = nrm(ks[17], (D_MODEL, ATT_KV_PROJ), D_MODEL ** -0.5)
    k_norm_g = gain(ks[18], (N_GROUPS, ATT_HD))
    attn_w_q = nrm(ks[19], (N_B_LAYERS, D_MODEL, ATT_Q_PROJ), D_MODEL ** -0.5)
    q_norm_g = gain(ks[20], (N_B_LAYERS, N_GROUPS, ATT_HD))
    attn_w_o = nrm(ks[21], (N_B_LAYERS, ATT_HQ * ATT_HD, D_MODEL), (ATT_HQ * ATT_HD) ** -0.5)
    return {"x": x, "c": c, "positions": positions, "ada_w": ada_w, "ada_b": ada_b,
            "norm_g": norm_g, "mlp_w1": mlp_w1, "mlp_w2": mlp_w2,
            "gdn_w_in": gdn_w_in, "gdn_conv_w": gdn_conv_w, "gdn_a_log": gdn_a_log,
            "gdn_dt_bias": gdn_dt_bias, "gdn_onorm_g": gdn_onorm_g, "gdn_w_out": gdn_w_out,
            "kv_norm_g": kv_norm_g, "kv_ada_w": kv_ada_w, "kv_ada_b": kv_ada_b, "w_kv": w_kv,
            "k_norm_g": k_norm_g, "attn_w_q": attn_w_q, "q_norm_g": q_norm_g, "attn_w_o": attn_w_o}


def reference(x, c, positions, ada_w, ada_b, norm_g, mlp_w1, mlp_w2,
              gdn_w_in, gdn_conv_w, gdn_a_log, gdn_dt_bias, gdn_onorm_g, gdn_w_out,
              kv_norm_g, kv_ada_w, kv_ada_b, w_kv, k_norm_g, attn_w_q, q_norm_g, attn_w_o):
    c_act = jax.nn.silu(c)
    keys, values = None, None
    for layer in range(DEPTH):
        mod = c_act @ ada_w[layer] + ada_b[layer]
        sh1, sc1, gt1, sh2, sc2, gt2 = jnp.split(mod, 6, axis=-1)
        if layer < N_A_LAYERS:
            h = modulate(x, norm_g[layer, 0], sh1, sc1)
            y = gated_deltanet_mixer(h, gdn_w_in[layer], gdn_conv_w[layer], gdn_a_log[layer],
                                     gdn_dt_bias[layer], gdn_onorm_g[layer], gdn_w_out[layer])
        else:
            if layer == N_A_LAYERS:
                keys, values = shared_kv(x, c_act, kv_norm_g, kv_ada_w, kv_ada_b, w_kv,
                                         k_norm_g, positions)
            j = layer - N_A_LAYERS
            h = modulate(x, norm_g[layer, 0], sh1, sc1)
            y = dilated_attention_mixer(h, attn_w_q[j], q_norm_g[j], attn_w_o[j],
                                        keys, values, positions)
        x = x + gt1[:, None, :] * y
        h = modulate(x, norm_g[layer, 1], sh2, sc2)
        x = x + gt2[:, None, :] * squared_relu_mlp(h, mlp_w1[layer], mlp_w2[layer])
    return x
```

```python
import numpy as np
from contextlib import ExitStack
import concourse.bass as bass
import concourse.mybir as mybir
from concourse.bass_utils import run_bass_kernel_spmd

F32 = mybir.dt.float32
BF16 = mybir.dt.bfloat16
I32 = mybir.dt.int32
AF = mybir.ActivationFunctionType
ALU = mybir.AluOpType
AX = mybir.AxisListType

T = 4096
D = 2048
KC = 16
NCORES = 8
EPS = 1e-6
SQD = float(np.sqrt(2048.0))


class Buf:
    __slots__ = ("name", "lw", "rd", "excl")

    def __init__(self, name="", excl=False):
        self.name = name
        self.lw = None
        self.rd = {}
        self.excl = excl


class Op:
    __slots__ = ("eng", "fn", "deps", "is_dma", "signal", "ticket", "dsem", "dval", "emitted")

    def __init__(self, eng, fn, is_dma):
        self.eng = eng
        self.fn = fn
        self.deps = []
        self.is_dma = is_dma
        self.signal = False
        self.ticket = None
        self.dsem = None
        self.dval = None
        self.emitted = False


class Prog:
    COMPUTE = ("pe", "act", "dve", "pool")
    ALLENG = ("pe", "act", "dve", "pool", "sp")

    def __init__(self, nc, n_dma_sems=32):
        self.nc = nc
        self.ops = []
        self.start = 0
        self.n_dma_sems = n_dma_sems
        self.eng_obj = {"pe": nc.tensor, "act": nc.scalar, "dve": nc.vector,
                        "pool": nc.gpsimd, "sp": nc.sync}
        self.sems = {e: nc.alloc_semaphore("s_" + e) for e in self.COMPUTE}
        self.cnt = {e: 0 for e in self.COMPUTE}
        self.dma_sems = [nc.alloc_semaphore("s_dma%d" % i) for i in range(n_dma_sems)]
        self.dma_val = [0] * n_dma_sems
        self.dma_rr = 0
        self.waited = {e: {} for e in self.ALLENG}
        self.n_inst = 0

    def op(self, eng, fn, reads=(), writes=(), dma=False):
        o = Op(eng, fn, dma)
        idx = len(self.ops)
        deps = {}
        ex = [b for b in reads if b.excl]
        if ex:
            writes = list(writes) + [b for b in ex if b not in writes]
            reads = [b for b in reads if not b.excl]
            for b in ex:
                if b.lw is not None:
                    deps[b.lw] = True
        for b in reads:
            if b.lw is not None:
                deps[b.lw] = True
        for b in writes:
            if b.lw is not None:
                deps.setdefault(b.lw, False)
            for r in b.rd.values():
                deps.setdefault(r, False)
        key = ("dma", idx) if dma else eng
        for b in reads:
            b.rd[key] = idx
        for b in writes:
            b.lw = idx
            b.rd = {}
        deps.pop(idx, None)
        o.deps = sorted(deps.items())
        self.ops.append(o)
        return idx

    def dma(self, q, out, in_, reads=(), writes=()):
        qe = self.eng_obj[q]
        return self.op(q, lambda: qe.dma_start(out=out, in_=in_), reads, writes, dma=True)

    def _wait(self, eng, key, sem, val):
        w = self.waited[eng]
        if w.get(key, 0) >= val:
            return
        self.eng_obj[eng].wait_ge(sem, val)
        self.n_inst += 1
        w[key] = val

    def flush(self, barrier=True):
        ops = self.ops
        new = ops[self.start:]
        for o in new:
            for d, raw in o.deps:
                p = ops[d]
                if p.is_dma or p.emitted:
                    continue
                if p.eng == o.eng and not raw and not o.is_dma:
                    continue
                p.signal = True
        last = {}
        for o in new:
            if not o.is_dma:
                last[o.eng] = o
        for o in last.values():
            o.signal = True
        pending_cover = {e: [] for e in self.COMPUTE}
        for o in new:
            e = o.eng
            for d, raw in o.deps:
                p = ops[d]
                if p.is_dma:
                    self._wait(e, ("d", p.dsem), self.dma_sems[p.dsem], p.dval)
                else:
                    if p.eng == e and not raw and not o.is_dma:
                        continue
                    assert p.ticket is not None, (p.eng, e)
                    self._wait(e, p.eng, self.sems[p.eng], p.ticket)
            if o.is_dma:
                i = self.dma_rr
                self.dma_rr = (self.dma_rr + 1) % self.n_dma_sems
                if self.dma_val[i] > 0:
                    self._wait(e, ("d", i), self.dma_sems[i], self.dma_val[i])
                ins = o.fn()
                self.dma_val[i] += 16
                ins.then_inc(self.dma_sems[i], 16)
                o.dsem = i
                o.dval = self.dma_val[i]
            else:
                ins = o.fn()
                if o.signal:
                    self.cnt[e] += 1
                    o.ticket = self.cnt[e]
                    ins.then_inc(self.sems[e], 1)
                    for q in pending_cover[e]:
                        q.ticket = o.ticket
                    pending_cover[e] = []
                else:
                    pending_cover[e].append(o)
            self.n_inst += 1
            o.emitted = True
            o.fn = None
        self.start = len(ops)
        if barrier:
            self.barrier()

    def barrier(self):
        for e in self.ALLENG:
            for f in self.COMPUTE:
                if f != e and self.cnt[f] > 0:
                    self._wait(e, f, self.sems[f], self.cnt[f])
            for i in range(self.n_dma_sems):
                if self.dma_val[i] > 0:
                    self._wait(e, ("d", i), self.dma_sems[i], self.dma_val[i])


class Rot:
    def __init__(self, tiles):
        self.tiles = tiles
        self.k = 0

    def get(self):
        t = self.tiles[self.k % len(self.tiles)]
        self.k += 1
        return t


_uid = [0]


def _nm(name):
    _uid[0] += 1
    return "%s_u%d" % (name, _uid[0])


def sb_rot(nc, es, name, n, shape, dtype):
    tiles = []
    for i in range(n):
        t = es.enter_context(nc.sbuf_tensor(_nm(name), shape, dtype))
        tiles.append((t, Buf("%s%d" % (name, i))))
    return Rot(tiles)


def sb(nc, es, name, shape, dtype):
    return es.enter_context(nc.sbuf_tensor(_nm(name), shape, dtype))


def f_mm(nc, out, lhsT, rhs, start, stop):
    return lambda: nc.tensor.matmul(out, lhsT, rhs, start=start, stop=stop)


def f_tt(eng, out, in0, in1, op):
    return lambda: eng.tensor_tensor(out=out, in0=in0, in1=in1, op=op)


def f_ts(eng, out, in0, s1, s2, op0, op1=None):
    if op1 is None:
        return lambda: eng.tensor_scalar(out=out, in0=in0, scalar1=s1, scalar2=None, op0=op0)
    return lambda: eng.tensor_scalar(out=out, in0=in0, scalar1=s1, scalar2=s2, op0=op0, op1=op1)


def f_stt(eng, out, in0, scalar, in1, op0, op1):
    return lambda: eng.scalar_tensor_tensor(out=out, in0=in0, scalar=scalar, in1=in1, op0=op0, op1=op1)


def f_act(nc, out, in_, func, bias=None, scale=None, accum_out=None):
    kw = {}
    if bias is not None:
        kw["bias"] = bias
    if scale is not None:
        kw["scale"] = scale
    if accum_out is not None:
        kw["accum_out"] = accum_out
    return lambda: nc.scalar.activation(out=out, in_=in_, func=func, **kw)


def f_copy(eng, out, in_):
    return lambda: eng.tensor_copy(out=out, in_=in_)


def f_memset(eng, ap, v):
    return lambda: eng.memset(ap, v)


def bc_t(col2d, n):
    return col2d.unsqueeze(2).to_broadcast([col2d.shape[0], col2d.shape[1], n])


def bc_c(row2d, c):
    return row2d.unsqueeze(1).to_broadcast([row2d.shape[0], c, row2d.shape[1]])


class K:
    def __init__(self, dbg=False):
        self.dbg = dbg
        nc = self.nc = bass.Bass("TRN2", target_bir_lowering=False)
        self.P = Prog(nc)
        self.inputs = {}
        self.psd = [nc.alloc_psum_tensor("psd%d" % i, [128, 1024], F32) for i in range(4)]
        self.psb = [self.psd[i // 2][:, (i % 2) * 512:(i % 2 + 1) * 512] for i in range(8)]
        self.psB = [Buf("psB%d" % i, excl=True) for i in range(8)]
        self.psq = Rot([(self.psb[i // 4][:, (i % 4) * 128:(i % 4 + 1) * 128], self.psB[i // 4])
                        for i in range(32)])
        self.bank_rr = 0

    def din(self, name, shape, dt=F32):
        t = self.nc.dram_tensor(name, list(shape), dt, kind="ExternalInput").ap()
        self.inputs[name] = (tuple(shape), dt)
        return t

    def dscr(self, name, shape, dt):
        isout = self.dbg is True or (isinstance(self.dbg, (set, list, tuple)) and any(name.startswith(n) for n in self.dbg))
        kind = "ExternalOutput" if isout else "Internal"
        return self.nc.dram_tensor(name, list(shape), dt, kind=kind).ap()

    def bank(self, lo=0, hi=8):
        i = lo + (self.bank_rr % (hi - lo))
        self.bank_rr += 1
        return self.psb[i], self.psB[i]

    def declare(self):
        d = self.din
        self.xT = d("xT", [D, T])
        self.ccol = d("ccol", [128, 16])
        self.pos = d("pos", [1, T], I32)
        self.ada_w = d("ada_w_t", [4, 24, 128, 16 * 512])
        self.ada_b = d("ada_b_t", [4, 128, 96])
        self.kvada_w = d("kvada_w_t", [8, 128, 16 * 512])
        self.kvada_b = d("kvada_b_t", [128, 32])
        self.normg = d("normg_t", [4, 2, 128, 16])
        self.kvnormg = d("kvnormg_t", [128, 16])
        self.w1 = d("mlp_w1_t", [4, 64, 128, 16 * 128])
        self.w2 = d("mlp_w2_t", [4, 16, 2, 128, 32 * 128])
        self.win_fm = d("gdn_win_fm", [2, 64, 128, 16 * 128])
        self.wz = d("gdn_wz_t", [2, 8, 128, 16 * 512])
        self.wba = d("gdn_wba_t", [2, 128, 16 * 64])
        self.convw = d("gdn_conv_t", [2, 128, 64 * 4])
        self.alog = d("gdn_alog_b", [2, 128, 32])
        self.dtb = d("gdn_dtb_b", [2, 128, 32])
        self.onorm = d("gdn_onorm_b", [2, 128, 128])
        self.wout = d("gdn_wout_t", [2, 16, 128, 32 * 128])
        self.wkvk = d("wkv_k_t", [12, 128, 16 * 128])
        self.wkvv = d("wkv_v_t", [3, 128, 16 * 512])
        self.knorm = d("knorm_t", [128, 3])
        self.qnorm = d("qnorm_t", [2, 128, 3])
        self.wq = d("attn_wq_t", [2, 48, 128, 16 * 128])
        self.wo = d("attn_wo_t", [2, 16, 128, 16 * 128])
        self.c_f32 = d("c_f32", [128, 10 * 128])
        self.c_bf = d("c_bf", [128, 5 * 128], BF16)
        self.c_invf = d("c_invf", [128, 1])
        self.yT = self.nc.dram_tensor("yT", [D, T], F32, kind="ExternalOutput").ap()
        s = self.dscr
        self.qT_s = s("qT_s", [16, 128, T], F32)
        self.kT_s = s("kT_s", [16, 128, T], F32)
        self.ktok_s = s("ktok_s", [16, T, 128], F32)
        self.vtok_s = s("vtok_s", [32, T, 128], F32)
        self.ztok_s = s("ztok_s", [T, 4096], BF16)
        self.ogT_s = s("ogT_s", [4096, T], BF16)
        self.KT_s = s("KT_s", [12, 128, T], BF16)
        self.Vb_s = s("Vb_s", [3, 32, 128, 512], BF16)
        self.QT_s = s("QT_s", [48, 128, T], BF16)
        self.aoT_s = s("aoT_s", [2048, T], BF16)

    def load_consts(self, es):
        nc, P = self.nc, self.P
        self.cf = sb(nc, es, "cf", [128, 10 * 128], F32)
        self.cb = sb(nc, es, "cbf", [128, 5 * 128], BF16)
        self.cB = Buf("consts")
        P.dma("sp", self.cf[:], self.c_f32[:, :], writes=[self.cB])
        P.dma("sp", self.cb[:], self.c_bf[:, :], writes=[self.cB])
        cf, cb = self.cf, self.cb
        sl = lambda i: slice(i * 128, (i + 1) * 128)
        self.ident = cf[:, sl(0)]
        self.U = cf[:, sl(1)]
        self.SL = cf[:, sl(2)]
        self.SU = cf[:, sl(3)]
        self.BD32 = cf[:, sl(4)]
        self.OFF1T = cf[:, sl(5)]
        self.OFF2T = cf[:, sl(6)]
        self.ones_f = cf[:, sl(7)]
        self.ident_b = cb[:, sl(0)]
        self.ones_b = cb[:, sl(1)]
        self.rotm = cb[:, sl(2)]
        self.maskPC = cb[:, 3 * 128:5 * 128]
        self.modT = sb(nc, es, "modT", [128, 5, 96], F32)
        self.modB = Buf("modT")
        self.cols = sb(nc, es, "cols", [128, 9, 2, 16], F32)
        self.colsB = Buf("cols")

    def st_mods(self):
        nc, P = self.nc, self.P
        with ExitStack() as es:
            cc = sb(nc, es, "cc", [128, 16], F32)
            ccb = sb(nc, es, "ccb", [128, 16], BF16)
            cB = Buf("cc")
            P.dma("sp", cc[:], self.ccol[:, :], writes=[cB])
            P.op("act", f_act(nc, ccb[:], cc[:], AF.Silu), [cB], [cB])
            wp = sb_rot(nc, es, "mw", 3, [128, 16, 512], BF16)
            ng = sb(nc, es, "ng", [128, 9, 16], F32)
            ab = sb(nc, es, "ab", [128, 5, 96], F32)
            ngB = Buf("ng")
            for l in range(4):
                for j in range(2):
                    P.dma("sp", ng[:, l * 2 + j, :], self.normg[l, j], writes=[ngB])
                P.dma("sp", ab[:, l, :], self.ada_b[l], writes=[ngB])
            P.dma("sp", ng[:, 8, :], self.kvnormg[:, :], writes=[ngB])
            P.dma("sp", ab[:, 4, 0:32], self.kvada_b[:, :], writes=[ngB])
            for l in range(5):
                nblk = 24 if l < 4 else 8
                ps, pb = self.bank()
                for blk in range(nblk):
                    wt, wb = wp.get()
                    src = self.ada_w[l, blk] if l < 4 else self.kvada_w[blk]
                    P.dma("pool", wt[:], src.rearrange("p (c n) -> p c n", c=16), writes=[wb])
                    for j in range(4):
                        col = blk * 4 + j
                        for kc in range(16):
                            P.op("pe", f_mm(nc, ps[:, col:col + 1], wt[:, kc, j * 128:(j + 1) * 128],
                                            ccb[:, kc:kc + 1], kc == 0, kc == 15), [wb, cB], [pb])
                nco = nblk * 4
                P.op("dve", f_tt(nc.vector, self.modT[:, l, 0:nco], ps[:, 0:nco], ab[:, l, 0:nco], ALU.add),
                     [pb, ngB], [self.modB])
            for st in range(9):
                if st < 8:
                    l, j = st // 2, st % 2
                    sh = self.modT[:, l, (0 + 48 * j):(16 + 48 * j)]
                    sc = self.modT[:, l, (16 + 48 * j):(32 + 48 * j)]
                else:
                    sh = self.modT[:, 4, 0:16]
                    sc = self.modT[:, 4, 16:32]
                P.op("dve", f_stt(nc.vector, self.cols[:, st, 0, :], sc, 1.0, ng[:, st, :], ALU.add, ALU.mult),
                     [self.modB, ngB], [self.colsB])
                P.op("dve", f_ts(nc.vector, self.cols[:, st, 0, :], self.cols[:, st, 0, :], SQD, None, ALU.mult),
                     [self.colsB], [self.colsB])
                P.op("dve", f_copy(nc.vector, self.cols[:, st, 1, :], sh), [self.modB], [self.colsB])
            P.flush()

    def rsqrt(self, dst, dstB, src, srcB, c):
        nc, P = self.nc, self.P
        P.op("act", f_act(nc, dst, src, AF.Sqrt, bias=float(c)), [srcB], [dstB])
        P.op("dve", (lambda: nc.vector.reciprocal(out=dst, in_=dst)), [dstB], [dstB])

    def gate(self, l, j):
        return self.modT[:, l, (32 + 48 * j):(48 + 48 * j)]

    def norm_tile(self, pools, x_src, t0, dst, dstB, st, W=512):
        nc, P = self.nc, self.P
        xp, sqp, rp = pools
        xv = x_src.rearrange("(c p) t -> p c t", p=128)
        xt, xb = xp.get()
        P.dma("sp", xt[:], xv[:, :, t0:t0 + W], writes=[xb])
        sq, sqb = sqp.get()
        P.op("act", f_act(nc, sq[:], xt[:], AF.Square), [xb], [sqb])
        ps, pb = self.bank(6, 8)
        for kc in range(16):
            P.op("pe", f_mm(nc, ps[:, 0:W], self.ones_b, sq[:, kc, :], kc == 0, kc == 15), [sqb, self.cB], [pb])
        r, rb = rp.get()
        self.rsqrt(r[:], rb, ps[:, 0:W], pb, 2048.0 * EPS)
        A = self.cols[:, st, 0, :]
        sh = self.cols[:, st, 1, :]
        P.op("dve", f_tt(nc.vector, xt[:], xt[:], bc_t(A, W), ALU.mult), [xb, self.colsB], [xb])
        P.op("pool", f_tt(nc.gpsimd, xt[:], xt[:], bc_c(r[:, :], 16), ALU.mult), [xb, rb], [xb])
        P.op("dve", f_tt(nc.vector, dst, xt[:], bc_t(sh, W), ALU.add), [xb, self.colsB], [dstB])

    def norm_pools(self, es, n=1, W=512):
        nc = self.nc
        return (sb_rot(nc, es, "nx", n, [128, 16, W], F32),
                sb_rot(nc, es, "nsq", n, [128, 16, W], BF16),
                sb_rot(nc, es, "nr", n, [128, W], F32))

    def st_mlp(self, l, x_in, x_out):
        nc, P = self.nc, self.P
        TT = 1024
        with ExitStack() as es:
            pools = self.norm_pools(es, 1)
            hp = sb_rot(nc, es, "mh", 1, [128, 16, TT], BF16)
            hid = sb(nc, es, "hid", [128, 32, TT], BF16)
            hidB = [[Buf("hid") for _ in range(2)] for _ in range(32)]
            w1p = sb_rot(nc, es, "w1", 3, [128, 16, 128], BF16)
            w2p = sb_rot(nc, es, "w2", 2, [128, 32, 128], BF16)
            rl = sb_rot(nc, es, "rl", 3, [128, 512], BF16)
            xo = sb_rot(nc, es, "xo", 3, [128, 512], F32)
            gt = self.gate(l, 1)
            xiv = x_in.rearrange("(c p) t -> p c t", p=128)
            xov = x_out.rearrange("(c p) t -> p c t", p=128)
            unit = 0
            for tb in range(T // TT):
                h, hB0 = hp.get()
                hB = [Buf("mhs") for _ in range(2)]
                for s in range(2):
                    self.norm_tile(pools, x_in, tb * TT + s * 512, h[:, :, s * 512:(s + 1) * 512], hB[s], l * 2 + 1)
                for half in range(2):
                    for fc in range(32):
                        wt, wb = w1p.get()
                        P.dma("pool", wt[:], self.w1[l, half * 32 + fc].rearrange("p (c n) -> p c n", c=16), writes=[wb])
                        for s in range(2):
                            ps, pb = self.bank(0, 4)
                            for kc in range(16):
                                P.op("pe", f_mm(nc, ps[:], wt[:, kc, :], h[:, kc, s * 512:(s + 1) * 512], kc == 0, kc == 15),
                                     [wb, hB[s]], [pb])
                            dst = hid[:, fc, s * 512:(s + 1) * 512]
                            r, rb = rl.get()
                            P.op("act", f_act(nc, r[:], ps[:], AF.Relu), [pb], [rb])
                            if unit % 2 == 0:
                                P.op("dve", f_tt(nc.vector, dst, r[:], r[:], ALU.mult), [rb], [hidB[fc][s]])
                            else:
                                P.op("pool", f_tt(nc.gpsimd, dst, r[:], r[:], ALU.mult), [rb], [hidB[fc][s]])
                            unit += 1
                    xsrc = xiv if half == 0 else xov
                    if half == 0:
                        xtok = [[Buf("xtok") for _ in range(2)] for _ in range(16)]
                    for fo in range(16):
                        wt, wb = w2p.get()
                        P.dma("pool", wt[:], self.w2[l, fo, half].rearrange("p (c n) -> p c n", c=32), writes=[wb])
                        for s in range(2):
                            t0 = tb * TT + s * 512
                            xt, xb = xo.get()
                            P.dma("sp", xt[:], xsrc[:, fo, t0:t0 + 512], reads=[xtok[fo][s]], writes=[xb])
                            ps, pb = self.bank(4, 6)
                            for kc in range(32):
                                P.op("pe", f_mm(nc, ps[:], wt[:, kc, :], hid[:, kc, s * 512:(s + 1) * 512], kc == 0, kc == 31),
                                     [wb, hidB[kc][s]], [pb])
                            P.op("dve", f_stt(nc.vector, xt[:], ps[:], gt[:, fo:fo + 1], xt[:], ALU.mult, ALU.add),
                                 [pb, xb, self.modB], [xb])
                            P.dma("sp", xov[:, fo, t0:t0 + 512], xt[:], reads=[xb], writes=[xtok[fo][s]])
            P.flush()

    def st_proj_resid(self, actT, kcn, w_t, gate, x_in, x_out):
        nc, P = self.nc, self.P
        TT = 1024
        with ExitStack() as es:
            ap_ = sb_rot(nc, es, "pa", 1, [128, kcn, TT], BF16)
            wp = sb_rot(nc, es, "pw", 3, [128, kcn, 128], BF16)
            xo = sb_rot(nc, es, "px", 4, [128, 512], F32)
            av = actT.rearrange("(c p) t -> p c t", p=128)
            xiv = x_in.rearrange("(c p) t -> p c t", p=128)
            xov = x_out.rearrange("(c p) t -> p c t", p=128)
            for tb in range(T // TT):
                a, aB = ap_.get()
                for c0 in range(0, kcn, 8):
                    P.dma("sp", a[:, c0:c0 + 8, :], av[:, c0:c0 + 8, tb * TT:(tb + 1) * TT], writes=[aB])
                for fo in range(16):
                    wt, wb = wp.get()
                    P.dma("pool", wt[:], w_t[fo].rearrange("p (c n) -> p c n", c=kcn), writes=[wb])
                    for s in range(2):
                        t0 = tb * TT + s * 512
                        xt, xb = xo.get()
                        P.dma("sp", xt[:], xiv[:, fo, t0:t0 + 512], writes=[xb])
                        ps, pb = self.bank(0, 4)
                        for kc in range(kcn):
                            P.op("pe", f_mm(nc, ps[:], wt[:, kc, :], a[:, kc, s * 512:(s + 1) * 512], kc == 0, kc == kcn - 1),
                                 [wb, aB], [pb])
                        P.op("dve", f_stt(nc.vector, xt[:], ps[:], gate[:, fo:fo + 1], xt[:], ALU.mult, ALU.add),
                             [pb, xb, self.modB], [xb])
                        P.dma("sp", xov[:, fo, t0:t0 + 512], xt[:], reads=[xb])
            P.flush()

    def st_gdn_proj(self, l, x_in, bg, bgB):
        nc, P = self.nc, self.P
        with ExitStack() as es:
            hT = sb(nc, es, "hT", [128, 16, T], BF16)
            hB = [Buf("hT%d" % i) for i in range(8)]
            with ExitStack() as es2:
                pools = self.norm_pools(es2, 1)
                for tt in range(8):
                    self.norm_tile(pools, x_in, tt * 512, hT[:, :, tt * 512:(tt + 1) * 512], hB[tt], l * 2)
                P.flush()
            with ExitStack() as es2:
                wba = sb(nc, es2, "wba", [128, 16, 64], BF16)
                wbB = Buf("wba")
                P.dma("pool", wba[:], self.wba[l].rearrange("p (c n) -> p c n", c=16), writes=[wbB])
                prm = sb(nc, es2, "prm", [128, 2, 32], F32)
                prB = Buf("prm")
                P.dma("sp", prm[:, 0, :], self.alog[l], writes=[prB])
                P.dma("sp", prm[:, 1, :], self.dtb[l], writes=[prB])
                tmp = sb(nc, es2, "bgtmp", [128, 32, 32], F32)
                tB = Buf("bgtmp")
                for ts in range(32):
                    ps, pb = self.bank(0, 4)
                    for kc in range(16):
                        P.op("pe", f_mm(nc, ps[:, 0:64], hT[:, kc, ts * 128:(ts + 1) * 128], wba[:, kc, :], kc == 0, kc == 15),
                             [hB[ts // 4], wbB], [pb])
                    P.op("act", (lambda ps=ps, ts=ts: nc.scalar.copy(out=bg[:, ts, :], in_=ps[:, 0:64])), [pb], [bgB])
                P.op("act", f_act(nc, bg[:, :, 0:32], bg[:, :, 0:32], AF.Sigmoid), [bgB], [bgB])
                P.op("dve", f_tt(nc.vector, tmp[:], bg[:, :, 32:64], bc_c(prm[:, 1, :], 32), ALU.add), [bgB, prB], [tB])
                P.op("act", f_act(nc, tmp[:], tmp[:], AF.Exp), [tB], [tB])
                P.op("act", f_act(nc, tmp[:], tmp[:], AF.Ln, bias=1.0), [tB], [tB])
                P.op("act", f_act(nc, prm[:, 0, :], prm[:, 0, :], AF.Exp), [prB], [prB])
                P.op("dve", f_stt(nc.vector, bg[:, :, 32:64], tmp[:], -1.0, bc_c(prm[:, 0, :], 32), ALU.mult, ALU.mult),
                     [tB, prB], [bgB])
                P.flush()
            with ExitStack() as es2:
                wzp = sb_rot(nc, es2, "wz", 2, [128, 16, 512], BF16)
                zo = sb_rot(nc, es2, "zo", 4, [128, 512], BF16)
                for zb in range(8):
                    wt, wb = wzp.get()
                    P.dma("pool", wt[:], self.wz[l, zb].rearrange("p (c n) -> p c n", c=16), writes=[wb])
                    for ts in range(32):
                        ps, pb = self.bank(0, 4)
                        for kc in range(16):
                            P.op("pe", f_mm(nc, ps[:], hT[:, kc, ts * 128:(ts + 1) * 128], wt[:, kc, :], kc == 0, kc == 15),
                                 [hB[ts // 4], wb], [pb])
                        z, zB = zo.get()
                        P.op("act", f_act(nc, z[:], ps[:], AF.Silu), [pb], [zB])
                        P.dma("sp", self.ztok_s[ts * 128:(ts + 1) * 128, zb * 512:(zb + 1) * 512], z[:], reads=[zB])
                P.flush()
            with ExitStack() as es2:
                cw = sb(nc, es2, "cw", [128, 64, 4], F32)
                cwB = Buf("cw")
                P.dma("sp", cw[:], self.convw[l].rearrange("p (c j) -> p c j", j=4), writes=[cwB])
                wp = sb_rot(nc, es2, "wi", 3, [128, 16, 128], BF16)
                pbuf = sb_rot(nc, es2, "pbuf", 3, [128, 515], F32)
                accp = sb_rot(nc, es2, "acc", 3, [128, 512], F32)
                svp = sb_rot(nc, es2, "sv", 3, [128, 512], F32)
                sqp = sb_rot(nc, es2, "sq", 2, [128, 512], BF16)
                rsp = sb_rot(nc, es2, "rs", 2, [128, 512], F32)
                qnp = sb_rot(nc, es2, "qn", 3, [128, 512], F32)
                tkp = sb_rot(nc, es2, "tk", 3, [128, 4, 128], F32)
                ctp = sb_rot(nc, es2, "ct", 2, [128, 512], F32)
                unit = 0
                for fc in range(64):
                    wt, wb = wp.get()
                    P.dma("pool", wt[:], self.win_fm[l, fc].rearrange("p (c n) -> p c n", c=16), writes=[wb])
                    prev = None
                    for tt in range(8):
                        ps, pb = self.bank(0, 4)
                        for kc in range(16):
                            P.op("pe", f_mm(nc, ps[:], wt[:, kc, :], hT[:, kc, tt * 512:(tt + 1) * 512], kc == 0, kc == 15),
                                 [wb, hB[tt]], [pb])
                        pbt, pbB = pbuf.get()
                        P.op("act", (lambda pbt=pbt, ps=ps: nc.scalar.copy(out=pbt[:, 3:515], in_=ps[:])), [pb], [pbB])
                        if prev is None:
                            P.op("pool", f_memset(nc.gpsimd, pbt[:, 0:3], 0.0), [], [pbB])
                        else:
                            P.op("pool", f_copy(nc.gpsimd, pbt[:, 0:3], prev[0][:, 512:515]), [prev[1]], [pbB])
                        prev = (pbt, pbB)
                        eng, en = (nc.vector, "dve") if unit % 2 == 0 else (nc.gpsimd, "pool")
                        unit += 1
                        acc, aB = accp.get()
                        P.op(en, f_ts(eng, acc[:], pbt[:, 0:512], cw[:, fc, 0:1], None, ALU.mult), [pbB, cwB], [aB])
                        for j in range(1, 4):
                            if en == "dve":
                                P.op(en, f_stt(eng, acc[:], pbt[:, j:j + 512], cw[:, fc, j:j + 1], acc[:], ALU.mult, ALU.add),
                                     [pbB, cwB, aB], [aB])
                            else:
                                ctmp, ctB = ctp.get()
                                P.op(en, f_ts(eng, ctmp[:], pbt[:, j:j + 512], cw[:, fc, j:j + 1], None, ALU.mult), [pbB, cwB], [ctB])
                                P.op(en, f_tt(eng, acc[:], acc[:], ctmp[:], ALU.add), [ctB, aB], [aB])
                        sv, sB = svp.get()
                        P.op("act", f_act(nc, sv[:], acc[:], AF.Silu), [aB], [sB])
                        if fc < 32:
                            sq, sqB = sqp.get()
                            isq = fc < 16
                            P.op("act", f_act(nc, sq[:], sv[:], AF.Square, scale=(float(np.sqrt(128.0)) if isq else 1.0)), [sB], [sqB])
                            ps2, pb2 = self.bank(4, 6)
                            P.op("pe", f_mm(nc, ps2[:], self.ones_b, sq[:], True, True), [sqB, self.cB], [pb2])
                            rs, rB = rsp.get()
                            self.rsqrt(rs[:], rB, ps2[:], pb2, (128.0 * EPS if isq else EPS))
                            qn, qB = qnp.get()
                            if fc < 16:
                                P.op("pool", f_tt(nc.gpsimd, qn[:], sv[:], rs[:], ALU.mult), [sB, rB], [qB])
                                P.dma("sp", self.qT_s[fc, :, tt * 512:(tt + 1) * 512], qn[:], reads=[qB])
                                src = None
                            else:
                                P.op("pool", f_tt(nc.gpsimd, qn[:], sv[:], rs[:], ALU.mult), [sB, rB], [qB])
                                P.dma("sp", self.kT_s[fc - 16, :, tt * 512:(tt + 1) * 512], qn[:], reads=[qB])
                                src, srcB = qn, qB
                                dstd = self.ktok_s[fc - 16]
                        else:
                            src, srcB = sv, sB
                            dstd = self.vtok_s[fc - 32]
                        if src is not None:
                            ps3, pb3 = self.bank(6, 8)
                            for j in range(4):
                                P.op("pe", f_mm(nc, ps3[:, j * 128:(j + 1) * 128], src[:, j * 128:(j + 1) * 128], self.ident, True, True),
                                     [srcB, self.cB], [pb3])
                            tk, tB = tkp.get()
                            P.op("act", (lambda tk=tk, ps3=ps3: nc.scalar.copy(out=tk[:].rearrange("p j d -> p (j d)"), in_=ps3[:])),
                                 [pb3], [tB])
                            P.dma("sp", dstd[tt * 512:(tt + 1) * 512, :].rearrange("(j p) d -> p j d", p=128), tk[:], reads=[tB])
                P.flush()
            if self.dbg:
                dbg_bg = self.dscr(_nm("dbg_bg"), [128, 32 * 64], F32)
                P.dma("sp", dbg_bg, bg[:].rearrange("p a b -> p (a b)"), reads=[bgB])
                dbg_mod = self.dscr(_nm("dbg_mod"), [128, 5 * 96], F32)
                P.dma("sp", dbg_mod, self.modT[:].rearrange("p a b -> p (a b)"), reads=[self.modB])
                P.flush()

    def st_gdn_chunk(self, l, bg, bgB, n_chunks=32):
        nc, P = self.nc, self.P
        V, G, A_ = nc.vector, nc.gpsimd, nc.scalar
        with ExitStack() as es:
            kT = sb(nc, es, "ckT", [128, 16, 128], F32)
            qT = sb(nc, es, "cqT", [128, 16, 128], F32)
            ktok = sb(nc, es, "cktok", [128, 16, 128], F32)
            vtok = sb(nc, es, "cvtok", [128, 32, 128], F32)
            zt = sb(nc, es, "czt", [128, 32, 128], BF16)
            kTB, qTB, ktB, vtB, ztB = Buf("kT"), Buf("qT"), Buf("ktok"), Buf("vtok"), Buf("zt")
            Gm = sb(nc, es, "Gm", [128, 32, 128], F32)
            GmB = Buf("Gm")
            zg = sb(nc, es, "zg", [128, 32, 128], F32)
            zgB = Buf("zg")
            S = sb(nc, es, "S", [128, 32, 128], F32)
            SB = [Buf("S%d" % h) for h in range(32)]
            ogT = sb(nc, es, "ogT", [128, 32, 128], BF16)
            ogB = Buf("ogT")
            eAll = sb(nc, es, "eAll", [128, 96], F32)
            eB = Buf("eAll")
            onr = sb(nc, es, "onr", [128, 128], F32)
            onB = Buf("onr")
            P.dma("sp", onr[:], self.onorm[l], writes=[onB])
            P.op("dve", f_ts(V, onr[:], onr[:], float(np.sqrt(128.0)), None, ALU.mult), [onB], [onB])
            P.op("pool", f_memset(G, S[:], 0.0), [], SB)
            tp = sb_rot(nc, es, "tp", 104, [128, 128], F32)
            tpl = sb_rot(nc, es, "tpl", 16, [128, 128], F32)
            tpb = sb_rot(nc, es, "tpb", 4, [128, 128], BF16)
            ssp = sb_rot(nc, es, "ss", 8, [128, 2], F32)
            cB = self.cB
            I_, U_, SL_, SU_, BD, O1T, O2T = self.ident, self.U, self.SL, self.SU, self.BD32, self.OFF1T, self.OFF2T

            def mmq(lhsT, rhs, rd, out=None, outB=None, start=True, stop=True):
                if out is None:
                    out, outB = self.psq.get()
                P.op("pe", f_mm(nc, out, lhsT, rhs, start, stop), rd, [outB])
                return out, outB

            def evac(eng_name, src, srcB, extra=()):
                t, tB = tp.get()
                if eng_name == "act":
                    P.op("act", (lambda t=t, src=src: nc.scalar.copy(out=t[:], in_=src)), [srcB] + list(extra), [tB])
                else:
                    P.op("dve", f_copy(V, t[:], src), [srcB] + list(extra), [tB])
                return t, tB

            dbgT = self.dscr(_nm("dbg_ch"), [128, 12 * 128], F32) if self.dbg else None

            def dump(n, hv, slot, t, tB):
                if self.dbg and n == 0 and hv == 5:
                    P.dma("sp", dbgT[:, slot * 128:(slot + 1) * 128], t, reads=[tB])

            def head_gen(n, hv, kk, qkt):
                hk = hv // 2
                beta = bg[:, n, hv:hv + 1]
                eG = eAll[:, hv:hv + 1]
                eGl = eAll[:, 32 + hv:33 + hv]
                eGt = eAll[:, 64 + hv:65 + hv]
                dps, dB = mmq(Gm[:, hv, :], U_, [GmB, cB])
                E, EB = tp.get()
                P.op("act", f_act(nc, E[:], dps, AF.Exp), [dB], [EB])
                dump(n, hv, 0, E[:], EB)
                yield
                DTs, DsB = tp.get()
                P.op("pool", f_tt(G, DTs[:], E[:], SU_, ALU.mult), [EB, cB], [DsB])
                DTi, DiB = tp.get()
                P.op("pool", f_tt(G, DTi[:], E[:], U_, ALU.mult), [EB, cB], [DiB])
                A, AB = tp.get()
                P.op("dve", f_stt(V, A[:], kk[0], beta, DTs[:], ALU.mult, ALU.mult), [kk[1], bgB, DsB], [AB])
                iT, iTB = tpl.get()
                P.op("dve", f_tt(V, iT[:], qkt[0], DTi[:], ALU.mult), [qkt[1], DiB], [iTB])
                dump(n, hv, 1, A[:], AB)
                dump(n, hv, 2, iT[:], iTB)
                yield
                aps, aB = mmq(A[:], I_, [AB, cB])
                AT, ATB = evac("act", aps, aB)
                yield
                B, BB = tp.get()
                P.op("pool", f_tt(G, B[:], A[:], BD, ALU.mult), [AB, cB], [BB])
                BT, BTB = tp.get()
                P.op("pool", f_tt(G, BT[:], AT[:], BD, ALU.mult), [ATB, cB], [BTB])
                Ao1T, o1B = tpl.get()
                P.op("pool", f_tt(G, Ao1T[:], AT[:], O1T, ALU.mult), [ATB, cB], [o1B])
                Ao2T, o2B = tpl.get()
                P.op("dve", f_tt(V, Ao2T[:], AT[:], O2T, ALU.mult), [ATB, cB], [o2B])
                Pm, PB = tp.get()
                P.op("dve", f_tt(V, Pm[:], I_, B[:], ALU.subtract), [BB, cB], [PB])
                yield
                for step in range(4):
                    last = (step == 3)
                    if not last:
                        b2ps, b2B = mmq(BT[:], B[:], [BTB, BB])
                    b2tps, b2tB = mmq(B[:], BT[:], [BB, BTB])
                    yield
                    if not last:
                        B2, B2B = evac("act", b2ps, b2B)
                    B2T, B2TB = evac("dve", b2tps, b2tB)
                    yield
                    pps, ppB = mmq(B2T[:], Pm[:], [B2TB, PB])
                    yield
                    Pn, PnB = tp.get()
                    P.op("dve", f_tt(V, Pn[:], pps, Pm[:], ALU.add), [ppB, PB], [PnB])
                    Pm, PB = Pn, PnB
                    if not last:
                        B, BB, BT, BTB = B2, B2B, B2T, B2TB
                    yield
                Td, TdB = Pm, PB
                dump(n, hv, 3, Td[:], TdB)
                for AoT, aoB in ((Ao1T, o1B), (Ao2T, o2B)):
                    xps, xB = mmq(AoT[:], Td[:], [aoB, TdB])
                    tps, tB = mmq(Td[:], I_, [TdB, cB])
                    yield
                    X, XB = evac("act", xps, xB)
                    TdT, TdTB = evac("dve", tps, tB)
                    yield
                    yps, yB = mmq(TdT[:], X[:], [TdTB, XB])
                    yield
                    Tn, TnB = tp.get()
                    P.op("dve", f_tt(V, Tn[:], Td[:], yps, ALU.subtract), [yB, TdB], [TnB])
                    Td, TdB = Tn, TnB
                    yield
                TT_, TTB = Td, TdB
                dump(n, hv, 4, TT_[:], TTB)
                keg, kegB = tp.get()
                P.op("pool", f_ts(G, keg[:], ktok[:, hk, :], eG, None, ALU.mult), [ktB, eB], [kegB])
                kd, kdB = tp.get()
                P.op("pool", f_ts(G, kd[:], ktok[:, hk, :], eGl, None, ALU.mult), [ktB, eB], [kdB])
                yield
                wps, wB = mmq(keg[:], TT_[:], [kegB, TTB])
                yield
                nwT, nwB = tp.get()
                P.op("act", (lambda nwT=nwT, wps=wps: nc.scalar.mul(out=nwT[:], in_=wps, mul=-1.0)), [wB], [nwB])
                yield
                vps, vB = mmq(TT_[:], vtok[:, hv, :], [TTB, vtB], start=True, stop=False)
                mmq(nwT[:], S[:, hv, :], [nwB, SB[hv]], out=vps, outB=vB, start=False, stop=True)
                o1ps, o1pB = mmq(qT[:, hk, :], S[:, hv, :], [qTB, SB[hv]])
                yield
                vn, vnB = tp.get()
                P.op("act", f_act(nc, vn[:], vps, AF.Copy, scale=beta), [vB, bgB], [vnB])
                o1s, o1sB = tp.get()
                P.op("act", f_act(nc, o1s[:], o1ps, AF.Copy, scale=eG), [o1pB, eB], [o1sB])
                dump(n, hv, 5, vn[:], vnB)
                yield
                o2ps, o2pB = mmq(iT[:], vn[:], [iTB, vnB])
                sups, suB = mmq(kd[:], vn[:], [kdB, vnB])
                yield
                o, oB = tp.get()
                P.op("dve", f_tt(V, o[:], o2ps, o1s[:], ALU.add), [o2pB, o1sB], [oB])
                P.op("dve", f_stt(V, S[:, hv, :], S[:, hv, :], eGt, sups, ALU.mult, ALU.add), [SB[hv], eB, suB], [SB[hv]])
                dump(n, hv, 6, o[:], oB)
                dump(n, hv, 7, S[:, hv, :], SB[hv])
                yield
                ss, ssB = ssp.get()
                junk, jB = tp.get()
                P.op("act", f_act(nc, junk[:], o[:], AF.Square, accum_out=ss[:, 0:1]), [oB], [jB, ssB])
                yield
                self.rsqrt(ss[:, 1:2], ssB, ss[:, 0:1], ssB, 128.0 * EPS)
                yield
                og, ogtB = tpb.get()
                P.op("dve", f_stt(V, og[:], o[:], ss[:, 1:2], zg[:, hv, :], ALU.mult, ALU.mult), [oB, ssB, zgB], [ogtB])
                yield
                gps, gB = mmq(og[:], self.ident_b, [ogtB, cB])
                yield
                P.op("act", (lambda gps=gps, hv=hv: nc.scalar.copy(out=ogT[:, hv, :], in_=gps)), [gB], [ogB])

            for n in range(n_chunks):
                t0 = n * 128
                P.dma("sp", kT[:], self.kT_s[:, :, t0:t0 + 128].rearrange("h d t -> d h t"), writes=[kTB])
                P.dma("sp", qT[:], self.qT_s[:, :, t0:t0 + 128].rearrange("h d t -> d h t"), writes=[qTB])
                P.dma("sp", ktok[:], self.ktok_s[:, t0:t0 + 128, :].rearrange("h t d -> t h d"), writes=[ktB])
                for hh in range(0, 32, 16):
                    P.dma("sp", vtok[:, hh:hh + 16, :], self.vtok_s[hh:hh + 16, t0:t0 + 128, :].rearrange("h t d -> t h d"), writes=[vtB])
                P.dma("sp", zt[:].rearrange("t h d -> t (h d)"), self.ztok_s[t0:t0 + 128, :], writes=[ztB])
                g_n = bg[:, n, 32:64]
                P.op("pool", f_tt(G, Gm[:], bc_t(g_n, 128), bc_c(SL_, 32), ALU.mult), [bgB, cB], [GmB])
                P.op("pool", f_tt(G, zg[:], zt[:], bc_c(onr[:, :], 32), ALU.mult), [ztB, onB], [zgB])
                eps_, epB = self.psq.get()
                P.op("pe", f_mm(nc, eps_[:, 0:32], U_, g_n, True, True), [bgB, cB], [epB])
                P.op("pe", f_mm(nc, eps_[:, 32:64], SL_, g_n, True, True), [bgB, cB], [epB])
                P.op("pe", f_mm(nc, eps_[:, 64:96], self.ones_f, g_n, True, True), [bgB, cB], [epB])
                P.op("act", f_act(nc, eAll[:], eps_[:, 0:96], AF.Exp), [epB], [eB])
                WAVE = 4
                for w0 in range(0, 32, WAVE):
                    gens = []
                    for hk in range(w0 // 2, (w0 + WAVE) // 2):
                        kk = mmq(kT[:, hk, :], kT[:, hk, :], [kTB])
                        qkt = mmq(kT[:, hk, :], qT[:, hk, :], [kTB, qTB])
                        gens.append(head_gen(n, 2 * hk, kk, qkt))
                        gens.append(head_gen(n, 2 * hk + 1, kk, qkt))
                    alive = list(gens)
                    while alive:
                        nxt = []
                        for g in alive:
                            try:
                                next(g)
                                nxt.append(g)
                            except StopIteration:
                                pass
                        alive = nxt
                P.dma("sp", self.ogT_s[:, t0:t0 + 128].rearrange("(h p) t -> p h t", p=128), ogT[:], reads=[ogB])
            P.flush()

    def rot_tables(self, es):
        nc, P = self.nc, self.P
        V = nc.vector
        C32 = sb(nc, es, "C32", [32, T], F32)
        S32 = sb(nc, es, "S32", [32, T], F32)
        rB = Buf("rot")
        with ExitStack() as es2:
            posi = sb(nc, es2, "posi", [32, T], I32)
            ang = sb(nc, es2, "ang", [32, T], F32)
            tmp = sb(nc, es2, "rtmp", [32, T], F32)
            ang2 = sb(nc, es2, "ang2", [32, T], F32)
            ivf = sb(nc, es2, "ivf", [32, 1], F32)
            pB = Buf("posi")
            src = bass.AP(self.pos.tensor, 0, [[0, 32], [1, T]])
            P.dma("sp", posi[:], src, writes=[pB])
            P.dma("sp", ivf[:], self.c_invf[0:32, :], writes=[pB])
            P.op("dve", f_copy(V, ang[:], posi[:]), [pB], [pB])
            P.op("dve", f_ts(V, ang[:], ang[:], ivf[:, 0:1], None, ALU.mult), [pB], [pB])
            pi = float(np.pi)
            for (dst, off) in ((S32, 0.0), (C32, 0.5 * pi)):
                P.op("dve", f_ts(V, ang2[:], ang[:], off, None, ALU.add), [pB], [pB])
                P.op("dve", f_ts(V, tmp[:], ang2[:], 1.0 / (2.0 * pi), None, ALU.mult), [pB], [pB])
                P.op("dve", f_copy(V, posi[:], tmp[:]), [pB], [pB])
                P.op("dve", f_copy(V, tmp[:], posi[:]), [pB], [pB])
                P.op("dve", f_stt(V, ang2[:], tmp[:], -2.0 * pi, ang2[:], ALU.mult, ALU.add), [pB], [pB])
                P.op("dve", f_ts(V, tmp[:], ang2[:], pi, 2.0 * pi, ALU.is_gt, ALU.mult), [pB], [pB])
                P.op("dve", f_tt(V, ang2[:], ang2[:], tmp[:], ALU.subtract), [pB], [pB])
                P.op("dve", f_ts(V, tmp[:], ang2[:], -pi, 2.0 * pi, ALU.is_lt, ALU.mult), [pB], [pB])
                P.op("dve", f_tt(V, ang2[:], ang2[:], tmp[:], ALU.add), [pB], [pB])
                P.op("act", f_act(nc, dst[:], ang2[:], AF.Sin), [pB], [rB, pB])
            P.flush()
        return C32, S32, rB

    def qk_pools(self, es):
        nc = self.nc
        return dict(sv=sb_rot(nc, es, "qsv", 2, [128, 512], F32), sq=sb_rot(nc, es, "qsq", 1, [128, 512], BF16),
                    rs=sb_rot(nc, es, "qrs", 1, [128, 512], F32), qb=sb_rot(nc, es, "qqb", 2, [128, 512], BF16),
                    t1=sb_rot(nc, es, "qt1", 1, [32, 512], F32), t2=sb_rot(nc, es, "qt2", 1, [32, 512], F32),
                    o=sb_rot(nc, es, "qo", 2, [128, 512], BF16))

    def qk_post(self, pl, ps, pb, gcol, gB, rot, t0, dst):
        nc, P = self.nc, self.P
        V, G = nc.vector, nc.gpsimd
        C32, S32, rB = rot
        sv, svB = pl["sv"].get()
        P.op("act", (lambda sv=sv, ps=ps: nc.scalar.copy(out=sv[:], in_=ps[:])), [pb], [svB])
        sq, sqB = pl["sq"].get()
        P.op("act", f_act(nc, sq[:], ps[:], AF.Square), [pb], [sqB])
        ps2, pb2 = self.bank(4, 6)
        P.op("pe", f_mm(nc, ps2[:], self.ones_b, sq[:], True, True), [sqB, self.cB], [pb2])
        rs, rsB = pl["rs"].get()
        self.rsqrt(rs[:], rsB, ps2[:], pb2, 128.0 * EPS)
        qb, qbB = pl["qb"].get()
        P.op("dve", f_stt(V, qb[:], sv[:], gcol, rs[:], ALU.mult, ALU.mult), [svB, rsB, gB], [qbB])
        ps3, pb3 = self.bank(6, 8)
        P.op("pe", f_mm(nc, ps3[0:32, :], self.rotm[:, 0:32], qb[:], True, True), [qbB, self.cB], [pb3])
        t1, t1B = pl["t1"].get()
        P.op("pool", f_tt(G, t1[:], qb[0:32, :], C32[:, t0:t0 + 512], ALU.mult), [qbB, rB], [t1B])
        t2, t2B = pl["t2"].get()
        P.op("dve", f_tt(V, t2[:], ps3[0:32, :], S32[:, t0:t0 + 512], ALU.mult), [pb3, rB], [t2B])
        o, oB = pl["o"].get()
        P.op("pool", f_copy(G, o[:], qb[:]), [qbB], [oB])
        P.op("pool", f_tt(G, o[0:32, :], t1[:], t2[:], ALU.add), [t1B, t2B, oB], [oB])
        P.dma("sp", dst, o[:], reads=[oB])

    def st_kv(self, x_in):
        nc, P = self.nc, self.P
        with ExitStack() as es:
            rot = self.rot_tables(es)
            hT = sb(nc, es, "hT", [128, 16, T], BF16)
            hB = [Buf("hT%d" % i) for i in range(8)]
            with ExitStack() as es2:
                pools = self.norm_pools(es2, 1, 256)
                for tt in range(16):
                    self.norm_tile(pools, x_in, tt * 256, hT[:, :, tt * 256:(tt + 1) * 256], hB[tt // 2], 8, 256)
                P.flush()
            with ExitStack() as es2:
                wvp = sb_rot(nc, es2, "wv", 1, [128, 16, 512], BF16)
                vo = sb_rot(nc, es2, "vo", 4, [128, 512], BF16)
                for gi, dil in enumerate((1, 4, 16)):
                    wt, wb = wvp.get()
                    P.dma("pool", wt[:], self.wkvv[gi].rearrange("p (c n) -> p c n", c=16), writes=[wb])
                    nbr = 32 // dil
                    for blk in range(32):
                        r, nb = blk // nbr, blk % nbr
                        st_ = r + dil * 128 * nb
                        ps, pb = self.bank(0, 4)
                        for kc in range(16):
                            P.op("pe", f_mm(nc, ps[:], hT[:, kc, st_:st_ + dil * 127 + 1:dil], wt[:, kc, :], kc == 0, kc == 15),
                                 hB + [wb], [pb])
                        v, vB = vo.get()
                        P.op("act", (lambda v=v, ps=ps: nc.scalar.copy(out=v[:], in_=ps[:])), [pb], [vB])
                        P.dma("sp", self.Vb_s[gi, blk], v[:], reads=[vB])
                P.flush()
            with ExitStack() as es2:
                pl = self.qk_pools(es2)
                gk = sb(nc, es2, "gk", [128, 3], F32)
                gB = Buf("gk")
                P.dma("sp", gk[:], self.knorm[:, :], writes=[gB])
                P.op("dve", f_ts(nc.vector, gk[:], gk[:], float(np.sqrt(128.0)), None, ALU.mult), [gB], [gB])
                wp = sb_rot(nc, es2, "wk", 3, [128, 16, 128], BF16)
                for c in range(12):
                    wt, wb = wp.get()
                    P.dma("pool", wt[:], self.wkvk[c].rearrange("p (c n) -> p c n", c=16), writes=[wb])
                    for tt in range(8):
                        ps, pb = self.bank(0, 4)
                        for kc in range(16):
                            P.op("pe", f_mm(nc, ps[:], wt[:, kc, :], hT[:, kc, tt * 512:(tt + 1) * 512], kc == 0, kc == 15),
                                 [wb, hB[tt]], [pb])
                        self.qk_post(pl, ps, pb, gk[:, c // 4:c // 4 + 1], gB, rot, tt * 512,
                                     self.KT_s[c, :, tt * 512:(tt + 1) * 512])
                P.flush()

    def st_q(self, l, x_in):
        nc, P = self.nc, self.P
        j = l - 2
        with ExitStack() as es:
            rot = self.rot_tables(es)
            hT = sb(nc, es, "hT", [128, 16, T], BF16)
            hB = [Buf("hT%d" % i) for i in range(8)]
            with ExitStack() as es2:
                pools = self.norm_pools(es2, 1, 256)
                for tt in range(16):
                    self.norm_tile(pools, x_in, tt * 256, hT[:, :, tt * 256:(tt + 1) * 256], hB[tt // 2], l * 2, 256)
                P.flush()
            with ExitStack() as es2:
                pl = self.qk_pools(es2)
                gq = sb(nc, es2, "gq", [128, 3], F32)
                gB = Buf("gq")
                P.dma("sp", gq[:], self.qnorm[j], writes=[gB])
                P.op("dve", f_ts(nc.vector, gq[:], gq[:], float(np.sqrt(128.0)), None, ALU.mult), [gB], [gB])
                wp = sb_rot(nc, es2, "wq", 3, [128, 16, 128], BF16)
                for c in range(48):
                    wt, wb = wp.get()
                    P.dma("pool", wt[:], self.wq[j, c].rearrange("p (c n) -> p c n", c=16), writes=[wb])
                    for tt in range(8):
                        ps, pb = self.bank(0, 4)
                        for kc in range(16):
                            P.op("pe", f_mm(nc, ps[:], wt[:, kc, :], hT[:, kc, tt * 512:(tt + 1) * 512], kc == 0, kc == 15),
                                 [wb, hB[tt]], [pb])
                        self.qk_post(pl, ps, pb, gq[:, c // 16:c // 16 + 1], gB, rot, tt * 512,
                                     self.QT_s[c, :, tt * 512:(tt + 1) * 512])
                P.flush()

    def st_attn(self):
        nc, P = self.nc, self.P
        V, G = nc.vector, nc.gpsimd
        HALF = 2048
        sc_scale = float(128.0 ** -0.5)
        with ExitStack() as es:
            accO = sb(nc, es, "accO", [128, 4, HALF], F32)
            accD = sb(nc, es, "accD", [128, 4, HALF], F32)
            aB, dB_ = Buf("accO"), Buf("accD")
            Qp = sb_rot(nc, es, "aQ", 2, [128, 4, HALF], BF16)
            Kp = sb_rot(nc, es, "aK", 2, [128, T], BF16)
            Vp = sb_rot(nc, es, "aV", 2, [128, 32, 128], BF16)
            PTp = sb_rot(nc, es, "aPT", 3, [128, 2, 4, 128], BF16)
            aop = sb_rot(nc, es, "aao", 1, [128, 4, HALF], BF16)
            cB = self.cB
            mask = self.maskPC.rearrange("p (b q) -> p b q", b=2)
            unit = 0
            for kvh in range(4):
                for a in range(2):
                    for gi, dil in enumerate((1, 4, 16)):
                        Q, QB = Qp.get()
                        for hq in range(4):
                            P.dma("sp", Q[:, hq, :], self.QT_s[gi * 16 + kvh * 4 + hq, :, a * HALF:(a + 1) * HALF], writes=[QB])
                        Kt, KB = Kp.get()
                        P.dma("sp", Kt[:], self.KT_s[gi * 4 + kvh], writes=[KB])
                        Vt, VB = Vp.get()
                        P.dma("sp", Vt[:], self.Vb_s[gi, :, :, kvh * 128:(kvh + 1) * 128].rearrange("b p d -> p b d"), writes=[VB])
                        nbr = 32 // dil
                        nbh = nbr // 2
                        for r in range(dil):
                            for nbi in range(nbh):
                                nb = a * nbh + nbi
                                blk = r * nbr + nb
                                ql = r + dil * 128 * nbi
                                qsl = slice(ql, ql + dil * 127 + 1, dil)
                                kc0 = r + dil * 128 * nb
                                ksl = slice(kc0, kc0 + dil * 127 + 1, dil)
                                has_prev = nb > 0
                                di = unit % 2
                                unit += 1
                                sc, scB = self.psd[di], self.psB[di * 2]
                                qr = Q[:, :, qsl]
                                if has_prev:
                                    kp0 = kc0 - dil * 128
                                    P.op("pe", f_mm(nc, sc[:, 0:512], Kt[:, kp0:kp0 + dil * 127 + 1:dil], qr, True, True), [KB, QB], [scB])
                                P.op("pe", f_mm(nc, sc[:, 512:1024], Kt[:, ksl], qr, True, True), [KB, QB], [scB])
                                PT, PB = PTp.get()
                                lo = 0 if has_prev else 1
                                P.op("act", f_act(nc, PT[:, lo:2].rearrange("p b h q -> p (b h q)"), sc[:, lo * 512:1024], AF.Exp, scale=sc_scale),
                                     [scB], [PB])
                                eng, en = (V, "dve") if unit % 2 == 0 else (G, "pool")
                                mk = mask[:, lo:2, :].unsqueeze(2).to_broadcast([128, 2 - lo, 4, 128])
                                P.op(en, f_tt(eng, PT[:, lo:2], PT[:, lo:2], mk, ALU.mult), [PB, cB], [PB])
                                ops_, opB = self.psb[4 + di * 2], self.psB[4 + di * 2]
                                dps_, dpB = self.psb[5 + di * 2], self.psB[5 + di * 2]
                                cur = PT[:, 1].rearrange("p h q -> p (h q)")
                                if has_prev:
                                    prv = PT[:, 0].rearrange("p h q -> p (h q)")
                                    P.op("pe", f_mm(nc, ops_[:], Vt[:, blk - 1, :], prv, True, False), [VB, PB], [opB])
                                    P.op("pe", f_mm(nc, ops_[:], Vt[:, blk, :], cur, False, True), [VB, PB], [opB])
                                    P.op("pe", f_mm(nc, dps_[:], self.ones_b, prv, True, False), [cB, PB], [dpB])
                                    P.op("pe", f_mm(nc, dps_[:], self.ones_b, cur, False, True), [cB, PB], [dpB])
                                else:
                                    P.op("pe", f_mm(nc, ops_[:], Vt[:, blk, :], cur, True, True), [VB, PB], [opB])
                                    P.op("pe", f_mm(nc, dps_[:], self.ones_b, cur, True, True), [cB, PB], [dpB])
                                ov = accO[:, :, qsl]
                                dv = accD[:, :, qsl]
                                o3 = ops_.rearrange("p (h q) -> p h q", h=4)
                                d3 = dps_.rearrange("p (h q) -> p h q", h=4)
                                if gi == 0:
                                    P.op("act", (lambda ov=ov, o3=o3: nc.scalar.copy(out=ov, in_=o3)), [opB], [aB])
                                    P.op("dve", f_copy(V, dv, d3), [dpB], [dB_])
                                else:
                                    P.op("dve", f_tt(V, ov, o3, ov, ALU.add), [opB, aB], [aB])
                                    P.op("dve", f_tt(V, dv, d3, dv, ALU.add), [dpB, dB_], [dB_])
                    P.op("dve", (lambda: nc.vector.reciprocal(out=accD[:], in_=accD[:])), [dB_], [dB_])
                    ao, aoB = aop.get()
                    P.op("pool", f_tt(G, ao[:], accO[:], accD[:], ALU.mult), [aB, dB_], [aoB])
                    for hq in range(4):
                        h = kvh * 4 + hq
                        P.dma("sp", self.aoT_s[h * 128:(h + 1) * 128, a * HALF:(a + 1) * HALF], ao[:, hq, :], reads=[aoB])
            P.flush()


def build(dbg=False, stages=None):
    k = K(dbg)
    k.declare()
    nc, P = k.nc, k.P
    allst = stages is None

    def want(s):
        return allst or s in stages

    with ExitStack() as es0:
        k.load_consts(es0)
        k.st_mods()
        for l in range(2):
            x_in = k.xT if l == 0 else k.yT
            if want("gdn%d" % l):
                with ExitStack() as esl:
                    bg = sb(nc, esl, "bg", [128, 32, 64], F32)
                    bgB = Buf("bg")
                    if want("gproj%d" % l):
                        k.st_gdn_proj(l, x_in, bg, bgB)
                    if want("gchunk%d" % l):
                        k.st_gdn_chunk(l, bg, bgB, n_chunks=(k.n_chunks if hasattr(k, "n_chunks") else 32))
            if want("gout%d" % l):
                k.st_proj_resid(k.ogT_s, 32, k.wout[l], k.gate(l, 0), x_in, k.yT)
            if want("mlp%d" % l):
                k.st_mlp(l, k.yT, k.yT)
        if (not allst) and "copyin" in stages:
            for c in range(16):
                P.dma("sp", k.yT[c * 128:(c + 1) * 128, :], k.xT[c * 128:(c + 1) * 128, :])
            P.flush()
        if want("kv"):
            k.st_kv(k.yT)
        for l in range(2, 4):
            if want("q%d" % l):
                k.st_q(l, k.yT)
            if want("attn%d" % l):
                k.st_attn()
            if want("aout%d" % l):
                k.st_proj_resid(k.aoT_s, 16, k.wo[l - 2], k.gate(l, 0), k.yT, k.yT)
            if want("mlp%d" % l):
                k.st_mlp(l, k.yT, k.yT)
        P.flush()
        P.barrier()
    return k


def tile_w(W, ncols):
    Kd, N = W.shape
    return np.ascontiguousarray(
        W.reshape(Kd // 128, 128, N // ncols, ncols).transpose(2, 1, 0, 3)).reshape(N // ncols, 128, (Kd // 128) * ncols)


def make_consts():
    import ml_dtypes
    i = np.arange(128)
    k_, m_ = i[:, None], i[None, :]
    ident = np.eye(128, dtype=np.float32)
    U = (k_ <= m_).astype(np.float32)
    SL = (k_ > m_).astype(np.float32)
    SU = (k_ < m_).astype(np.float32)
    BD32 = ((k_ // 32) == (m_ // 32)).astype(np.float32)
    rb, cb = k_ // 32, m_ // 32
    OFF1T = (((rb == 1) & (cb == 0)) | ((rb == 3) & (cb == 2))).astype(np.float32)
    OFF2T = ((rb >= 2) & (cb < 2)).astype(np.float32)
    ones = np.ones((128, 128), np.float32)
    z = np.zeros((128, 128), np.float32)
    c_f32 = np.concatenate([ident, U, SL, SU, BD32, OFF1T, OFF2T, ones, z, z], axis=1)
    rotm = np.zeros((128, 128), np.float32)
    for m in range(16):
        rotm[m + 16, m] = -1.0
        rotm[m, m + 16] = 1.0
    maskP = (k_ >= m_).astype(np.float32)
    maskC = (k_ <= m_).astype(np.float32)
    c_bf = np.concatenate([ident, ones, rotm, maskP, maskC], axis=1).astype(ml_dtypes.bfloat16)
    inv = (500000.0 ** (-np.arange(0, 32, 2, dtype=np.float32) / 32.0)).astype(np.float32)
    invf = np.zeros((128, 1), np.float32)
    invf[0:32, 0] = np.concatenate([inv, inv])
    return dict(c_f32=np.ascontiguousarray(c_f32), c_bf=np.ascontiguousarray(c_bf), c_invf=invf)


def prep_shared(inp):
    f = lambda a: np.ascontiguousarray(np.asarray(a, dtype=np.float32))
    sh = {}
    ada_w = f(inp["ada_w"])
    sh["ada_w_t"] = np.stack([tile_w(ada_w[l], 512) for l in range(4)])
    sh["ada_b_t"] = np.stack([f(inp["ada_b"][l]).reshape(96, 128).T for l in range(4)]).copy()
    sh["kvada_w_t"] = tile_w(f(inp["kv_ada_w"]), 512)
    sh["kvada_b_t"] = f(inp["kv_ada_b"]).reshape(32, 128).T.copy()
    ng = f(inp["norm_g"])
    sh["normg_t"] = np.ascontiguousarray(ng.reshape(4, 2, 16, 128).transpose(0, 1, 3, 2))
    sh["kvnormg_t"] = f(inp["kv_norm_g"]).reshape(16, 128).T.copy()
    w1, w2 = f(inp["mlp_w1"]), f(inp["mlp_w2"])
    sh["mlp_w1_t"] = np.stack([tile_w(w1[l], 128) for l in range(4)])
    sh["mlp_w2_t"] = np.stack([np.stack([tile_w(w2[l][h * 4096:(h + 1) * 4096], 128) for h in range(2)], axis=1)
                               for l in range(4)])
    win = f(inp["gdn_w_in"])
    sh["gdn_win_fm"] = np.stack([tile_w(win[l][:, 0:8192], 128) for l in range(2)])
    sh["gdn_wz_t"] = np.stack([tile_w(win[l][:, 8192:12288], 512) for l in range(2)])
    sh["gdn_wba_t"] = np.stack([tile_w(win[l][:, 12288:12352], 64)[0] for l in range(2)])
    cw = f(inp["gdn_conv_w"])
    sh["gdn_conv_t"] = np.ascontiguousarray(cw.reshape(2, 4, 64, 128).transpose(0, 3, 2, 1)).reshape(2, 128, 256)
    sh["gdn_alog_b"] = np.ascontiguousarray(np.broadcast_to(f(inp["gdn_a_log"])[:, None, :], (2, 128, 32)))
    sh["gdn_dtb_b"] = np.ascontiguousarray(np.broadcast_to(f(inp["gdn_dt_bias"])[:, None, :], (2, 128, 32)))
    sh["gdn_onorm_b"] = np.ascontiguousarray(np.broadcast_to(f(inp["gdn_onorm_g"])[:, None, :], (2, 128, 128)))
    wout = f(inp["gdn_w_out"])
    sh["gdn_wout_t"] = np.stack([tile_w(wout[l], 128) for l in range(2)])
    wkv = f(inp["w_kv"])
    sh["wkv_k_t"] = np.stack([tile_w(wkv[:, gi * 1024 + h * 128: gi * 1024 + (h + 1) * 128], 128)[0]
                              for gi in range(3) for h in range(4)])
    sh["wkv_v_t"] = np.stack([tile_w(wkv[:, gi * 1024 + 512: gi * 1024 + 1024], 512)[0] for gi in range(3)])
    sh["knorm_t"] = f(inp["k_norm_g"]).T.copy()
    sh["qnorm_t"] = np.ascontiguousarray(f(inp["q_norm_g"]).transpose(0, 2, 1))
    wq, wo = f(inp["attn_w_q"]), f(inp["attn_w_o"])
    sh["attn_wq_t"] = np.stack([tile_w(wq[j], 128) for j in range(2)])
    sh["attn_wo_t"] = np.stack([tile_w(wo[j], 128) for j in range(2)])
    sh.update(make_consts())
    return sh


def prep_core(inp, b):
    d = {}
    d["xT"] = np.ascontiguousarray(np.asarray(inp["x"][b], dtype=np.float32).T)
    d["ccol"] = np.ascontiguousarray(np.asarray(inp["c"][b], dtype=np.float32).reshape(16, 128).T)
    d["pos"] = np.ascontiguousarray(np.asarray(inp["positions"][b], dtype=np.int32)[None, :])
    return d


def kernel(**inputs):
    k = build()
    shared = prep_shared(inputs)
    in_maps = []
    for b in range(NCORES):
        m = dict(shared)
        m.update(prep_core(inputs, b))
        in_maps.append(m)
    res = run_bass_kernel_spmd(k.nc, in_maps, core_ids=list(range(NCORES)))
    out = np.stack([np.ascontiguousarray(np.asarray(r["yT"]).T) for r in res.results])
    return out.astype(np.float32)
```

```python
import numpy as np
from contextlib import ExitStack
import concourse.bass as bass
import concourse.mybir as mybir
from concourse.bass_utils import run_bass_kernel_spmd

F32 = mybir.dt.float32
BF16 = mybir.dt.bfloat16
I32 = mybir.dt.int32
AF = mybir.ActivationFunctionType
ALU = mybir.AluOpType
AX = mybir.AxisListType

T = 4096
D = 2048
KC = 16
NCORES = 8
EPS = 1e-6
SQD = float(np.sqrt(2048.0))


class Buf:
    __slots__ = ("name", "lw", "rd", "excl")

    def __init__(self, name="", excl=False):
        self.name = name
        self.lw = None
        self.rd = {}
        self.excl = excl


class Op:
    __slots__ = ("eng", "fn", "deps", "is_dma", "signal", "ticket", "dsem", "dval", "emitted")

    def __init__(self, eng, fn, is_dma):
        self.eng = eng
        self.fn = fn
        self.deps = []
        self.is_dma = is_dma
        self.signal = False
        self.ticket = None
        self.dsem = None
        self.dval = None
        self.emitted = False


class Prog:
    COMPUTE = ("pe", "act", "dve", "pool")
    ALLENG = ("pe", "act", "dve", "pool", "sp")

    def __init__(self, nc, n_dma_sems=32):
        self.nc = nc
        self.ops = []
        self.start = 0
        self.n_dma_sems = n_dma_sems
        self.eng_obj = {"pe": nc.tensor, "act": nc.scalar, "dve": nc.vector,
                        "pool": nc.gpsimd, "sp": nc.sync}
        self.sems = {e: nc.alloc_semaphore("s_" + e) for e in self.COMPUTE}
        self.cnt = {e: 0 for e in self.COMPUTE}
        self.dma_sems = [nc.alloc_semaphore("s_dma%d" % i) for i in range(n_dma_sems)]
        self.dma_val = [0] * n_dma_sems
        self.dma_rr = 0
        self.waited = {e: {} for e in self.ALLENG}
        self.n_inst = 0

    def op(self, eng, fn, reads=(), writes=(), dma=False):
        o = Op(eng, fn, dma)
        idx = len(self.ops)
        deps = {}
        ex = [b for b in reads if b.excl]
        if ex:
            writes = list(writes) + [b for b in ex if b not in writes]
            reads = [b for b in reads if not b.excl]
            for b in ex:
                if b.lw is not None:
                    deps[b.lw] = True
        for b in reads:
            if b.lw is not None:
                deps[b.lw] = True
        for b in writes:
            if b.lw is not None:
                deps.setdefault(b.lw, False)
            for r in b.rd.values():
                deps.setdefault(r, False)
        key = ("dma", idx) if dma else eng
        for b in reads:
            b.rd[key] = idx
        for b in writes:
            b.lw = idx
            b.rd = {}
        deps.pop(idx, None)
        o.deps = sorted(deps.items())
        self.ops.append(o)
        return idx

    def dma(self, q, out, in_, reads=(), writes=()):
        qe = self.eng_obj[q]
        return self.op(q, lambda: qe.dma_start(out=out, in_=in_), reads, writes, dma=True)

    def _wait(self, eng, key, sem, val):
        w = self.waited[eng]
        if w.get(key, 0) >= val:
            return
        self.eng_obj[eng].wait_ge(sem, val)
        self.n_inst += 1
        w[key] = val

    def flush(self, barrier=True):
        ops = self.ops
        new = ops[self.start:]
        for o in new:
            for d, raw in o.deps:
                p = ops[d]
                if p.is_dma or p.emitted:
                    continue
                if p.eng == o.eng and not raw and not o.is_dma:
                    continue
                p.signal = True
        last = {}
        for o in new:
            if not o.is_dma:
                last[o.eng] = o
        for o in last.values():
            o.signal = True
        pending_cover = {e: [] for e in self.COMPUTE}
        for o in new:
            e = o.eng
            for d, raw in o.deps:
                p = ops[d]
                if p.is_dma:
                    self._wait(e, ("d", p.dsem), self.dma_sems[p.dsem], p.dval)
                else:
                    if p.eng == e and not raw and not o.is_dma:
                        continue
                    assert p.ticket is not None, (p.eng, e)
                    self._wait(e, p.eng, self.sems[p.eng], p.ticket)
            if o.is_dma:
                i = self.dma_rr
                self.dma_rr = (self.dma_rr + 1) % self.n_dma_sems
                if self.dma_val[i] > 0:
                    self._wait(e, ("d", i), self.dma_sems[i], self.dma_val[i])
                ins = o.fn()
                self.dma_val[i] += 16
                ins.then_inc(self.dma_sems[i], 16)
                o.dsem = i
                o.dval = self.dma_val[i]
            else:
                ins = o.fn()
                if o.signal:
                    self.cnt[e] += 1
                    o.ticket = self.cnt[e]
                    ins.then_inc(self.sems[e], 1)
                    for q in pending_cover[e]:
                        q.ticket = o.ticket
                    pending_cover[e] = []
                else:
                    pending_cover[e].append(o)
            self.n_inst += 1
            o.emitted = True
            o.fn = None
        self.start = len(ops)
        if barrier:
            self.barrier()

    def barrier(self):
        for e in self.ALLENG:
            for f in self.COMPUTE:
                if f != e and self.cnt[f] > 0:
                    self._wait(e, f, self.sems[f], self.cnt[f])
            for i in range(self.n_dma_sems):
                if self.dma_val[i] > 0:
                    self._wait(e, ("d", i), self.dma_sems[i], self.dma_val[i])


class Rot:
    def __init__(self, tiles):
        self.tiles = tiles
        self.k = 0

    def get(self):
        t = self.tiles[self.k % len(self.tiles)]
        self.k += 1
        return t


_uid = [0]


def _nm(name):
    _uid[0] += 1
    return "%s_u%d" % (name, _uid[0])


def sb_rot(nc, es, name, n, shape, dtype):
    tiles = []
    for i in range(n):
        t = es.enter_context(nc.sbuf_tensor(_nm(name), shape, dtype))
        tiles.append((t, Buf("%s%d" % (name, i))))
    return Rot(tiles)


def sb(nc, es, name, shape, dtype):
    return es.enter_context(nc.sbuf_tensor(_nm(name), shape, dtype))


def f_mm(nc, out, lhsT, rhs, start, stop):
    return lambda: nc.tensor.matmul(out, lhsT, rhs, start=start, stop=stop)


def f_tt(eng, out, in0, in1, op):
    return lambda: eng.tensor_tensor(out=out, in0=in0, in1=in1, op=op)


def f_ts(eng, out, in0, s1, s2, op0, op1=None):
    if op1 is None:
        return lambda: eng.tensor_scalar(out=out, in0=in0, scalar1=s1, scalar2=None, op0=op0)
    return lambda: eng.tensor_scalar(out=out, in0=in0, scalar1=s1, scalar2=s2, op0=op0, op1=op1)


def f_stt(eng, out, in0, scalar, in1, op0, op1):
    return lambda: eng.scalar_tensor_tensor(out=out, in0=in0, scalar=scalar, in1=in1, op0=op0, op1=op1)


def f_act(nc, out, in_, func, bias=None, scale=None, accum_out=None):
    kw = {}
    if bias is not None:
        kw["bias"] = bias
    if scale is not None:
        kw["scale"] = scale
    if accum_out is not None:
        kw["accum_out"] = accum_out
    return lambda: nc.scalar.activation(out=out, in_=in_, func=func, **kw)


def f_copy(eng, out, in_):
    return lambda: eng.tensor_copy(out=out, in_=in_)


def f_memset(eng, ap, v):
    return lambda: eng.memset(ap, v)


def bc_t(col2d, n):
    return col2d.unsqueeze(2).to_broadcast([col2d.shape[0], col2d.shape[1], n])


def bc_c(row2d, c):
    return row2d.unsqueeze(1).to_broadcast([row2d.shape[0], c, row2d.shape[1]])


class K:
    def __init__(self, dbg=False):
        self.dbg = dbg
        nc = self.nc = bass.Bass("TRN2", target_bir_lowering=False)
        self.P = Prog(nc)
        self.inputs = {}
        self.psd = [nc.alloc_psum_tensor("psd%d" % i, [128, 1024], F32) for i in range(4)]
        self.psb = [self.psd[i // 2][:, (i % 2) * 512:(i % 2 + 1) * 512] for i in range(8)]
        self.psB = [Buf("psB%d" % i, excl=True) for i in range(8)]
        self.psq = Rot([(self.psb[i // 4][:, (i % 4) * 128:(i % 4 + 1) * 128], self.psB[i // 4])
                        for i in range(32)])
        self.bank_rr = 0

    def din(self, name, shape, dt=F32):
        t = self.nc.dram_tensor(name, list(shape), dt, kind="ExternalInput").ap()
        self.inputs[name] = (tuple(shape), dt)
        return t

    def dscr(self, name, shape, dt):
        isout = self.dbg is True or (isinstance(self.dbg, (set, list, tuple)) and any(name.startswith(n) for n in self.dbg))
        kind = "ExternalOutput" if isout else "Internal"
        return self.nc.dram_tensor(name, list(shape), dt, kind=kind).ap()

    def bank(self, lo=0, hi=8):
        i = lo + (self.bank_rr % (hi - lo))
        self.bank_rr += 1
        return self.psb[i], self.psB[i]

    def declare(self):
        d = self.din
        self.xT = d("xT", [D, T])
        self.ccol = d("ccol", [128, 16])
        self.pos = d("pos", [1, T], I32)
        self.ada_w = d("ada_w_t", [4, 24, 128, 16 * 512])
        self.ada_b = d("ada_b_t", [4, 128, 96])
        self.kvada_w = d("kvada_w_t", [8, 128, 16 * 512])
        self.kvada_b = d("kvada_b_t", [128, 32])
        self.normg = d("normg_t", [4, 2, 128, 16])
        self.kvnormg = d("kvnormg_t", [128, 16])
        self.w1 = d("mlp_w1_t", [4, 64, 128, 16 * 128])
        self.w2 = d("mlp_w2_t", [4, 16, 2, 128, 32 * 128])
        self.win_fm = d("gdn_win_fm", [2, 64, 128, 16 * 128])
        self.wz = d("gdn_wz_t", [2, 8, 128, 16 * 512])
        self.wba = d("gdn_wba_t", [2, 128, 16 * 64])
        self.convw = d("gdn_conv_t", [2, 128, 64 * 4])
        self.alog = d("gdn_alog_b", [2, 128, 32])
        self.dtb = d("gdn_dtb_b", [2, 128, 32])
        self.onorm = d("gdn_onorm_b", [2, 128, 128])
        self.wout = d("gdn_wout_t", [2, 16, 128, 32 * 128])
        self.wkvk = d("wkv_k_t", [12, 128, 16 * 128])
        self.wkvv = d("wkv_v_t", [3, 128, 16 * 512])
        self.knorm = d("knorm_t", [128, 3])
        self.qnorm = d("qnorm_t", [2, 128, 3])
        self.wq = d("attn_wq_t", [2, 48, 128, 16 * 128])
        self.wo = d("attn_wo_t", [2, 16, 128, 16 * 128])
        self.c_f32 = d("c_f32", [128, 10 * 128])
        self.c_bf = d("c_bf", [128, 5 * 128], BF16)
        self.c_invf = d("c_invf", [128, 1])
        self.yT = self.nc.dram_tensor("yT", [D, T], F32, kind="ExternalOutput").ap()
        s = self.dscr
        self.qT_s = s("qT_s", [16, 128, T], BF16)
        self.kT_s = s("kT_s", [16, 128, T], BF16)
        self.ktok_s = s("ktok_s", [16, T, 128], BF16)
        self.vtok_s = s("vtok_s", [32, T, 128], BF16)
        self.ztok_s = s("ztok_s", [T, 4096], BF16)
        self.ogT_s = s("ogT_s", [4096, T], BF16)
        self.KT_s = s("KT_s", [12, 128, T], BF16)
        self.Vb_s = s("Vb_s", [3, 32, 128, 512], BF16)
        self.QT_s = s("QT_s", [48, 128, T], BF16)
        self.aoT_s = s("aoT_s", [2048, T], BF16)

    def load_consts(self, es):
        nc, P = self.nc, self.P
        self.cf = sb(nc, es, "cf", [128, 10 * 128], F32)
        self.cb = sb(nc, es, "cbf", [128, 5 * 128], BF16)
        self.cB = Buf("consts")
        P.dma("sp", self.cf[:], self.c_f32[:, :], writes=[self.cB])
        P.dma("sp", self.cb[:], self.c_bf[:, :], writes=[self.cB])
        cf, cb = self.cf, self.cb
        sl = lambda i: slice(i * 128, (i + 1) * 128)
        self.ident = cf[:, sl(0)]
        self.U = cf[:, sl(1)]
        self.SL = cf[:, sl(2)]
        self.SU = cf[:, sl(3)]
        self.BD32 = cf[:, sl(4)]
        self.OFF1T = cf[:, sl(5)]
        self.OFF2T = cf[:, sl(6)]
        self.ones_f = cf[:, sl(7)]
        self.ident_b = cb[:, sl(0)]
        self.ones_b = cb[:, sl(1)]
        self.rotm = cb[:, sl(2)]
        self.maskPC = cb[:, 3 * 128:5 * 128]
        self.modT = sb(nc, es, "modT", [128, 5, 96], F32)
        self.modB = Buf("modT")
        self.cols = sb(nc, es, "cols", [128, 9, 2, 16], F32)
        self.colsB = Buf("cols")

    def st_mods(self):
        nc, P = self.nc, self.P
        with ExitStack() as es:
            cc = sb(nc, es, "cc", [128, 16], F32)
            ccb = sb(nc, es, "ccb", [128, 16], BF16)
            cB = Buf("cc")
            P.dma("sp", cc[:], self.ccol[:, :], writes=[cB])
            P.op("act", f_act(nc, ccb[:], cc[:], AF.Silu), [cB], [cB])
            wp = sb_rot(nc, es, "mw", 3, [128, 16, 512], BF16)
            ng = sb(nc, es, "ng", [128, 9, 16], F32)
            ab = sb(nc, es, "ab", [128, 5, 96], F32)
            ngB = Buf("ng")
            for l in range(4):
                for j in range(2):
                    P.dma("sp", ng[:, l * 2 + j, :], self.normg[l, j], writes=[ngB])
                P.dma("sp", ab[:, l, :], self.ada_b[l], writes=[ngB])
            P.dma("sp", ng[:, 8, :], self.kvnormg[:, :], writes=[ngB])
            P.dma("sp", ab[:, 4, 0:32], self.kvada_b[:, :], writes=[ngB])
            for l in range(5):
                nblk = 24 if l < 4 else 8
                ps, pb = self.bank()
                for blk in range(nblk):
                    wt, wb = wp.get()
                    src = self.ada_w[l, blk] if l < 4 else self.kvada_w[blk]
                    P.dma("pool", wt[:], src.rearrange("p (c n) -> p c n", c=16), writes=[wb])
                    for j in range(4):
                        col = blk * 4 + j
                        for kc in range(16):
                            P.op("pe", f_mm(nc, ps[:, col:col + 1], wt[:, kc, j * 128:(j + 1) * 128],
                                            ccb[:, kc:kc + 1], kc == 0, kc == 15), [wb, cB], [pb])
                nco = nblk * 4
                P.op("dve", f_tt(nc.vector, self.modT[:, l, 0:nco], ps[:, 0:nco], ab[:, l, 0:nco], ALU.add),
                     [pb, ngB], [self.modB])
            for st in range(9):
                if st < 8:
                    l, j = st // 2, st % 2
                    sh = self.modT[:, l, (0 + 48 * j):(16 + 48 * j)]
                    sc = self.modT[:, l, (16 + 48 * j):(32 + 48 * j)]
                else:
                    sh = self.modT[:, 4, 0:16]
                    sc = self.modT[:, 4, 16:32]
                P.op("dve", f_stt(nc.vector, self.cols[:, st, 0, :], sc, 1.0, ng[:, st, :], ALU.add, ALU.mult),
                     [self.modB, ngB], [self.colsB])
                P.op("dve", f_ts(nc.vector, self.cols[:, st, 0, :], self.cols[:, st, 0, :], SQD, None, ALU.mult),
                     [self.colsB], [self.colsB])
                P.op("dve", f_copy(nc.vector, self.cols[:, st, 1, :], sh), [self.modB], [self.colsB])
            P.flush()

    def rsqrt(self, dst, dstB, src, srcB, c):
        nc, P = self.nc, self.P
        P.op("act", f_act(nc, dst, src, AF.Sqrt, bias=float(c)), [srcB], [dstB])
        P.op("dve", (lambda: nc.vector.reciprocal(out=dst, in_=dst)), [dstB], [dstB])

    def gate(self, l, j):
        return self.modT[:, l, (32 + 48 * j):(48 + 48 * j)]

    def norm_tile(self, pools, x_src, t0, dst, dstB, st, W=512):
        nc, P = self.nc, self.P
        xp, sqp, rp = pools
        xv = x_src.rearrange("(c p) t -> p c t", p=128)
        xt, xb = xp.get()
        P.dma("sp", xt[:], xv[:, :, t0:t0 + W], writes=[xb])
        sq, sqb = sqp.get()
        P.op("act", f_act(nc, sq[:], xt[:], AF.Square), [xb], [sqb])
        ps, pb = self.bank(6, 8)
        for kc in range(16):
            P.op("pe", f_mm(nc, ps[:, 0:W], self.ones_b, sq[:, kc, :], kc == 0, kc == 15), [sqb, self.cB], [pb])
        r, rb = rp.get()
        self.rsqrt(r[:], rb, ps[:, 0:W], pb, 2048.0 * EPS)
        A = self.cols[:, st, 0, :]
        sh = self.cols[:, st, 1, :]
        P.op("dve", f_tt(nc.vector, xt[:], xt[:], bc_t(A, W), ALU.mult), [xb, self.colsB], [xb])
        P.op("pool", f_tt(nc.gpsimd, xt[:], xt[:], bc_c(r[:, :], 16), ALU.mult), [xb, rb], [xb])
        P.op("dve", f_tt(nc.vector, dst, xt[:], bc_t(sh, W), ALU.add), [xb, self.colsB], [dstB])

    def norm_pools(self, es, n=1, W=512):
        nc = self.nc
        return (sb_rot(nc, es, "nx", n, [128, 16, W], F32),
                sb_rot(nc, es, "nsq", n, [128, 16, W], BF16),
                sb_rot(nc, es, "nr", n, [128, W], F32))

    def st_mlp(self, l, x_in, x_out):
        nc, P = self.nc, self.P
        TT = 1024
        with ExitStack() as es:
            pools = self.norm_pools(es, 1)
            hp = sb_rot(nc, es, "mh", 1, [128, 16, TT], BF16)
            hid = sb(nc, es, "hid", [128, 32, TT], BF16)
            hidB = [[Buf("hid") for _ in range(2)] for _ in range(32)]
            w1p = sb_rot(nc, es, "w1", 3, [128, 16, 128], BF16)
            w2p = sb_rot(nc, es, "w2", 2, [128, 32, 128], BF16)
            rl = sb_rot(nc, es, "rl", 3, [128, 512], BF16)
            xo = sb_rot(nc, es, "xo", 3, [128, 512], F32)
            gt = self.gate(l, 1)
            xiv = x_in.rearrange("(c p) t -> p c t", p=128)
            xov = x_out.rearrange("(c p) t -> p c t", p=128)
            unit = 0
            for tb in range(T // TT):
                h, hB0 = hp.get()
                hB = [Buf("mhs") for _ in range(2)]
                for s in range(2):
                    self.norm_tile(pools, x_in, tb * TT + s * 512, h[:, :, s * 512:(s + 1) * 512], hB[s], l * 2 + 1)
                for half in range(2):
                    for fc in range(32):
                        wt, wb = w1p.get()
                        P.dma("pool", wt[:], self.w1[l, half * 32 + fc].rearrange("p (c n) -> p c n", c=16), writes=[wb])
                        for s in range(2):
                            ps, pb = self.bank(0, 4)
                            for kc in range(16):
                                P.op("pe", f_mm(nc, ps[:], wt[:, kc, :], h[:, kc, s * 512:(s + 1) * 512], kc == 0, kc == 15),
                                     [wb, hB[s]], [pb])
                            dst = hid[:, fc, s * 512:(s + 1) * 512]
                            r, rb = rl.get()
                            P.op("act", f_act(nc, r[:], ps[:], AF.Relu), [pb], [rb])
                            if unit % 2 == 0:
                                P.op("dve", f_tt(nc.vector, dst, r[:], r[:], ALU.mult), [rb], [hidB[fc][s]])
                            else:
                                P.op("pool", f_tt(nc.gpsimd, dst, r[:], r[:], ALU.mult), [rb], [hidB[fc][s]])
                            unit += 1
                    xsrc = xiv if half == 0 else xov
                    if half == 0:
                        xtok = [[Buf("xtok") for _ in range(2)] for _ in range(16)]
                    for fo in range(16):
                        wt, wb = w2p.get()
                        P.dma("pool", wt[:], self.w2[l, fo, half].rearrange("p (c n) -> p c n", c=32), writes=[wb])
                        for s in range(2):
                            t0 = tb * TT + s * 512
                            xt, xb = xo.get()
                            P.dma("sp", xt[:], xsrc[:, fo, t0:t0 + 512], reads=[xtok[fo][s]], writes=[xb])
                            ps, pb = self.bank(4, 6)
                            for kc in range(32):
                                P.op("pe", f_mm(nc, ps[:], wt[:, kc, :], hid[:, kc, s * 512:(s + 1) * 512], kc == 0, kc == 31),
                                     [wb, hidB[kc][s]], [pb])
                            P.op("dve", f_stt(nc.vector, xt[:], ps[:], gt[:, fo:fo + 1], xt[:], ALU.mult, ALU.add),
                                 [pb, xb, self.modB], [xb])
                            P.dma("sp", xov[:, fo, t0:t0 + 512], xt[:], reads=[xb], writes=[xtok[fo][s]])
            P.flush()

    def st_proj_resid(self, actT, kcn, w_t, gate, x_in, x_out):
        nc, P = self.nc, self.P
        TT = 1024
        with ExitStack() as es:
            ap_ = sb_rot(nc, es, "pa", 1, [128, kcn, TT], BF16)
            wp = sb_rot(nc, es, "pw", 3, [128, kcn, 128], BF16)
            xo = sb_rot(nc, es, "px", 4, [128, 512], F32)
            av = actT.rearrange("(c p) t -> p c t", p=128)
            xiv = x_in.rearrange("(c p) t -> p c t", p=128)
            xov = x_out.rearrange("(c p) t -> p c t", p=128)
            for tb in range(T // TT):
                a, aB = ap_.get()
                for c0 in range(0, kcn, 8):
                    P.dma("sp", a[:, c0:c0 + 8, :], av[:, c0:c0 + 8, tb * TT:(tb + 1) * TT], writes=[aB])
                for fo in range(16):
                    wt, wb = wp.get()
                    P.dma("pool", wt[:], w_t[fo].rearrange("p (c n) -> p c n", c=kcn), writes=[wb])
                    for s in range(2):
                        t0 = tb * TT + s * 512
                        xt, xb = xo.get()
                        P.dma("sp", xt[:], xiv[:, fo, t0:t0 + 512], writes=[xb])
                        ps, pb = self.bank(0, 4)
                        for kc in range(kcn):
                            P.op("pe", f_mm(nc, ps[:], wt[:, kc, :], a[:, kc, s * 512:(s + 1) * 512], kc == 0, kc == kcn - 1),
                                 [wb, aB], [pb])
                        P.op("dve", f_stt(nc.vector, xt[:], ps[:], gate[:, fo:fo + 1], xt[:], ALU.mult, ALU.add),
                             [pb, xb, self.modB], [xb])
                        P.dma("sp", xov[:, fo, t0:t0 + 512], xt[:], reads=[xb])
            P.flush()

    def st_gdn_proj(self, l, x_in, bg, bgB):
        nc, P = self.nc, self.P
        with ExitStack() as es:
            hT = sb(nc, es, "hT", [128, 16, T], BF16)
            hB = [Buf("hT%d" % i) for i in range(8)]
            with ExitStack() as es2:
                pools = self.norm_pools(es2, 1)
                for tt in range(8):
                    self.norm_tile(pools, x_in, tt * 512, hT[:, :, tt * 512:(tt + 1) * 512], hB[tt], l * 2)
                P.flush()
            with ExitStack() as es2:
                wba = sb(nc, es2, "wba", [128, 16, 64], BF16)
                wbB = Buf("wba")
                P.dma("pool", wba[:], self.wba[l].rearrange("p (c n) -> p c n", c=16), writes=[wbB])
                prm = sb(nc, es2, "prm", [128, 2, 32], F32)
                prB = Buf("prm")
                P.dma("sp", prm[:, 0, :], self.alog[l], writes=[prB])
                P.dma("sp", prm[:, 1, :], self.dtb[l], writes=[prB])
                tmp = sb(nc, es2, "bgtmp", [128, 32, 32], F32)
                tB = Buf("bgtmp")
                for ts in range(32):
                    ps, pb = self.bank(0, 4)
                    for kc in range(16):
                        P.op("pe", f_mm(nc, ps[:, 0:64], hT[:, kc, ts * 128:(ts + 1) * 128], wba[:, kc, :], kc == 0, kc == 15),
                             [hB[ts // 4], wbB], [pb])
                    P.op("act", (lambda ps=ps, ts=ts: nc.scalar.copy(out=bg[:, ts, :], in_=ps[:, 0:64])), [pb], [bgB])
                P.op("act", f_act(nc, bg[:, :, 0:32], bg[:, :, 0:32], AF.Sigmoid), [bgB], [bgB])
                P.op("dve", f_tt(nc.vector, tmp[:], bg[:, :, 32:64], bc_c(prm[:, 1, :], 32), ALU.add), [bgB, prB], [tB])
                P.op("act", f_act(nc, tmp[:], tmp[:], AF.Exp), [tB], [tB])
                P.op("act", f_act(nc, tmp[:], tmp[:], AF.Ln, bias=1.0), [tB], [tB])
                P.op("act", f_act(nc, prm[:, 0, :], prm[:, 0, :], AF.Exp), [prB], [prB])
                P.op("dve", f_stt(nc.vector, bg[:, :, 32:64], tmp[:], -1.0, bc_c(prm[:, 0, :], 32), ALU.mult, ALU.mult),
                     [tB, prB], [bgB])
                P.flush()
            with ExitStack() as es2:
                wzp = sb_rot(nc, es2, "wz", 2, [128, 16, 512], BF16)
                zo = sb_rot(nc, es2, "zo", 4, [128, 512], BF16)
                for zb in range(8):
                    wt, wb = wzp.get()
                    P.dma("pool", wt[:], self.wz[l, zb].rearrange("p (c n) -> p c n", c=16), writes=[wb])
                    for ts in range(32):
                        ps, pb = self.bank(0, 4)
                        for kc in range(16):
                            P.op("pe", f_mm(nc, ps[:], hT[:, kc, ts * 128:(ts + 1) * 128], wt[:, kc, :], kc == 0, kc == 15),
                                 [hB[ts // 4], wb], [pb])
                        z, zB = zo.get()
                        P.op("act", f_act(nc, z[:], ps[:], AF.Silu), [pb], [zB])
                        P.dma("sp", self.ztok_s[ts * 128:(ts + 1) * 128, zb * 512:(zb + 1) * 512], z[:], reads=[zB])
                P.flush()
            with ExitStack() as es2:
                cw = sb(nc, es2, "cw", [128, 64, 4], F32)
                cwB = Buf("cw")
                P.dma("sp", cw[:], self.convw[l].rearrange("p (c j) -> p c j", j=4), writes=[cwB])
                wp = sb_rot(nc, es2, "wi", 3, [128, 16, 128], BF16)
                pbuf = sb_rot(nc, es2, "pbuf", 3, [128, 515], F32)
                accp = sb_rot(nc, es2, "acc", 3, [128, 512], F32)
                svp = sb_rot(nc, es2, "sv", 3, [128, 512], F32)
                sqp = sb_rot(nc, es2, "sq", 2, [128, 512], BF16)
                rsp = sb_rot(nc, es2, "rs", 2, [128, 512], F32)
                qnp = sb_rot(nc, es2, "qn", 3, [128, 512], BF16)
                svbp = sb_rot(nc, es2, "svb", 3, [128, 512], BF16)
                tkp = sb_rot(nc, es2, "tk", 3, [128, 4, 128], BF16)
                ctp = sb_rot(nc, es2, "ct", 2, [128, 512], F32)
                unit = 0
                for fc in range(64):
                    wt, wb = wp.get()
                    P.dma("pool", wt[:], self.win_fm[l, fc].rearrange("p (c n) -> p c n", c=16), writes=[wb])
                    prev = None
                    for tt in range(8):
                        ps, pb = self.bank(0, 4)
                        for kc in range(16):
                            P.op("pe", f_mm(nc, ps[:], wt[:, kc, :], hT[:, kc, tt * 512:(tt + 1) * 512], kc == 0, kc == 15),
                                 [wb, hB[tt]], [pb])
                        pbt, pbB = pbuf.get()
                        P.op("act", (lambda pbt=pbt, ps=ps: nc.scalar.copy(out=pbt[:, 3:515], in_=ps[:])), [pb], [pbB])
                        if prev is None:
                            P.op("pool", f_memset(nc.gpsimd, pbt[:, 0:3], 0.0), [], [pbB])
                        else:
                            P.op("pool", f_copy(nc.gpsimd, pbt[:, 0:3], prev[0][:, 512:515]), [prev[1]], [pbB])
                        prev = (pbt, pbB)
                        eng, en = (nc.vector, "dve")
                        unit += 1
                        acc, aB = accp.get()
                        P.op(en, f_ts(eng, acc[:], pbt[:, 0:512], cw[:, fc, 0:1], None, ALU.mult), [pbB, cwB], [aB])
                        for j in range(1, 4):
                            if en == "dve":
                                P.op(en, f_stt(eng, acc[:], pbt[:, j:j + 512], cw[:, fc, j:j + 1], acc[:], ALU.mult, ALU.add),
                                     [pbB, cwB, aB], [aB])
                            else:
                                ctmp, ctB = ctp.get()
                                P.op(en, f_ts(eng, ctmp[:], pbt[:, j:j + 512], cw[:, fc, j:j + 1], None, ALU.mult), [pbB, cwB], [ctB])
                                P.op(en, f_tt(eng, acc[:], acc[:], ctmp[:], ALU.add), [ctB, aB], [aB])
                        sv, sB = svp.get() if fc < 32 else svbp.get()
                        P.op("act", f_act(nc, sv[:], acc[:], AF.Silu), [aB], [sB])
                        if fc < 32:
                            sq, sqB = sqp.get()
                            isq = fc < 16
                            P.op("act", f_act(nc, sq[:], sv[:], AF.Square, scale=(float(np.sqrt(128.0)) if isq else 1.0)), [sB], [sqB])
                            ps2, pb2 = self.bank(4, 6)
                            P.op("pe", f_mm(nc, ps2[:], self.ones_b, sq[:], True, True), [sqB, self.cB], [pb2])
                            rs, rB = rsp.get()
                            self.rsqrt(rs[:], rB, ps2[:], pb2, (128.0 * EPS if isq else EPS))
                            qn, qB = qnp.get()
                            if fc < 16:
                                P.op("pool", f_tt(nc.gpsimd, qn[:], sv[:], rs[:], ALU.mult), [sB, rB], [qB])
                                P.dma("sp", self.qT_s[fc, :, tt * 512:(tt + 1) * 512], qn[:], reads=[qB])
                                src = None
                            else:
                                P.op("pool", f_tt(nc.gpsimd, qn[:], sv[:], rs[:], ALU.mult), [sB, rB], [qB])
                                P.dma("sp", self.kT_s[fc - 16, :, tt * 512:(tt + 1) * 512], qn[:], reads=[qB])
                                src, srcB = qn, qB
                                dstd = self.ktok_s[fc - 16]
                        else:
                            src, srcB = sv, sB
                            dstd = self.vtok_s[fc - 32]
                        if src is not None:
                            ps3, pb3 = self.bank(6, 8)
                            for j in range(4):
                                P.op("pe", f_mm(nc, ps3[:, j * 128:(j + 1) * 128], src[:, j * 128:(j + 1) * 128], self.ident_b, True, True),
                                     [srcB, self.cB], [pb3])
                            tk, tB = tkp.get()
                            P.op("act", (lambda tk=tk, ps3=ps3: nc.scalar.copy(out=tk[:].rearrange("p j d -> p (j d)"), in_=ps3[:])),
                                 [pb3], [tB])
                            P.dma("sp", dstd[tt * 512:(tt + 1) * 512, :].rearrange("(j p) d -> p j d", p=128), tk[:], reads=[tB])
                P.flush()
            if self.dbg:
                dbg_bg = self.dscr(_nm("dbg_bg"), [128, 32 * 64], F32)
                P.dma("sp", dbg_bg, bg[:].rearrange("p a b -> p (a b)"), reads=[bgB])
                dbg_mod = self.dscr(_nm("dbg_mod"), [128, 5 * 96], F32)
                P.dma("sp", dbg_mod, self.modT[:].rearrange("p a b -> p (a b)"), reads=[self.modB])
                P.flush()

    def st_gdn_chunk(self, l, bg, bgB, n_chunks=32):
        nc, P = self.nc, self.P
        V, G, A_ = nc.vector, nc.gpsimd, nc.scalar
        WAVE = 8
        with ExitStack() as es:
            kT = sb(nc, es, "ckT", [128, 16, 128], BF16)
            qT = sb(nc, es, "cqT", [128, 16, 128], BF16)
            ktok = sb(nc, es, "cktok", [128, 16, 128], BF16)
            vtok = sb(nc, es, "cvtok", [128, 32, 128], BF16)
            zt = sb(nc, es, "czt", [128, 32, 128], BF16)
            kTB, qTB, ktB, vtB, ztB = Buf("kT"), Buf("qT"), Buf("ktok"), Buf("vtok"), Buf("zt")
            Gm = sb(nc, es, "Gm", [128, 32, 128], F32)
            GmB = Buf("Gm")
            zg = sb(nc, es, "zg", [128, 32, 128], F32)
            zgB = Buf("zg")
            S = sb(nc, es, "S", [128, 32, 128], F32)
            Sb = sb(nc, es, "Sb", [128, 32, 128], BF16)
            SB = [Buf("S%d" % h) for h in range(32)]
            SbB = [Buf("Sb%d" % h) for h in range(32)]
            ogT = sb(nc, es, "ogT", [128, 32, 128], BF16)
            ogB = Buf("ogT")
            eAll = sb(nc, es, "eAll", [128, 96], F32)
            eB = Buf("eAll")
            onr = sb(nc, es, "onr", [128, 128], F32)
            onB = Buf("onr")
            P.dma("sp", onr[:], self.onorm[l], writes=[onB])
            P.op("dve", f_ts(V, onr[:], onr[:], float(np.sqrt(128.0)), None, ALU.mult), [onB], [onB])
            P.op("pool", f_memset(G, S[:], 0.0), [], SB)
            P.op("pool", f_memset(G, Sb[:], 0.0), [], SbB)
            tp = sb_rot(nc, es, "tp", 64, [128, 128], F32)
            tpb = sb_rot(nc, es, "tpb", 96, [128, 128], BF16)
            tpl = sb_rot(nc, es, "tpl", 48, [128, 128], BF16)
            ssp = sb_rot(nc, es, "ss", 16, [128, 2], F32)
            cB = self.cB
            I_, U_, SL_, SU_, BD, O1T, O2T = self.ident, self.U, self.SL, self.SU, self.BD32, self.OFF1T, self.OFF2T
            Ib = self.ident_b
            rr = [0]
            bank_of = {id(b): i for i, b in enumerate(self.psB)}

            def mmq(lhsT, rhs, rd, out=None, outB=None, start=True, stop=True):
                if out is None:
                    out, outB = self.psq.get()
                P.op("pe", f_mm(nc, out, lhsT, rhs, start, stop), rd, [outB])
                return out, outB

            def evb(src, srcB, eng=None):
                t, tB = tpb.get()
                if eng is None:
                    eng = "act" if bank_of[id(srcB)] % 2 else "dve"
                if eng == "act":
                    P.op("act", (lambda t=t, src=src: nc.scalar.copy(out=t[:], in_=src)), [srcB], [tB])
                else:
                    P.op("dve", f_copy(V, t[:], src), [srcB], [tB])
                return t, tB

            def shadow(src, srcB):
                t, tB = tpb.get()
                rr[0] += 1
                if rr[0] % 2:
                    P.op("pool", f_copy(G, t[:], src[:]), [srcB], [tB])
                else:
                    P.op("act", (lambda t=t, src=src: nc.scalar.copy(out=t[:], in_=src[:])), [srcB], [tB])
                return t, tB

            dbgT = self.dscr(_nm("dbg_ch"), [128, 12 * 128], F32) if self.dbg else None

            def head_gen(n, hv, kk, qkt):
                hk = hv // 2
                beta = bg[:, n, hv:hv + 1]
                eG = eAll[:, hv:hv + 1]
                eGl = eAll[:, 32 + hv:33 + hv]
                eGt = eAll[:, 64 + hv:65 + hv]
                dps, dB = mmq(Gm[:, hv, :], U_, [GmB, cB])
                E, EB = tp.get()
                P.op("act", f_act(nc, E[:], dps, AF.Exp), [dB], [EB])
                yield
                DTs, DsB = tp.get()
                P.op("pool", f_tt(G, DTs[:], E[:], SU_, ALU.mult), [EB, cB], [DsB])
                DTi, DiB = tp.get()
                P.op("pool", f_tt(G, DTi[:], E[:], U_, ALU.mult), [EB, cB], [DiB])
                A, AB = tpb.get()
                P.op("dve", f_stt(V, A[:], kk[0], beta, DTs[:], ALU.mult, ALU.mult), [kk[1], bgB, DsB], [AB])
                iT, iTB = tpl.get()
                P.op("dve", f_tt(V, iT[:], qkt[0], DTi[:], ALU.mult), [qkt[1], DiB], [iTB])
                yield
                aps, aB = mmq(A[:], Ib, [AB, cB])
                yield
                AT, ATB = evb(aps, aB, "act")
                yield
                B, BB = tpb.get()
                P.op("pool", f_tt(G, B[:], A[:], BD, ALU.mult), [AB, cB], [BB])
                BT, BTB = tpb.get()
                P.op("pool", f_tt(G, BT[:], AT[:], BD, ALU.mult), [ATB, cB], [BTB])
                Ao1T, o1B = tpl.get()
                P.op("pool", f_tt(G, Ao1T[:], AT[:], O1T, ALU.mult), [ATB, cB], [o1B])
                Ao2T, o2B = tpl.get()
                P.op("dve", f_tt(V, Ao2T[:], AT[:], O2T, ALU.mult), [ATB, cB], [o2B])
                yield
                Pm, PB = tp.get()
                P.op("dve", f_tt(V, Pm[:], I_, B[:], ALU.subtract), [BB, cB], [PB])
                Pb, PbB = shadow(Pm, PB)
                yield
                for step in range(4):
                    last = (step == 3)
                    if not last:
                        b2ps, b2B = mmq(BT[:], B[:], [BTB, BB])
                    b2tps, b2tB = mmq(B[:], BT[:], [BB, BTB])
                    yield
                    if not last:
                        B2, B2B = evb(b2ps, b2B)
                    B2T, B2TB = evb(b2tps, b2tB)
                    yield
                    pps, ppB = mmq(B2T[:], Pb[:], [B2TB, PbB])
                    yield
                    Pn, PnB = tp.get()
                    P.op("dve", f_tt(V, Pn[:], pps, Pm[:], ALU.add), [ppB, PB], [PnB])
                    Pm, PB = Pn, PnB
                    Pb, PbB = shadow(Pm, PB)
                    if not last:
                        B, BB, BT, BTB = B2, B2B, B2T, B2TB
                    yield
                Td, TdB, Tdb, TdbB = Pm, PB, Pb, PbB
                for AoT, aoB in ((Ao1T, o1B), (Ao2T, o2B)):
                    xps, xB = mmq(AoT[:], Tdb[:], [aoB, TdbB])
                    tps, tB = mmq(Tdb[:], Ib, [TdbB, cB])
                    yield
                    X, XB = evb(xps, xB)
                    TdT, TdTB = evb(tps, tB)
                    yield
                    yps, yB = mmq(TdT[:], X[:], [TdTB, XB])
                    yield
                    Tn, TnB = tp.get()
                    P.op("dve", f_tt(V, Tn[:], Td[:], yps, ALU.subtract), [yB, TdB], [TnB])
                    Td, TdB = Tn, TnB
                    Tdb, TdbB = shadow(Td, TdB)
                    yield
                TTb, TTbB = Tdb, TdbB
                keg, kegB = tpb.get()
                P.op("pool", f_ts(G, keg[:], ktok[:, hk, :], eG, None, ALU.mult), [ktB, eB], [kegB])
                kd, kdB = tpl.get()
                P.op("pool", f_ts(G, kd[:], ktok[:, hk, :], eGl, None, ALU.mult), [ktB, eB], [kdB])
                yield
                wps, wB = mmq(keg[:], TTb[:], [kegB, TTbB])
                yield
                nwT, nwB = tpb.get()
                P.op("act", (lambda nwT=nwT, wps=wps: nc.scalar.mul(out=nwT[:], in_=wps, mul=-1.0)), [wB], [nwB])
                yield
                vps, vB = mmq(TTb[:], vtok[:, hv, :], [TTbB, vtB], start=True, stop=False)
                mmq(nwT[:], Sb[:, hv, :], [nwB, SbB[hv]], out=vps, outB=vB, start=False, stop=True)
                o1ps, o1pB = mmq(qT[:, hk, :], Sb[:, hv, :], [qTB, SbB[hv]])
                yield
                vn, vnB = tpb.get()
                P.op("act", f_act(nc, vn[:], vps, AF.Copy, scale=beta), [vB, bgB], [vnB])
                o1s, o1sB = tp.get()
                P.op("act", f_act(nc, o1s[:], o1ps, AF.Copy, scale=eG), [o1pB, eB], [o1sB])
                yield
                o2ps, o2pB = mmq(iT[:], vn[:], [iTB, vnB])
                sups, suB = mmq(kd[:], vn[:], [kdB, vnB])
                yield
                o, oB = tp.get()
                P.op("dve", f_tt(V, o[:], o2ps, o1s[:], ALU.add), [o2pB, o1sB], [oB])
                P.op("dve", f_stt(V, S[:, hv, :], S[:, hv, :], eGt, sups, ALU.mult, ALU.add), [SB[hv], eB, suB], [SB[hv]])
                P.op("pool", f_copy(G, Sb[:, hv, :], S[:, hv, :]), [SB[hv]], [SbB[hv]])
                yield
                ss, ssB = ssp.get()
                junk, jB = tp.get()
                P.op("act", f_act(nc, junk[:], o[:], AF.Square, accum_out=ss[:, 0:1]), [oB], [jB, ssB])
                yield
                self.rsqrt(ss[:, 1:2], ssB, ss[:, 0:1], ssB, 128.0 * EPS)
                yield
                og, ogtB = tpb.get()
                P.op("dve", f_stt(V, og[:], o[:], ss[:, 1:2], zg[:, hv, :], ALU.mult, ALU.mult), [oB, ssB, zgB], [ogtB])
                yield
                gps, gB = mmq(og[:], Ib, [ogtB, cB])
                yield
                P.op("act", (lambda gps=gps, hv=hv: nc.scalar.copy(out=ogT[:, hv, :], in_=gps)), [gB], [ogB])

            for n in range(n_chunks):
                t0 = n * 128
                P.dma("sp", kT[:], self.kT_s[:, :, t0:t0 + 128].rearrange("h d t -> d h t"), writes=[kTB])
                P.dma("sp", qT[:], self.qT_s[:, :, t0:t0 + 128].rearrange("h d t -> d h t"), writes=[qTB])
                P.dma("sp", ktok[:], self.ktok_s[:, t0:t0 + 128, :].rearrange("h t d -> t h d"), writes=[ktB])
                P.dma("sp", vtok[:], self.vtok_s[:, t0:t0 + 128, :].rearrange("h t d -> t h d"), writes=[vtB])
                P.dma("sp", zt[:].rearrange("t h d -> t (h d)"), self.ztok_s[t0:t0 + 128, :], writes=[ztB])
                g_n = bg[:, n, 32:64]
                P.op("pool", f_tt(G, Gm[:], bc_t(g_n, 128), bc_c(SL_, 32), ALU.mult), [bgB, cB], [GmB])
                P.op("pool", f_tt(G, zg[:], zt[:], bc_c(onr[:, :], 32), ALU.mult), [ztB, onB], [zgB])
                eps_, epB = self.psq.get()
                P.op("pe", f_mm(nc, eps_[:, 0:32], U_, g_n, True, True), [bgB, cB], [epB])
                P.op("pe", f_mm(nc, eps_[:, 32:64], SL_, g_n, True, True), [bgB, cB], [epB])
                P.op("pe", f_mm(nc, eps_[:, 64:96], self.ones_f, g_n, True, True), [bgB, cB], [epB])
                P.op("act", f_act(nc, eAll[:], eps_[:, 0:96], AF.Exp), [epB], [eB])
                for w0 in range(0, 32, WAVE):
                    gens = []
                    for hk in range(w0 // 2, (w0 + WAVE) // 2):
                        kk = mmq(kT[:, hk, :], kT[:, hk, :], [kTB])
                        qkt = mmq(kT[:, hk, :], qT[:, hk, :], [kTB, qTB])
                        gens.append(head_gen(n, 2 * hk, kk, qkt))
                        gens.append(head_gen(n, 2 * hk + 1, kk, qkt))
                    alive = list(gens)
                    while alive:
                        nxt = []
                        for g in alive:
                            try:
                                next(g)
                                nxt.append(g)
                            except StopIteration:
                                pass
                        alive = nxt
                P.dma("sp", self.ogT_s[:, t0:t0 + 128].rearrange("(h p) t -> p h t", p=128), ogT[:], reads=[ogB])
            P.flush()

    def rot_tables(self, es):
        nc, P = self.nc, self.P
        V = nc.vector
        C32 = sb(nc, es, "C32", [32, T], F32)
        S32 = sb(nc, es, "S32", [32, T], F32)
        rB = Buf("rot")
        with ExitStack() as es2:
            posi = sb(nc, es2, "posi", [32, T], I32)
            ang = sb(nc, es2, "ang", [32, T], F32)
            tmp = sb(nc, es2, "rtmp", [32, T], F32)
            ang2 = sb(nc, es2, "ang2", [32, T], F32)
            ivf = sb(nc, es2, "ivf", [32, 1], F32)
            pB = Buf("posi")
            src = bass.AP(self.pos.tensor, 0, [[0, 32], [1, T]])
            P.dma("sp", posi[:], src, writes=[pB])
            P.dma("sp", ivf[:], self.c_invf[0:32, :], writes=[pB])
            P.op("dve", f_copy(V, ang[:], posi[:]), [pB], [pB])
            P.op("dve", f_ts(V, ang[:], ang[:], ivf[:, 0:1], None, ALU.mult), [pB], [pB])
            pi = float(np.pi)
            for (dst, off) in ((S32, 0.0), (C32, 0.5 * pi)):
                P.op("dve", f_ts(V, ang2[:], ang[:], off, None, ALU.add), [pB], [pB])
                P.op("dve", f_ts(V, tmp[:], ang2[:], 1.0 / (2.0 * pi), None, ALU.mult), [pB], [pB])
                P.op("dve", f_copy(V, posi[:], tmp[:]), [pB], [pB])
                P.op("dve", f_copy(V, tmp[:], posi[:]), [pB], [pB])
                P.op("dve", f_stt(V, ang2[:], tmp[:], -2.0 * pi, ang2[:], ALU.mult, ALU.add), [pB], [pB])
                P.op("dve", f_ts(V, tmp[:], ang2[:], pi, 2.0 * pi, ALU.is_gt, ALU.mult), [pB], [pB])
                P.op("dve", f_tt(V, ang2[:], ang2[:], tmp[:], ALU.subtract), [pB], [pB])
                P.op("dve", f_ts(V, tmp[:], ang2[:], -pi, 2.0 * pi, ALU.is_lt, ALU.mult), [pB], [pB])
                P.op("dve", f_tt(V, ang2[:], ang2[:], tmp[:], ALU.add), [pB], [pB])
                P.op("act", f_act(nc, dst[:], ang2[:], AF.Sin), [pB], [rB, pB])
            P.flush()
        return C32, S32, rB

    def qk_pools(self, es):
        nc = self.nc
        return dict(sv=sb_rot(nc, es, "qsv", 2, [128, 512], F32), sq=sb_rot(nc, es, "qsq", 1, [128, 512], BF16),
                    rs=sb_rot(nc, es, "qrs", 1, [128, 512], F32), qb=sb_rot(nc, es, "qqb", 2, [128, 512], BF16),
                    t1=sb_rot(nc, es, "qt1", 1, [32, 512], F32), t2=sb_rot(nc, es, "qt2", 1, [32, 512], F32),
                    o=sb_rot(nc, es, "qo", 2, [128, 512], BF16))

    def qk_post(self, pl, ps, pb, gcol, gB, rot, t0, dst):
        nc, P = self.nc, self.P
        V, G = nc.vector, nc.gpsimd
        C32, S32, rB = rot
        sv, svB = pl["sv"].get()
        P.op("act", (lambda sv=sv, ps=ps: nc.scalar.copy(out=sv[:], in_=ps[:])), [pb], [svB])
        sq, sqB = pl["sq"].get()
        P.op("act", f_act(nc, sq[:], ps[:], AF.Square), [pb], [sqB])
        ps2, pb2 = self.bank(4, 6)
        P.op("pe", f_mm(nc, ps2[:], self.ones_b, sq[:], True, True), [sqB, self.cB], [pb2])
        rs, rsB = pl["rs"].get()
        self.rsqrt(rs[:], rsB, ps2[:], pb2, 128.0 * EPS)
        qb, qbB = pl["qb"].get()
        P.op("dve", f_stt(V, qb[:], sv[:], gcol, rs[:], ALU.mult, ALU.mult), [svB, rsB, gB], [qbB])
        ps3, pb3 = self.bank(6, 8)
        P.op("pe", f_mm(nc, ps3[0:32, :], self.rotm[:, 0:32], qb[:], True, True), [qbB, self.cB], [pb3])
        t1, t1B = pl["t1"].get()
        P.op("pool", f_tt(G, t1[:], qb[0:32, :], C32[:, t0:t0 + 512], ALU.mult), [qbB, rB], [t1B])
        t2, t2B = pl["t2"].get()
        P.op("dve", f_tt(V, t2[:], ps3[0:32, :], S32[:, t0:t0 + 512], ALU.mult), [pb3, rB], [t2B])
        o, oB = pl["o"].get()
        P.op("pool", f_copy(G, o[:], qb[:]), [qbB], [oB])
        P.op("pool", f_tt(G, o[0:32, :], t1[:], t2[:], ALU.add), [t1B, t2B, oB], [oB])
        P.dma("sp", dst, o[:], reads=[oB])

    def st_kv(self, x_in):
        nc, P = self.nc, self.P
        with ExitStack() as es:
            rot = self.rot_tables(es)
            hT = sb(nc, es, "hT", [128, 16, T], BF16)
            hB = [Buf("hT%d" % i) for i in range(8)]
            with ExitStack() as es2:
                pools = self.norm_pools(es2, 1, 256)
                for tt in range(16):
                    self.norm_tile(pools, x_in, tt * 256, hT[:, :, tt * 256:(tt + 1) * 256], hB[tt // 2], 8, 256)
                P.flush()
            with ExitStack() as es2:
                wvp = sb_rot(nc, es2, "wv", 1, [128, 16, 512], BF16)
                vo = sb_rot(nc, es2, "vo", 4, [128, 512], BF16)
                for gi, dil in enumerate((1, 4, 16)):
                    wt, wb = wvp.get()
                    P.dma("pool", wt[:], self.wkvv[gi].rearrange("p (c n) -> p c n", c=16), writes=[wb])
                    nbr = 32 // dil
                    for blk in range(32):
                        r, nb = blk // nbr, blk % nbr
                        st_ = r + dil * 128 * nb
                        ps, pb = self.bank(0, 4)
                        for kc in range(16):
                            P.op("pe", f_mm(nc, ps[:], hT[:, kc, st_:st_ + dil * 127 + 1:dil], wt[:, kc, :], kc == 0, kc == 15),
                                 hB + [wb], [pb])
                        v, vB = vo.get()
                        P.op("act", (lambda v=v, ps=ps: nc.scalar.copy(out=v[:], in_=ps[:])), [pb], [vB])
                        P.dma("sp", self.Vb_s[gi, blk], v[:], reads=[vB])
                P.flush()
            with ExitStack() as es2:
                pl = self.qk_pools(es2)
                gk = sb(nc, es2, "gk", [128, 3], F32)
                gB = Buf("gk")
                P.dma("sp", gk[:], self.knorm[:, :], writes=[gB])
                P.op("dve", f_ts(nc.vector, gk[:], gk[:], float(np.sqrt(128.0)), None, ALU.mult), [gB], [gB])
                wp = sb_rot(nc, es2, "wk", 3, [128, 16, 128], BF16)
                for c in range(12):
                    wt, wb = wp.get()
                    P.dma("pool", wt[:], self.wkvk[c].rearrange("p (c n) -> p c n", c=16), writes=[wb])
                    for tt in range(8):
                        ps, pb = self.bank(0, 4)
                        for kc in range(16):
                            P.op("pe", f_mm(nc, ps[:], wt[:, kc, :], hT[:, kc, tt * 512:(tt + 1) * 512], kc == 0, kc == 15),
                                 [wb, hB[tt]], [pb])
                        self.qk_post(pl, ps, pb, gk[:, c // 4:c // 4 + 1], gB, rot, tt * 512,
                                     self.KT_s[c, :, tt * 512:(tt + 1) * 512])
                P.flush()

    def st_q(self, l, x_in):
        nc, P = self.nc, self.P
        j = l - 2
        with ExitStack() as es:
            rot = self.rot_tables(es)
            hT = sb(nc, es, "hT", [128, 16, T], BF16)
            hB = [Buf("hT%d" % i) for i in range(8)]
            with ExitStack() as es2:
                pools = self.norm_pools(es2, 1, 256)
                for tt in range(16):
                    self.norm_tile(pools, x_in, tt * 256, hT[:, :, tt * 256:(tt + 1) * 256], hB[tt // 2], l * 2, 256)
                P.flush()
            with ExitStack() as es2:
                pl = self.qk_pools(es2)
                gq = sb(nc, es2, "gq", [128, 3], F32)
                gB = Buf("gq")
                P.dma("sp", gq[:], self.qnorm[j], writes=[gB])
                P.op("dve", f_ts(nc.vector, gq[:], gq[:], float(np.sqrt(128.0)), None, ALU.mult), [gB], [gB])
                wp = sb_rot(nc, es2, "wq", 3, [128, 16, 128], BF16)
                for c in range(48):
                    wt, wb = wp.get()
                    P.dma("pool", wt[:], self.wq[j, c].rearrange("p (c n) -> p c n", c=16), writes=[wb])
                    for tt in range(8):
                        ps, pb = self.bank(0, 4)
                        for kc in range(16):
                            P.op("pe", f_mm(nc, ps[:], wt[:, kc, :], hT[:, kc, tt * 512:(tt + 1) * 512], kc == 0, kc == 15),
                                 [wb, hB[tt]], [pb])
                        self.qk_post(pl, ps, pb, gq[:, c // 16:c // 16 + 1], gB, rot, tt * 512,
                                     self.QT_s[c, :, tt * 512:(tt + 1) * 512])
                P.flush()

    def st_attn(self):
        nc, P = self.nc, self.P
        V, G = nc.vector, nc.gpsimd
        HALF = 2048
        sc_scale = float(128.0 ** -0.5)
        with ExitStack() as es:
            accO = sb(nc, es, "accO", [128, 4, HALF], F32)
            accD = sb(nc, es, "accD", [128, 4, HALF], F32)
            aB, dB_ = Buf("accO"), Buf("accD")
            Qp = sb_rot(nc, es, "aQ", 2, [128, 4, HALF], BF16)
            Kp = sb_rot(nc, es, "aK", 2, [128, T], BF16)
            Vp = sb_rot(nc, es, "aV", 2, [128, 32, 128], BF16)
            PTp = sb_rot(nc, es, "aPT", 3, [128, 2, 4, 128], BF16)
            aop = sb_rot(nc, es, "aao", 1, [128, 4, HALF], BF16)
            cB = self.cB
            mask = self.maskPC.rearrange("p (b q) -> p b q", b=2)
            unit = 0
            for kvh in range(4):
                for a in range(2):
                    for gi, dil in enumerate((1, 4, 16)):
                        Q, QB = Qp.get()
                        for hq in range(4):
                            P.dma("sp", Q[:, hq, :], self.QT_s[gi * 16 + kvh * 4 + hq, :, a * HALF:(a + 1) * HALF], writes=[QB])
                        Kt, KB = Kp.get()
                        P.dma("sp", Kt[:], self.KT_s[gi * 4 + kvh], writes=[KB])
                        Vt, VB = Vp.get()
                        P.dma("sp", Vt[:], self.Vb_s[gi, :, :, kvh * 128:(kvh + 1) * 128].rearrange("b p d -> p b d"), writes=[VB])
                        nbr = 32 // dil
                        nbh = nbr // 2
                        for r in range(dil):
                            for nbi in range(nbh):
                                nb = a * nbh + nbi
                                blk = r * nbr + nb
                                ql = r + dil * 128 * nbi
                                qsl = slice(ql, ql + dil * 127 + 1, dil)
                                kc0 = r + dil * 128 * nb
                                ksl = slice(kc0, kc0 + dil * 127 + 1, dil)
                                has_prev = nb > 0
                                di = unit % 2
                                unit += 1
                                sc, scB = self.psd[di], self.psB[di * 2]
                                qr = Q[:, :, qsl]
                                if has_prev:
                                    kp0 = kc0 - dil * 128
                                    P.op("pe", f_mm(nc, sc[:, 0:512], Kt[:, kp0:kp0 + dil * 127 + 1:dil], qr, True, True), [KB, QB], [scB])
                                P.op("pe", f_mm(nc, sc[:, 512:1024], Kt[:, ksl], qr, True, True), [KB, QB], [scB])
                                PT, PB = PTp.get()
                                lo = 0 if has_prev else 1
                                P.op("act", f_act(nc, PT[:, lo:2].rearrange("p b h q -> p (b h q)"), sc[:, lo * 512:1024], AF.Exp, scale=sc_scale),
                                     [scB], [PB])
                                eng, en = (V, "dve") if unit % 2 == 0 else (G, "pool")
                                mk = mask[:, lo:2, :].unsqueeze(2).to_broadcast([128, 2 - lo, 4, 128])
                                P.op(en, f_tt(eng, PT[:, lo:2], PT[:, lo:2], mk, ALU.mult), [PB, cB], [PB])
                                ops_, opB = self.psb[4 + di * 2], self.psB[4 + di * 2]
                                dps_, dpB = self.psb[5 + di * 2], self.psB[5 + di * 2]
                                cur = PT[:, 1].rearrange("p h q -> p (h q)")
                                if has_prev:
                                    prv = PT[:, 0].rearrange("p h q -> p (h q)")
                                    P.op("pe", f_mm(nc, ops_[:], Vt[:, blk - 1, :], prv, True, False), [VB, PB], [opB])
                                    P.op("pe", f_mm(nc, ops_[:], Vt[:, blk, :], cur, False, True), [VB, PB], [opB])
                                    P.op("pe", f_mm(nc, dps_[:], self.ones_b, prv, True, False), [cB, PB], [dpB])
                                    P.op("pe", f_mm(nc, dps_[:], self.ones_b, cur, False, True), [cB, PB], [dpB])
                                else:
                                    P.op("pe", f_mm(nc, ops_[:], Vt[:, blk, :], cur, True, True), [VB, PB], [opB])
                                    P.op("pe", f_mm(nc, dps_[:], self.ones_b, cur, True, True), [cB, PB], [dpB])
                                ov = accO[:, :, qsl]
                                dv = accD[:, :, qsl]
                                o3 = ops_.rearrange("p (h q) -> p h q", h=4)
                                d3 = dps_.rearrange("p (h q) -> p h q", h=4)
                                if gi == 0:
                                    P.op("act", (lambda ov=ov, o3=o3: nc.scalar.copy(out=ov, in_=o3)), [opB], [aB])
                                    P.op("dve", f_copy(V, dv, d3), [dpB], [dB_])
                                else:
                                    P.op("dve", f_tt(V, ov, o3, ov, ALU.add), [opB, aB], [aB])
                                    P.op("dve", f_tt(V, dv, d3, dv, ALU.add), [dpB, dB_], [dB_])
                    P.op("dve", (lambda: nc.vector.reciprocal(out=accD[:], in_=accD[:])), [dB_], [dB_])
                    ao, aoB = aop.get()
                    P.op("pool", f_tt(G, ao[:], accO[:], accD[:], ALU.mult), [aB, dB_], [aoB])
                    for hq in range(4):
                        h = kvh * 4 + hq
                        P.dma("sp", self.aoT_s[h * 128:(h + 1) * 128, a * HALF:(a + 1) * HALF], ao[:, hq, :], reads=[aoB])
            P.flush()


def build(dbg=False, stages=None):
    k = K(dbg)
    k.declare()
    nc, P = k.nc, k.P
    allst = stages is None

    def want(s):
        return allst or s in stages

    with ExitStack() as es0:
        k.load_consts(es0)
        k.st_mods()
        for l in range(2):
            x_in = k.xT if l == 0 else k.yT
            if want("gdn%d" % l):
                with ExitStack() as esl:
                    bg = sb(nc, esl, "bg", [128, 32, 64], F32)
                    bgB = Buf("bg")
                    if want("gproj%d" % l):
                        k.st_gdn_proj(l, x_in, bg, bgB)
                    if want("gchunk%d" % l):
                        k.st_gdn_chunk(l, bg, bgB, n_chunks=(k.n_chunks if hasattr(k, "n_chunks") else 32))
            if want("gout%d" % l):
                k.st_proj_resid(k.ogT_s, 32, k.wout[l], k.gate(l, 0), x_in, k.yT)
            if want("mlp%d" % l):
                k.st_mlp(l, k.yT, k.yT)
        if (not allst) and "copyin" in stages:
            for c in range(16):
                P.dma("sp", k.yT[c * 128:(c + 1) * 128, :], k.xT[c * 128:(c + 1) * 128, :])
            P.flush()
        if want("kv"):
            k.st_kv(k.yT)
        for l in range(2, 4):
            if want("q%d" % l):
                k.st_q(l, k.yT)
            if want("attn%d" % l):
                k.st_attn()
            if want("aout%d" % l):
                k.st_proj_resid(k.aoT_s, 16, k.wo[l - 2], k.gate(l, 0), k.yT, k.yT)
            if want("mlp%d" % l):
                k.st_mlp(l, k.yT, k.yT)
        P.flush()
        P.barrier()
    return k


def tile_w(W, ncols):
    Kd, N = W.shape
    return np.ascontiguousarray(
        W.reshape(Kd // 128, 128, N // ncols, ncols).transpose(2, 1, 0, 3)).reshape(N // ncols, 128, (Kd // 128) * ncols)


def make_consts():
    import ml_dtypes
    i = np.arange(128)
    k_, m_ = i[:, None], i[None, :]
    ident = np.eye(128, dtype=np.float32)
    U = (k_ <= m_).astype(np.float32)
    SL = (k_ > m_).astype(np.float32)
    SU = (k_ < m_).astype(np.float32)
    BD32 = ((k_ // 32) == (m_ // 32)).astype(np.float32)
    rb, cb = k_ // 32, m_ // 32
    OFF1T = (((rb == 1) & (cb == 0)) | ((rb == 3) & (cb == 2))).astype(np.float32)
    OFF2T = ((rb >= 2) & (cb < 2)).astype(np.float32)
    ones = np.ones((128, 128), np.float32)
    z = np.zeros((128, 128), np.float32)
    c_f32 = np.concatenate([ident, U, SL, SU, BD32, OFF1T, OFF2T, ones, z, z], axis=1)
    rotm = np.zeros((128, 128), np.float32)
    for m in range(16):
        rotm[m + 16, m] = -1.0
        rotm[m, m + 16] = 1.0
    maskP = (k_ >= m_).astype(np.float32)
    maskC = (k_ <= m_).astype(np.float32)
    c_bf = np.concatenate([ident, ones, rotm, maskP, maskC], axis=1).astype(ml_dtypes.bfloat16)
    inv = (500000.0 ** (-np.arange(0, 32, 2, dtype=np.float32) / 32.0)).astype(np.float32)
    invf = np.zeros((128, 1), np.float32)
    invf[0:32, 0] = np.concatenate([inv, inv])
    return dict(c_f32=np.ascontiguousarray(c_f32), c_bf=np.ascontiguousarray(c_bf), c_invf=invf)


def prep_shared(inp):
    f = lambda a: np.ascontiguousarray(np.asarray(a, dtype=np.float32))
    sh = {}
    ada_w = f(inp["ada_w"])
    sh["ada_w_t"] = np.stack([tile_w(ada_w[l], 512) for l in range(4)])
    sh["ada_b_t"] = np.stack([f(inp["ada_b"][l]).reshape(96, 128).T for l in range(4)]).copy()
    sh["kvada_w_t"] = tile_w(f(inp["kv_ada_w"]), 512)
    sh["kvada_b_t"] = f(inp["kv_ada_b"]).reshape(32, 128).T.copy()
    ng = f(inp["norm_g"])
    sh["normg_t"] = np.ascontiguousarray(ng.reshape(4, 2, 16, 128).transpose(0, 1, 3, 2))
    sh["kvnormg_t"] = f(inp["kv_norm_g"]).reshape(16, 128).T.copy()
    w1, w2 = f(inp["mlp_w1"]), f(inp["mlp_w2"])
    sh["mlp_w1_t"] = np.stack([tile_w(w1[l], 128) for l in range(4)])
    sh["mlp_w2_t"] = np.stack([np.stack([tile_w(w2[l][h * 4096:(h + 1) * 4096], 128) for h in range(2)], axis=1)
                               for l in range(4)])
    win = f(inp["gdn_w_in"])
    sh["gdn_win_fm"] = np.stack([tile_w(win[l][:, 0:8192], 128) for l in range(2)])
    sh["gdn_wz_t"] = np.stack([tile_w(win[l][:, 8192:12288], 512) for l in range(2)])
    sh["gdn_wba_t"] = np.stack([tile_w(win[l][:, 12288:12352], 64)[0] for l in range(2)])
    cw = f(inp["gdn_conv_w"])
    sh["gdn_conv_t"] = np.ascontiguousarray(cw.reshape(2, 4, 64, 128).transpose(0, 3, 2, 1)).reshape(2, 128, 256)
    sh["gdn_alog_b"] = np.ascontiguousarray(np.broadcast_to(f(inp["gdn_a_log"])[:, None, :], (2, 128, 32)))
    sh["gdn_dtb_b"] = np.ascontiguousarray(np.broadcast_to(f(inp["gdn_dt_bias"])[:, None, :], (2, 128, 32)))
    sh["gdn_onorm_b"] = np.ascontiguousarray(np.broadcast_to(f(inp["gdn_onorm_g"])[:, None, :], (2, 128, 128)))
    wout = f(inp["gdn_w_out"])
    sh["gdn_wout_t"] = np.stack([tile_w(wout[l], 128) for l in range(2)])
    wkv = f(inp["w_kv"])
    sh["wkv_k_t"] = np.stack([tile_w(wkv[:, gi * 1024 + h * 128: gi * 1024 + (h + 1) * 128], 128)[0]
                              for gi in range(3) for h in range(4)])
    sh["wkv_v_t"] = np.stack([tile_w(wkv[:, gi * 1024 + 512: gi * 1024 + 1024], 512)[0] for gi in range(3)])
    sh["knorm_t"] = f(inp["k_norm_g"]).T.copy()
    sh["qnorm_t"] = np.ascontiguousarray(f(inp["q_norm_g"]).transpose(0, 2, 1))
    wq, wo = f(inp["attn_w_q"]), f(inp["attn_w_o"])
    sh["attn_wq_t"] = np.stack([tile_w(wq[j], 128) for j in range(2)])
    sh["attn_wo_t"] = np.stack([tile_w(wo[j], 128) for j in range(2)])
    sh.update(make_consts())
    return sh


def prep_core(inp, b):
    d = {}
    d["xT"] = np.ascontiguousarray(np.asarray(inp["x"][b], dtype=np.float32).T)
    d["ccol"] = np.ascontiguousarray(np.asarray(inp["c"][b], dtype=np.float32).reshape(16, 128).T)
    d["pos"] = np.ascontiguousarray(np.asarray(inp["positions"][b], dtype=np.int32)[None, :])
    return d


def kernel(**inputs):
    k = build()
    shared = prep_shared(inputs)
    in_maps = []
    for b in range(NCORES):
        m = dict(shared)
        m.update(prep_core(inputs, b))
        in_maps.append(m)
    res = run_bass_kernel_spmd(k.nc, in_maps, core_ids=list(range(NCORES)))
    out = np.stack([np.ascontiguousarray(np.asarray(r["yT"]).T) for r in res.results])
    return out.astype(np.float32)
```

```python
import numpy as np
from contextlib import ExitStack
import concourse.bass as bass
import concourse.mybir as mybir
from concourse.bass_utils import run_bass_kernel_spmd

F32 = mybir.dt.float32
BF16 = mybir.dt.bfloat16
I32 = mybir.dt.int32
AF = mybir.ActivationFunctionType
ALU = mybir.AluOpType
AX = mybir.AxisListType

T = 4096
D = 2048
KC = 16
NCORES = 8
EPS = 1e-6
SQD = float(np.sqrt(2048.0))


class Buf:
    __slots__ = ("name", "lw", "rd", "excl")

    def __init__(self, name="", excl=False):
        self.name = name
        self.lw = None
        self.rd = {}
        self.excl = excl


class Op:
    __slots__ = ("eng", "fn", "deps", "is_dma", "signal", "ticket", "dsem", "dval", "emitted")

    def __init__(self, eng, fn, is_dma):
        self.eng = eng
        self.fn = fn
        self.deps = []
        self.is_dma = is_dma
        self.signal = False
        self.ticket = None
        self.dsem = None
        self.dval = None
        self.emitted = False


class Prog:
    COMPUTE = ("pe", "act", "dve", "pool")
    ALLENG = ("pe", "act", "dve", "pool", "sp")

    def __init__(self, nc, n_dma_sems=32):
        self.nc = nc
        self.ops = []
        self.start = 0
        self.n_dma_sems = n_dma_sems
        self.eng_obj = {"pe": nc.tensor, "act": nc.scalar, "dve": nc.vector,
                        "pool": nc.gpsimd, "sp": nc.sync}
        self.sems = {e: nc.alloc_semaphore("s_" + e) for e in self.COMPUTE}
        self.cnt = {e: 0 for e in self.COMPUTE}
        self.dma_sems = [nc.alloc_semaphore("s_dma%d" % i) for i in range(n_dma_sems)]
        self.dma_val = [0] * n_dma_sems
        self.dma_rr = 0
        self.waited = {e: {} for e in self.ALLENG}
        self.n_inst = 0

    def op(self, eng, fn, reads=(), writes=(), dma=False):
        o = Op(eng, fn, dma)
        idx = len(self.ops)
        deps = {}
        ex = [b for b in reads if b.excl]
        if ex:
            writes = list(writes) + [b for b in ex if b not in writes]
            reads = [b for b in reads if not b.excl]
            for b in ex:
                if b.lw is not None:
                    deps[b.lw] = True
        for b in reads:
            if b.lw is not None:
                deps[b.lw] = True
        for b in writes:
            if b.lw is not None:
                deps.setdefault(b.lw, False)
            for r in b.rd.values():
                deps.setdefault(r, False)
        key = ("dma", idx) if dma else eng
        for b in reads:
            b.rd[key] = idx
        for b in writes:
            b.lw = idx
            b.rd = {}
        deps.pop(idx, None)
        o.deps = sorted(deps.items())
        self.ops.append(o)
        return idx

    def dma(self, q, out, in_, reads=(), writes=()):
        qe = self.eng_obj[q]
        return self.op(q, lambda: qe.dma_start(out=out, in_=in_), reads, writes, dma=True)

    def _wait(self, eng, key, sem, val):
        w = self.waited[eng]
        if w.get(key, 0) >= val:
            return
        self.eng_obj[eng].wait_ge(sem, val)
        self.n_inst += 1
        w[key] = val

    def flush(self, barrier=True):
        ops = self.ops
        new = ops[self.start:]
        for o in new:
            for d, raw in o.deps:
                p = ops[d]
                if p.is_dma or p.emitted:
                    continue
                if p.eng == o.eng and not raw and not o.is_dma:
                    continue
                p.signal = True
        last = {}
        for o in new:
            if not o.is_dma:
                last[o.eng] = o
        for o in last.values():
            o.signal = True
        pending_cover = {e: [] for e in self.COMPUTE}
        for o in new:
            e = o.eng
            for d, raw in o.deps:
                p = ops[d]
                if p.is_dma:
                    self._wait(e, ("d", p.dsem), self.dma_sems[p.dsem], p.dval)
                else:
                    if p.eng == e and not raw and not o.is_dma:
                        continue
                    assert p.ticket is not None, (p.eng, e)
                    self._wait(e, p.eng, self.sems[p.eng], p.ticket)
            if o.is_dma:
                i = self.dma_rr
                self.dma_rr = (self.dma_rr + 1) % self.n_dma_sems
                if self.dma_val[i] > 0:
                    self._wait(e, ("d", i), self.dma_sems[i], self.dma_val[i])
                ins = o.fn()
                self.dma_val[i] += 16
                ins.then_inc(self.dma_sems[i], 16)
                o.dsem = i
                o.dval = self.dma_val[i]
            else:
                ins = o.fn()
                if o.signal:
                    self.cnt[e] += 1
                    o.ticket = self.cnt[e]
                    ins.then_inc(self.sems[e], 1)
                    for q in pending_cover[e]:
                        q.ticket = o.ticket
                    pending_cover[e] = []
                else:
                    pending_cover[e].append(o)
            self.n_inst += 1
            o.emitted = True
            o.fn = None
        self.start = len(ops)
        if barrier:
            self.barrier()

    def barrier(self):
        for e in self.ALLENG:
            for f in self.COMPUTE:
                if f != e and self.cnt[f] > 0:
                    self._wait(e, f, self.sems[f], self.cnt[f])
            for i in range(self.n_dma_sems):
                if self.dma_val[i] > 0:
                    self._wait(e, ("d", i), self.dma_sems[i], self.dma_val[i])


class Rot:
    def __init__(self, tiles):
        self.tiles = tiles
        self.k = 0

    def get(self):
        t = self.tiles[self.k % len(self.tiles)]
        self.k += 1
        return t


_uid = [0]


def _nm(name):
    _uid[0] += 1
    return "%s_u%d" % (name, _uid[0])


def sb_rot(nc, es, name, n, shape, dtype):
    tiles = []
    for i in range(n):
        t = es.enter_context(nc.sbuf_tensor(_nm(name), shape, dtype))
        tiles.append((t, Buf("%s%d" % (name, i))))
    return Rot(tiles)


def sb(nc, es, name, shape, dtype):
    return es.enter_context(nc.sbuf_tensor(_nm(name), shape, dtype))


def f_mm(nc, out, lhsT, rhs, start, stop):
    return lambda: nc.tensor.matmul(out, lhsT, rhs, start=start, stop=stop)


def f_tt(eng, out, in0, in1, op):
    return lambda: eng.tensor_tensor(out=out, in0=in0, in1=in1, op=op)


def f_ts(eng, out, in0, s1, s2, op0, op1=None):
    if op1 is None:
        return lambda: eng.tensor_scalar(out=out, in0=in0, scalar1=s1, scalar2=None, op0=op0)
    return lambda: eng.tensor_scalar(out=out, in0=in0, scalar1=s1, scalar2=s2, op0=op0, op1=op1)


def f_stt(eng, out, in0, scalar, in1, op0, op1):
    return lambda: eng.scalar_tensor_tensor(out=out, in0=in0, scalar=scalar, in1=in1, op0=op0, op1=op1)


def f_act(nc, out, in_, func, bias=None, scale=None, accum_out=None):
    kw = {}
    if bias is not None:
        kw["bias"] = bias
    if scale is not None:
        kw["scale"] = scale
    if accum_out is not None:
        kw["accum_out"] = accum_out
    return lambda: nc.scalar.activation(out=out, in_=in_, func=func, **kw)


def f_copy(eng, out, in_):
    return lambda: eng.tensor_copy(out=out, in_=in_)


def f_memset(eng, ap, v):
    return lambda: eng.memset(ap, v)


def bc_t(col2d, n):
    return col2d.unsqueeze(2).to_broadcast([col2d.shape[0], col2d.shape[1], n])


def bc_c(row2d, c):
    return row2d.unsqueeze(1).to_broadcast([row2d.shape[0], c, row2d.shape[1]])


class K:
    def __init__(self, dbg=False):
        self.dbg = dbg
        nc = self.nc = bass.Bass("TRN2", target_bir_lowering=False)
        self.P = Prog(nc)
        self.inputs = {}
        self.psd = [nc.alloc_psum_tensor("psd%d" % i, [128, 1024], F32) for i in range(4)]
        self.psb = [self.psd[i // 2][:, (i % 2) * 512:(i % 2 + 1) * 512] for i in range(8)]
        self.psB = [Buf("psB%d" % i, excl=True) for i in range(8)]
        self.psq = Rot([(self.psb[i // 4][:, (i % 4) * 128:(i % 4 + 1) * 128], self.psB[i // 4])
                        for i in range(32)])
        self.bank_rr = 0

    def din(self, name, shape, dt=F32):
        t = self.nc.dram_tensor(name, list(shape), dt, kind="ExternalInput").ap()
        self.inputs[name] = (tuple(shape), dt)
        return t

    def dscr(self, name, shape, dt):
        isout = self.dbg is True or (isinstance(self.dbg, (set, list, tuple)) and any(name.startswith(n) for n in self.dbg))
        kind = "ExternalOutput" if isout else "Internal"
        return self.nc.dram_tensor(name, list(shape), dt, kind=kind).ap()

    def bank(self, lo=0, hi=8):
        i = lo + (self.bank_rr % (hi - lo))
        self.bank_rr += 1
        return self.psb[i], self.psB[i]

    def declare(self):
        d = self.din
        self.xT = d("xT", [D, T])
        self.ccol = d("ccol", [128, 16])
        self.pos = d("pos", [1, T], I32)
        self.ada_w = d("ada_w_t", [4, 24, 128, 16 * 512])
        self.ada_b = d("ada_b_t", [4, 128, 96])
        self.kvada_w = d("kvada_w_t", [8, 128, 16 * 512])
        self.kvada_b = d("kvada_b_t", [128, 32])
        self.normg = d("normg_t", [4, 2, 128, 16])
        self.kvnormg = d("kvnormg_t", [128, 16])
        self.w1 = d("mlp_w1_t", [4, 64, 128, 16 * 128])
        self.w2 = d("mlp_w2_t", [4, 16, 2, 128, 32 * 128])
        self.win_fm = d("gdn_win_fm", [2, 64, 128, 16 * 128])
        self.wz = d("gdn_wz_t", [2, 8, 128, 16 * 512])
        self.wba = d("gdn_wba_t", [2, 128, 16 * 64])
        self.convw = d("gdn_conv_t", [2, 128, 64 * 4])
        self.alog = d("gdn_alog_b", [2, 128, 32])
        self.dtb = d("gdn_dtb_b", [2, 128, 32])
        self.onorm = d("gdn_onorm_b", [2, 128, 128])
        self.wout = d("gdn_wout_t", [2, 16, 128, 32 * 128])
        self.wkvk = d("wkv_k_t", [12, 128, 16 * 128])
        self.wkvv = d("wkv_v_t", [3, 128, 16 * 512])
        self.knorm = d("knorm_t", [128, 3])
        self.qnorm = d("qnorm_t", [2, 128, 3])
        self.wq = d("attn_wq_t", [2, 48, 128, 16 * 128])
        self.wo = d("attn_wo_t", [2, 16, 128, 16 * 128])
        self.c_f32 = d("c_f32", [128, 10 * 128])
        self.c_bf = d("c_bf", [128, 5 * 128], BF16)
        self.c_invf = d("c_invf", [128, 1])
        self.yT = self.nc.dram_tensor("yT", [D, T], F32, kind="ExternalOutput").ap()
        s = self.dscr
        self.qT_s = s("qT_s", [16, 128, T], BF16)
        self.kT_s = s("kT_s", [16, 128, T], BF16)
        self.ktok_s = s("ktok_s", [16, T, 128], BF16)
        self.vtok_s = s("vtok_s", [32, T, 128], BF16)
        self.ztok_s = s("ztok_s", [T, 4096], BF16)
        self.ogT_s = s("ogT_s", [4096, T], BF16)
        self.KT_s = s("KT_s", [12, 128, T], BF16)
        self.Vb_s = s("Vb_s", [3, 32, 128, 512], BF16)
        self.QT_s = s("QT_s", [48, 128, T], BF16)
        self.aoT_s = s("aoT_s", [2048, T], BF16)

    def load_consts(self, es):
        nc, P = self.nc, self.P
        self.cf = sb(nc, es, "cf", [128, 10 * 128], F32)
        self.cb = sb(nc, es, "cbf", [128, 5 * 128], BF16)
        self.cB = Buf("consts")
        P.dma("sp", self.cf[:], self.c_f32[:, :], writes=[self.cB])
        P.dma("sp", self.cb[:], self.c_bf[:, :], writes=[self.cB])
        cf, cb = self.cf, self.cb
        sl = lambda i: slice(i * 128, (i + 1) * 128)
        self.ident = cf[:, sl(0)]
        self.U = cf[:, sl(1)]
        self.SL = cf[:, sl(2)]
        self.SU = cf[:, sl(3)]
        self.BD32 = cf[:, sl(4)]
        self.OFF1T = cf[:, sl(5)]
        self.OFF2T = cf[:, sl(6)]
        self.ones_f = cf[:, sl(7)]
        self.ident_b = cb[:, sl(0)]
        self.ones_b = cb[:, sl(1)]
        self.rotm = cb[:, sl(2)]
        self.maskPC = cb[:, 3 * 128:5 * 128]
        self.modT = sb(nc, es, "modT", [128, 5, 96], F32)
        self.modB = Buf("modT")
        self.cols = sb(nc, es, "cols", [128, 9, 2, 16], F32)
        self.colsB = Buf("cols")

    def st_mods(self):
        nc, P = self.nc, self.P
        with ExitStack() as es:
            cc = sb(nc, es, "cc", [128, 16], F32)
            ccb = sb(nc, es, "ccb", [128, 16], BF16)
            cB = Buf("cc")
            P.dma("sp", cc[:], self.ccol[:, :], writes=[cB])
            P.op("act", f_act(nc, ccb[:], cc[:], AF.Silu), [cB], [cB])
            wp = sb_rot(nc, es, "mw", 3, [128, 16, 512], BF16)
            ng = sb(nc, es, "ng", [128, 9, 16], F32)
            ab = sb(nc, es, "ab", [128, 5, 96], F32)
            ngB = Buf("ng")
            for l in range(4):
                for j in range(2):
                    P.dma("sp", ng[:, l * 2 + j, :], self.normg[l, j], writes=[ngB])
                P.dma("sp", ab[:, l, :], self.ada_b[l], writes=[ngB])
            P.dma("sp", ng[:, 8, :], self.kvnormg[:, :], writes=[ngB])
            P.dma("sp", ab[:, 4, 0:32], self.kvada_b[:, :], writes=[ngB])
            for l in range(5):
                nblk = 24 if l < 4 else 8
                ps, pb = self.bank()
                for blk in range(nblk):
                    wt, wb = wp.get()
                    src = self.ada_w[l, blk] if l < 4 else self.kvada_w[blk]
                    P.dma("pool", wt[:], src.rearrange("p (c n) -> p c n", c=16), writes=[wb])
                    for j in range(4):
                        col = blk * 4 + j
                        for kc in range(16):
                            P.op("pe", f_mm(nc, ps[:, col:col + 1], wt[:, kc, j * 128:(j + 1) * 128],
                                            ccb[:, kc:kc + 1], kc == 0, kc == 15), [wb, cB], [pb])
                nco = nblk * 4
                P.op("dve", f_tt(nc.vector, self.modT[:, l, 0:nco], ps[:, 0:nco], ab[:, l, 0:nco], ALU.add),
                     [pb, ngB], [self.modB])
            for st in range(9):
                if st < 8:
                    l, j = st // 2, st % 2
                    sh = self.modT[:, l, (0 + 48 * j):(16 + 48 * j)]
                    sc = self.modT[:, l, (16 + 48 * j):(32 + 48 * j)]
                else:
                    sh = self.modT[:, 4, 0:16]
                    sc = self.modT[:, 4, 16:32]
                P.op("dve", f_stt(nc.vector, self.cols[:, st, 0, :], sc, 1.0, ng[:, st, :], ALU.add, ALU.mult),
                     [self.modB, ngB], [self.colsB])
                P.op("dve", f_ts(nc.vector, self.cols[:, st, 0, :], self.cols[:, st, 0, :], SQD, None, ALU.mult),
                     [self.colsB], [self.colsB])
                P.op("dve", f_copy(nc.vector, self.cols[:, st, 1, :], sh), [self.modB], [self.colsB])
            P.flush()

    def rsqrt(self, dst, dstB, src, srcB, c):
        nc, P = self.nc, self.P
        P.op("act", f_act(nc, dst, src, AF.Sqrt, bias=float(c)), [srcB], [dstB])
        P.op("dve", (lambda: nc.vector.reciprocal(out=dst, in_=dst)), [dstB], [dstB])

    def gate(self, l, j):
        return self.modT[:, l, (32 + 48 * j):(48 + 48 * j)]

    def norm_tile(self, pools, x_src, t0, dst, dstB, st, W=512):
        nc, P = self.nc, self.P
        xp, sqp, rp = pools
        xv = x_src.rearrange("(c p) t -> p c t", p=128)
        xt, xb = xp.get()
        P.dma("sp", xt[:], xv[:, :, t0:t0 + W], writes=[xb])
        sq, sqb = sqp.get()
        P.op("act", f_act(nc, sq[:], xt[:], AF.Square), [xb], [sqb])
        ps, pb = self.bank(6, 8)
        for kc in range(16):
            P.op("pe", f_mm(nc, ps[:, 0:W], self.ones_b, sq[:, kc, :], kc == 0, kc == 15), [sqb, self.cB], [pb])
        r, rb = rp.get()
        self.rsqrt(r[:], rb, ps[:, 0:W], pb, 2048.0 * EPS)
        A = self.cols[:, st, 0, :]
        sh = self.cols[:, st, 1, :]
        P.op("dve", f_tt(nc.vector, xt[:], xt[:], bc_t(A, W), ALU.mult), [xb, self.colsB], [xb])
        P.op("pool", f_tt(nc.gpsimd, xt[:], xt[:], bc_c(r[:, :], 16), ALU.mult), [xb, rb], [xb])
        P.op("dve", f_tt(nc.vector, dst, xt[:], bc_t(sh, W), ALU.add), [xb, self.colsB], [dstB])

    def norm_pools(self, es, n=1, W=512):
        nc = self.nc
        return (sb_rot(nc, es, "nx", n, [128, 16, W], F32),
                sb_rot(nc, es, "nsq", n, [128, 16, W], BF16),
                sb_rot(nc, es, "nr", n, [128, W], F32))

    def st_mlp(self, l, x_in, x_out):
        nc, P = self.nc, self.P
        TT = 1024
        with ExitStack() as es:
            pools = self.norm_pools(es, 1)
            hp = sb_rot(nc, es, "mh", 1, [128, 16, TT], BF16)
            hid = sb(nc, es, "hid", [128, 32, TT], BF16)
            hidB = [[Buf("hid") for _ in range(2)] for _ in range(32)]
            w1p = sb_rot(nc, es, "w1", 3, [128, 16, 128], BF16)
            w2p = sb_rot(nc, es, "w2", 2, [128, 32, 128], BF16)
            rl = sb_rot(nc, es, "rl", 3, [128, 512], BF16)
            xo = sb_rot(nc, es, "xo", 3, [128, 512], F32)
            gt = self.gate(l, 1)
            xiv = x_in.rearrange("(c p) t -> p c t", p=128)
            xov = x_out.rearrange("(c p) t -> p c t", p=128)
            unit = 0
            for tb in range(T // TT):
                h, hB0 = hp.get()
                hB = [Buf("mhs") for _ in range(2)]
                for s in range(2):
                    self.norm_tile(pools, x_in, tb * TT + s * 512, h[:, :, s * 512:(s + 1) * 512], hB[s], l * 2 + 1)
                for half in range(2):
                    for fc in range(32):
                        wt, wb = w1p.get()
                        P.dma("pool", wt[:], self.w1[l, half * 32 + fc].rearrange("p (c n) -> p c n", c=16), writes=[wb])
                        for s in range(2):
                            ps, pb = self.bank(0, 4)
                            for kc in range(16):
                                P.op("pe", f_mm(nc, ps[:], wt[:, kc, :], h[:, kc, s * 512:(s + 1) * 512], kc == 0, kc == 15),
                                     [wb, hB[s]], [pb])
                            dst = hid[:, fc, s * 512:(s + 1) * 512]
                            r, rb = rl.get()
                            P.op("act", f_act(nc, r[:], ps[:], AF.Relu), [pb], [rb])
                            if unit % 2 == 0:
                                P.op("dve", f_tt(nc.vector, dst, r[:], r[:], ALU.mult), [rb], [hidB[fc][s]])
                            else:
                                P.op("pool", f_tt(nc.gpsimd, dst, r[:], r[:], ALU.mult), [rb], [hidB[fc][s]])
                            unit += 1
                    xsrc = xiv if half == 0 else xov
                    if half == 0:
                        xtok = [[Buf("xtok") for _ in range(2)] for _ in range(16)]
                    for fo in range(16):
                        wt, wb = w2p.get()
                        P.dma("pool", wt[:], self.w2[l, fo, half].rearrange("p (c n) -> p c n", c=32), writes=[wb])
                        for s in range(2):
                            t0 = tb * TT + s * 512
                            xt, xb = xo.get()
                            P.dma("sp", xt[:], xsrc[:, fo, t0:t0 + 512], reads=[xtok[fo][s]], writes=[xb])
                            ps, pb = self.bank(4, 6)
                            for kc in range(32):
                                P.op("pe", f_mm(nc, ps[:], wt[:, kc, :], hid[:, kc, s * 512:(s + 1) * 512], kc == 0, kc == 31),
                                     [wb, hidB[kc][s]], [pb])
                            P.op("dve", f_stt(nc.vector, xt[:], ps[:], gt[:, fo:fo + 1], xt[:], ALU.mult, ALU.add),
                                 [pb, xb, self.modB], [xb])
                            P.dma("sp", xov[:, fo, t0:t0 + 512], xt[:], reads=[xb], writes=[xtok[fo][s]])
            P.flush()

    def st_proj_resid(self, actT, kcn, w_t, gate, x_in, x_out):
        nc, P = self.nc, self.P
        TT = 1024
        with ExitStack() as es:
            ap_ = sb_rot(nc, es, "pa", 1, [128, kcn, TT], BF16)
            wp = sb_rot(nc, es, "pw", 3, [128, kcn, 128], BF16)
            xo = sb_rot(nc, es, "px", 4, [128, 512], F32)
            av = actT.rearrange("(c p) t -> p c t", p=128)
            xiv = x_in.rearrange("(c p) t -> p c t", p=128)
            xov = x_out.rearrange("(c p) t -> p c t", p=128)
            for tb in range(T // TT):
                a, aB = ap_.get()
                for c0 in range(0, kcn, 8):
                    P.dma("sp", a[:, c0:c0 + 8, :], av[:, c0:c0 + 8, tb * TT:(tb + 1) * TT], writes=[aB])
                for fo in range(16):
                    wt, wb = wp.get()
                    P.dma("pool", wt[:], w_t[fo].rearrange("p (c n) -> p c n", c=kcn), writes=[wb])
                    for s in range(2):
                        t0 = tb * TT + s * 512
                        xt, xb = xo.get()
                        P.dma("sp", xt[:], xiv[:, fo, t0:t0 + 512], writes=[xb])
                        ps, pb = self.bank(0, 4)
                        for kc in range(kcn):
                            P.op("pe", f_mm(nc, ps[:], wt[:, kc, :], a[:, kc, s * 512:(s + 1) * 512], kc == 0, kc == kcn - 1),
                                 [wb, aB], [pb])
                        P.op("dve", f_stt(nc.vector, xt[:], ps[:], gate[:, fo:fo + 1], xt[:], ALU.mult, ALU.add),
                             [pb, xb, self.modB], [xb])
                        P.dma("sp", xov[:, fo, t0:t0 + 512], xt[:], reads=[xb])
            P.flush()

    def st_gdn_proj(self, l, x_in, bg, bgB):
        nc, P = self.nc, self.P
        with ExitStack() as es:
            hT = sb(nc, es, "hT", [128, 16, T], BF16)
            hB = [Buf("hT%d" % i) for i in range(8)]
            with ExitStack() as es2:
                pools = self.norm_pools(es2, 1)
                for tt in range(8):
                    self.norm_tile(pools, x_in, tt * 512, hT[:, :, tt * 512:(tt + 1) * 512], hB[tt], l * 2)
                P.flush()
            with ExitStack() as es2:
                wba = sb(nc, es2, "wba", [128, 16, 64], BF16)
                wbB = Buf("wba")
                P.dma("pool", wba[:], self.wba[l].rearrange("p (c n) -> p c n", c=16), writes=[wbB])
                prm = sb(nc, es2, "prm", [128, 2, 32], F32)
                prB = Buf("prm")
                P.dma("sp", prm[:, 0, :], self.alog[l], writes=[prB])
                P.dma("sp", prm[:, 1, :], self.dtb[l], writes=[prB])
                tmp = sb(nc, es2, "bgtmp", [128, 32, 32], F32)
                tB = Buf("bgtmp")
                for ts in range(32):
                    ps, pb = self.bank(0, 4)
                    for kc in range(16):
                        P.op("pe", f_mm(nc, ps[:, 0:64], hT[:, kc, ts * 128:(ts + 1) * 128], wba[:, kc, :], kc == 0, kc == 15),
                             [hB[ts // 4], wbB], [pb])
                    P.op("act", (lambda ps=ps, ts=ts: nc.scalar.copy(out=bg[:, ts, :], in_=ps[:, 0:64])), [pb], [bgB])
                P.op("act", f_act(nc, bg[:, :, 0:32], bg[:, :, 0:32], AF.Sigmoid), [bgB], [bgB])
                P.op("dve", f_tt(nc.vector, tmp[:], bg[:, :, 32:64], bc_c(prm[:, 1, :], 32), ALU.add), [bgB, prB], [tB])
                P.op("act", f_act(nc, tmp[:], tmp[:], AF.Exp), [tB], [tB])
                P.op("act", f_act(nc, tmp[:], tmp[:], AF.Ln, bias=1.0), [tB], [tB])
                P.op("act", f_act(nc, prm[:, 0, :], prm[:, 0, :], AF.Exp), [prB], [prB])
                P.op("dve", f_stt(nc.vector, bg[:, :, 32:64], tmp[:], -1.0, bc_c(prm[:, 0, :], 32), ALU.mult, ALU.mult),
                     [tB, prB], [bgB])
                P.flush()
            with ExitStack() as es2:
                wzp = sb_rot(nc, es2, "wz", 2, [128, 16, 512], BF16)
                zo = sb_rot(nc, es2, "zo", 4, [128, 512], BF16)
                for zb in range(8):
                    wt, wb = wzp.get()
                    P.dma("pool", wt[:], self.wz[l, zb].rearrange("p (c n) -> p c n", c=16), writes=[wb])
                    for ts in range(32):
                        ps, pb = self.bank(0, 4)
                        for kc in range(16):
                            P.op("pe", f_mm(nc, ps[:], hT[:, kc, ts * 128:(ts + 1) * 128], wt[:, kc, :], kc == 0, kc == 15),
                                 [hB[ts // 4], wb], [pb])
                        z, zB = zo.get()
                        P.op("act", f_act(nc, z[:], ps[:], AF.Silu), [pb], [zB])
                        P.dma("sp", self.ztok_s[ts * 128:(ts + 1) * 128, zb * 512:(zb + 1) * 512], z[:], reads=[zB])
                P.flush()
            with ExitStack() as es2:
                cw = sb(nc, es2, "cw", [128, 64, 4], F32)
                cwB = Buf("cw")
                P.dma("sp", cw[:], self.convw[l].rearrange("p (c j) -> p c j", j=4), writes=[cwB])
                wp = sb_rot(nc, es2, "wi", 3, [128, 16, 128], BF16)
                pbuf = sb_rot(nc, es2, "pbuf", 3, [128, 515], F32)
                accp = sb_rot(nc, es2, "acc", 3, [128, 512], F32)
                svp = sb_rot(nc, es2, "sv", 3, [128, 512], F32)
                sqp = sb_rot(nc, es2, "sq", 2, [128, 512], BF16)
                rsp = sb_rot(nc, es2, "rs", 2, [128, 512], F32)
                qnp = sb_rot(nc, es2, "qn", 3, [128, 512], BF16)
                svbp = sb_rot(nc, es2, "svb", 3, [128, 512], BF16)
                tkp = sb_rot(nc, es2, "tk", 3, [128, 4, 128], BF16)
                ctp = sb_rot(nc, es2, "ct", 2, [128, 512], F32)
                unit = 0
                for fc in range(64):
                    wt, wb = wp.get()
                    P.dma("pool", wt[:], self.win_fm[l, fc].rearrange("p (c n) -> p c n", c=16), writes=[wb])
                    prev = None
                    for tt in range(8):
                        ps, pb = self.bank(0, 4)
                        for kc in range(16):
                            P.op("pe", f_mm(nc, ps[:], wt[:, kc, :], hT[:, kc, tt * 512:(tt + 1) * 512], kc == 0, kc == 15),
                                 [wb, hB[tt]], [pb])
                        pbt, pbB = pbuf.get()
                        P.op("act", (lambda pbt=pbt, ps=ps: nc.scalar.copy(out=pbt[:, 3:515], in_=ps[:])), [pb], [pbB])
                        if prev is None:
                            P.op("pool", f_memset(nc.gpsimd, pbt[:, 0:3], 0.0), [], [pbB])
                        else:
                            P.op("pool", f_copy(nc.gpsimd, pbt[:, 0:3], prev[0][:, 512:515]), [prev[1]], [pbB])
                        prev = (pbt, pbB)
                        eng, en = (nc.vector, "dve")
                        unit += 1
                        acc, aB = accp.get()
                        P.op(en, f_ts(eng, acc[:], pbt[:, 0:512], cw[:, fc, 0:1], None, ALU.mult), [pbB, cwB], [aB])
                        for j in range(1, 4):
                            if en == "dve":
                                P.op(en, f_stt(eng, acc[:], pbt[:, j:j + 512], cw[:, fc, j:j + 1], acc[:], ALU.mult, ALU.add),
                                     [pbB, cwB, aB], [aB])
                            else:
                                ctmp, ctB = ctp.get()
                                P.op(en, f_ts(eng, ctmp[:], pbt[:, j:j + 512], cw[:, fc, j:j + 1], None, ALU.mult), [pbB, cwB], [ctB])
                                P.op(en, f_tt(eng, acc[:], acc[:], ctmp[:], ALU.add), [ctB, aB], [aB])
                        sv, sB = svp.get() if fc < 32 else svbp.get()
                        P.op("act", f_act(nc, sv[:], acc[:], AF.Silu), [aB], [sB])
                        if fc < 32:
                            sq, sqB = sqp.get()
                            isq = fc < 16
                            P.op("act", f_act(nc, sq[:], sv[:], AF.Square, scale=(float(np.sqrt(128.0)) if isq else 1.0)), [sB], [sqB])
                            ps2, pb2 = self.bank(4, 6)
                            P.op("pe", f_mm(nc, ps2[:], self.ones_b, sq[:], True, True), [sqB, self.cB], [pb2])
                            rs, rB = rsp.get()
                            self.rsqrt(rs[:], rB, ps2[:], pb2, (128.0 * EPS if isq else EPS))
                            qn, qB = qnp.get()
                            if fc < 16:
                                P.op("pool", f_tt(nc.gpsimd, qn[:], sv[:], rs[:], ALU.mult), [sB, rB], [qB])
                                P.dma("sp", self.qT_s[fc, :, tt * 512:(tt + 1) * 512], qn[:], reads=[qB])
                                src = None
                            else:
                                P.op("pool", f_tt(nc.gpsimd, qn[:], sv[:], rs[:], ALU.mult), [sB, rB], [qB])
                                P.dma("sp", self.kT_s[fc - 16, :, tt * 512:(tt + 1) * 512], qn[:], reads=[qB])
                                src, srcB = qn, qB
                                dstd = self.ktok_s[fc - 16]
                        else:
                            src, srcB = sv, sB
                            dstd = self.vtok_s[fc - 32]
                        if src is not None:
                            ps3, pb3 = self.bank(6, 8)
                            for j in range(4):
                                P.op("pe", f_mm(nc, ps3[:, j * 128:(j + 1) * 128], src[:, j * 128:(j + 1) * 128], self.ident_b, True, True),
                                     [srcB, self.cB], [pb3])
                            tk, tB = tkp.get()
                            P.op("act", (lambda tk=tk, ps3=ps3: nc.scalar.copy(out=tk[:].rearrange("p j d -> p (j d)"), in_=ps3[:])),
                                 [pb3], [tB])
                            P.dma("sp", dstd[tt * 512:(tt + 1) * 512, :].rearrange("(j p) d -> p j d", p=128), tk[:], reads=[tB])
                P.flush()
            if self.dbg:
                dbg_bg = self.dscr(_nm("dbg_bg"), [128, 32 * 64], F32)
                P.dma("sp", dbg_bg, bg[:].rearrange("p a b -> p (a b)"), reads=[bgB])
                dbg_mod = self.dscr(_nm("dbg_mod"), [128, 5 * 96], F32)
                P.dma("sp", dbg_mod, self.modT[:].rearrange("p a b -> p (a b)"), reads=[self.modB])
                P.flush()

    def st_gdn_chunk(self, l, bg, bgB, n_chunks=32):
        nc, P = self.nc, self.P
        V, G = nc.vector, nc.gpsimd
        W = 8
        SH = [128, W, 128]
        with ExitStack() as es:
            kT = sb(nc, es, "ckT", [128, 16, 128], BF16)
            qT = sb(nc, es, "cqT", [128, 16, 128], BF16)
            ktok = sb(nc, es, "cktok", [128, 16, 128], BF16)
            vtok = sb(nc, es, "cvtok", [128, 32, 128], BF16)
            zt = sb(nc, es, "czt", [128, 32, 128], BF16)
            kTB, qTB, ktB, vtB, ztB = Buf("kT"), Buf("qT"), Buf("ktok"), Buf("vtok"), Buf("zt")
            S = sb(nc, es, "S", [128, 32, 128], F32)
            Sb = sb(nc, es, "Sb", [128, 32, 128], BF16)
            SB = [Buf("S%d" % w) for w in range(4)]
            SbB = [Buf("Sb%d" % w) for w in range(4)]
            ogT = sb(nc, es, "ogT", [128, 32, 128], BF16)
            ogB = Buf("ogT")
            eAll = sb(nc, es, "eAll", [128, 96], F32)
            eB = Buf("eAll")
            onr = sb(nc, es, "onr", [128, 128], F32)
            onB = Buf("onr")
            P.dma("sp", onr[:], self.onorm[l], writes=[onB])
            P.op("dve", f_ts(V, onr[:], onr[:], float(np.sqrt(128.0)), None, ALU.mult), [onB], [onB])
            P.op("pool", f_memset(G, S[:], 0.0), [], SB)
            P.op("pool", f_memset(G, Sb[:], 0.0), [], SbB)
            cB = self.cB
            I_, U_, SL_, SU_, BD, O1T, O2T = self.ident, self.U, self.SL, self.SU, self.BD32, self.OFF1T, self.OFF2T
            Ib = self.ident_b

            def mk_slot(i):
                d = {}
                for nm in ("E", "DTs", "DTi", "P0", "P1", "kq", "Gm"):
                    d[nm] = (sb(nc, es, "s%d%s" % (i, nm), SH, F32), Buf(nm))
                for nm in ("A", "AT", "B0", "BT0", "B1", "BT1", "Ao1T", "Ao2T", "iT", "Pb0", "Pb1", "keg", "kd", "vn", "og"):
                    d[nm] = (sb(nc, es, "s%d%s" % (i, nm), SH, BF16), Buf(nm))
                d["ss"] = (sb(nc, es, "s%dss" % i, [128, 2, W], F32), Buf("ss"))
                return d

            slots = [mk_slot(0), mk_slot(1)]
            pd_rr = [0]
            pd_gen = [0, 0, 0, 0]

            def pd():
                i = pd_rr[0] % 4
                pd_rr[0] += 1
                pd_gen[i] += 1
                t = self.psd[i][:, :].rearrange("p (h d) -> p h d", h=W)
                return (t, [self.psB[2 * i], self.psB[2 * i + 1]], i, pd_gen[i])

            def chk(pt):
                assert pd_gen[pt[2]] == pt[3], "PSUM tile reused before its reader was emitted"
                return pt[0]

            def mm(pt, h, lhsT, rhs, rd, start=True, stop=True):
                P.op("pe", f_mm(nc, chk(pt)[:, h, :], lhsT, rhs, start, stop), rd, pt[1])

            def bc8(c):
                return bc_c(c, W)

            def pair(ap3):
                return ap3.rearrange("p (j t) d -> p j t d", t=2)

            def wave_gen(sl, n, w0):
                wv = w0 // W
                hk0 = w0 // 2
                T_ = lambda nm: sl[nm][0]
                B_ = lambda nm: sl[nm][1]
                beta = bc_t(bg[:, n, w0:w0 + W], 128)
                eG = bc_t(eAll[:, w0:w0 + W], 128)
                eGl = bc_t(eAll[:, 32 + w0:32 + w0 + W], 128)
                eGt = bc_t(eAll[:, 64 + w0:64 + w0 + W], 128)
                P.op("pool", f_tt(G, T_("Gm")[:], bc_t(bg[:, n, 32 + w0:32 + w0 + W], 128), bc8(SL_), ALU.mult), [bgB, cB], [B_("Gm")])
                pkq = pd()
                for j in range(4):
                    hk = hk0 + j
                    mm(pkq, 2 * j, kT[:, hk, :], kT[:, hk, :], [kTB])
                    mm(pkq, 2 * j + 1, kT[:, hk, :], qT[:, hk, :], [kTB, qTB])
                pdp = pd()
                for h in range(W):
                    mm(pdp, h, T_("Gm")[:, h, :], U_, [B_("Gm"), cB])
                yield
                P.op("dve", f_copy(V, T_("kq")[:], chk(pkq)), pkq[1], [B_("kq")])
                P.op("act", f_act(nc, T_("E")[:], chk(pdp), AF.Exp), pdp[1], [B_("E")])
                yield
                P.op("pool", f_tt(G, T_("DTs")[:], T_("E")[:], bc8(SU_), ALU.mult), [B_("E"), cB], [B_("DTs")])
                P.op("pool", f_tt(G, T_("DTi")[:], T_("E")[:], bc8(U_), ALU.mult), [B_("E"), cB], [B_("DTi")])
                kqv = pair(T_("kq")[:])
                KKb = kqv[:, :, 0, :].unsqueeze(2).to_broadcast([128, 4, 2, 128])
                QKb = kqv[:, :, 1, :].unsqueeze(2).to_broadcast([128, 4, 2, 128])
                P.op("dve", f_tt(V, pair(T_("DTs")[:]), pair(T_("DTs")[:]), KKb, ALU.mult), [B_("DTs"), B_("kq")], [B_("DTs")])
                P.op("dve", f_tt(V, T_("A")[:], T_("DTs")[:], beta, ALU.mult), [B_("DTs"), bgB], [B_("A")])
                P.op("dve", f_tt(V, pair(T_("iT")[:]), pair(T_("DTi")[:]), QKb, ALU.mult), [B_("DTi"), B_("kq")], [B_("iT")])
                yield
                pa = pd()
                for h in range(W):
                    mm(pa, h, T_("A")[:, h, :], Ib, [B_("A"), cB])
                yield
                P.op("act", (lambda d=T_("AT"), s_=chk(pa): nc.scalar.copy(out=d[:], in_=s_)), pa[1], [B_("AT")])
                P.op("pool", f_tt(G, T_("B0")[:], T_("A")[:], bc8(BD), ALU.mult), [B_("A"), cB], [B_("B0")])
                yield
                P.op("pool", f_tt(G, T_("BT0")[:], T_("AT")[:], bc8(BD), ALU.mult), [B_("AT"), cB], [B_("BT0")])
                P.op("pool", f_tt(G, T_("Ao1T")[:], T_("AT")[:], bc8(O1T), ALU.mult), [B_("AT"), cB], [B_("Ao1T")])
                P.op("dve", f_tt(V, T_("Ao2T")[:], T_("AT")[:], bc8(O2T), ALU.mult), [B_("AT"), cB], [B_("Ao2T")])
                P.op("dve", f_tt(V, T_("P0")[:], bc8(I_), T_("B0")[:], ALU.subtract), [B_("B0"), cB], [B_("P0")])
                P.op("act", (lambda d=T_("Pb0"), s_=T_("P0"): nc.scalar.copy(out=d[:], in_=s_[:])), [B_("P0")], [B_("Pb0")])
                yield
                cur, nxt = 0, 1
                for step in range(4):
                    last = (step == 3)
                    Bc, BTc = "B%d" % cur, "BT%d" % cur
                    Bn, BTn = "B%d" % nxt, "BT%d" % nxt
                    Pc, Pbc = "P%d" % cur, "Pb%d" % cur
                    Pn, Pbn = "P%d" % nxt, "Pb%d" % nxt
                    if not last:
                        p2 = pd()
                        for h in range(W):
                            mm(p2, h, T_(BTc)[:, h, :], T_(Bc)[:, h, :], [B_(BTc), B_(Bc)])
                    p2t = pd()
                    for h in range(W):
                        mm(p2t, h, T_(Bc)[:, h, :], T_(BTc)[:, h, :], [B_(BTc), B_(Bc)])
                    yield
                    if not last:
                        P.op("act", (lambda d=T_(Bn), s_=chk(p2): nc.scalar.copy(out=d[:], in_=s_)), p2[1], [B_(Bn)])
                    P.op("dve", f_copy(V, T_(BTn)[:], chk(p2t)), p2t[1], [B_(BTn)])
                    yield
                    pp = pd()
                    for h in range(W):
                        mm(pp, h, T_(BTn)[:, h, :], T_(Pbc)[:, h, :], [B_(BTn), B_(Pbc)])
                    yield
                    P.op("dve", f_tt(V, T_(Pn)[:], chk(pp), T_(Pc)[:], ALU.add), pp[1] + [B_(Pc)], [B_(Pn)])
                    P.op("pool", f_copy(G, T_(Pbn)[:], T_(Pn)[:]), [B_(Pn)], [B_(Pbn)])
                    cur, nxt = nxt, cur
                    yield
                for AoT in ("Ao1T", "Ao2T"):
                    Pc, Pbc = "P%d" % cur, "Pb%d" % cur
                    Pn, Pbn = "P%d" % nxt, "Pb%d" % nxt
                    px = pd()
                    for h in range(W):
                        mm(px, h, T_(AoT)[:, h, :], T_(Pbc)[:, h, :], [B_(AoT), B_(Pbc)])
                    pt_ = pd()
                    for h in range(W):
                        mm(pt_, h, T_(Pbc)[:, h, :], Ib, [B_(Pbc), cB])
                    yield
                    P.op("act", (lambda d=T_("A"), s_=chk(px): nc.scalar.copy(out=d[:], in_=s_)), px[1], [B_("A")])
                    P.op("dve", f_copy(V, T_("AT")[:], chk(pt_)), pt_[1], [B_("AT")])
                    yield
                    py = pd()
                    for h in range(W):
                        mm(py, h, T_("AT")[:, h, :], T_("A")[:, h, :], [B_("AT"), B_("A")])
                    yield
                    P.op("dve", f_tt(V, T_(Pn)[:], T_(Pc)[:], chk(py), ALU.subtract), py[1] + [B_(Pc)], [B_(Pn)])
                    P.op("pool", f_copy(G, T_(Pbn)[:], T_(Pn)[:]), [B_(Pn)], [B_(Pbn)])
                    cur, nxt = nxt, cur
                    yield
                TTb = "Pb%d" % cur
                ktb = ktok[:, hk0:hk0 + 4, :].unsqueeze(2).to_broadcast([128, 4, 2, 128])
                P.op("pool", f_tt(G, pair(T_("keg")[:]), ktb, pair(eG), ALU.mult), [ktB, eB], [B_("keg")])
                P.op("pool", f_tt(G, pair(T_("kd")[:]), ktb, pair(eGl), ALU.mult), [ktB, eB], [B_("kd")])
                yield
                pw = pd()
                for h in range(W):
                    mm(pw, h, T_("keg")[:, h, :], T_(TTb)[:, h, :], [B_("keg"), B_(TTb)])
                yield
                P.op("act", (lambda d=T_("A"), s_=chk(pw): nc.scalar.mul(out=d[:], in_=s_, mul=-1.0)), pw[1], [B_("A")])
                yield
                pv = pd()
                for h in range(W):
                    hv = w0 + h
                    mm(pv, h, T_(TTb)[:, h, :], vtok[:, hv, :], [B_(TTb), vtB], True, False)
                    mm(pv, h, T_("A")[:, h, :], Sb[:, hv, :], [B_("A"), SbB[wv]], False, True)
                po1 = pd()
                for h in range(W):
                    hv = w0 + h
                    mm(po1, h, qT[:, hk0 + h // 2, :], Sb[:, hv, :], [qTB, SbB[wv]])
                yield
                P.op("dve", f_tt(V, T_("vn")[:], chk(pv), beta, ALU.mult), pv[1] + [bgB], [B_("vn")])
                P.op("dve", f_tt(V, T_("E")[:], chk(po1), eG, ALU.mult), po1[1] + [eB], [B_("E")])
                yield
                po2 = pd()
                for h in range(W):
                    mm(po2, h, T_("iT")[:, h, :], T_("vn")[:, h, :], [B_("iT"), B_("vn")])
                psu = pd()
                for h in range(W):
                    mm(psu, h, T_("kd")[:, h, :], T_("vn")[:, h, :], [B_("kd"), B_("vn")])
                yield
                P.op("dve", f_tt(V, T_("DTs")[:], chk(po2), T_("E")[:], ALU.add), po2[1] + [B_("E")], [B_("DTs")])
                Sw = S[:, w0:w0 + W, :]
                P.op("pool", f_tt(G, Sw, Sw, eGt, ALU.mult), [SB[wv], eB], [SB[wv]])
                P.op("dve", f_tt(V, Sw, Sw, chk(psu), ALU.add), psu[1] + [SB[wv]], [SB[wv]])
                P.op("act", (lambda d=Sb[:, w0:w0 + W, :], s_=Sw: nc.scalar.copy(out=d, in_=s_)), [SB[wv]], [SbB[wv]])
                yield
                ss = T_("ss")
                P.op("pool", f_tt(G, T_("DTi")[:], T_("DTs")[:], T_("DTs")[:], ALU.mult), [B_("DTs")], [B_("DTi")])
                P.op("dve", (lambda d=ss[:, 0, :], s_=T_("DTi"): nc.vector.reduce_sum(out=d, in_=s_[:], axis=AX.X)), [B_("DTi")], [B_("ss")])
                self.rsqrt(ss[:, 1, :], B_("ss"), ss[:, 0, :], B_("ss"), 128.0 * EPS)
                yield
                P.op("dve", f_tt(V, T_("DTs")[:], T_("DTs")[:], bc_t(ss[:, 1, :], 128), ALU.mult), [B_("DTs"), B_("ss")], [B_("DTs")])
                P.op("pool", f_tt(G, T_("DTs")[:], T_("DTs")[:], bc8(onr[:, :]), ALU.mult), [B_("DTs"), onB], [B_("DTs")])
                P.op("dve", f_tt(V, T_("og")[:], T_("DTs")[:], zt[:, w0:w0 + W, :], ALU.mult), [B_("DTs"), ztB], [B_("og")])
                yield
                pg = pd()
                for h in range(W):
                    mm(pg, h, T_("og")[:, h, :], Ib, [B_("og"), cB])
                yield
                P.op("act", (lambda d=ogT[:, w0:w0 + W, :], s_=chk(pg): nc.scalar.copy(out=d, in_=s_)), pg[1], [ogB])

            for n in range(n_chunks):
                t0 = n * 128
                P.dma("sp", kT[:], self.kT_s[:, :, t0:t0 + 128].rearrange("h d t -> d h t"), writes=[kTB])
                P.dma("sp", qT[:], self.qT_s[:, :, t0:t0 + 128].rearrange("h d t -> d h t"), writes=[qTB])
                P.dma("sp", ktok[:], self.ktok_s[:, t0:t0 + 128, :].rearrange("h t d -> t h d"), writes=[ktB])
                P.dma("sp", vtok[:], self.vtok_s[:, t0:t0 + 128, :].rearrange("h t d -> t h d"), writes=[vtB])
                P.dma("sp", zt[:].rearrange("t h d -> t (h d)"), self.ztok_s[t0:t0 + 128, :], writes=[ztB])
                g_n = bg[:, n, 32:64]
                eps_, epB = self.psq.get()
                P.op("pe", f_mm(nc, eps_[:, 0:32], U_, g_n, True, True), [bgB, cB], [epB])
                P.op("pe", f_mm(nc, eps_[:, 32:64], SL_, g_n, True, True), [bgB, cB], [epB])
                P.op("pe", f_mm(nc, eps_[:, 64:96], self.ones_f, g_n, True, True), [bgB, cB], [epB])
                P.op("act", f_act(nc, eAll[:], eps_[:, 0:96], AF.Exp), [epB], [eB])
                for w0 in (0, 16):
                    alive = [wave_gen(slots[0], n, w0), wave_gen(slots[1], n, w0 + W)]
                    while alive:
                        nxt_ = []
                        for g in alive:
                            try:
                                next(g)
                                nxt_.append(g)
                            except StopIteration:
                                pass
                        alive = nxt_
                P.dma("sp", self.ogT_s[:, t0:t0 + 128].rearrange("(h p) t -> p h t", p=128), ogT[:], reads=[ogB])
            P.flush()

    def rot_tables(self, es):
        nc, P = self.nc, self.P
        V = nc.vector
        C32 = sb(nc, es, "C32", [32, T], F32)
        S32 = sb(nc, es, "S32", [32, T], F32)
        rB = Buf("rot")
        with ExitStack() as es2:
            posi = sb(nc, es2, "posi", [32, T], I32)
            ang = sb(nc, es2, "ang", [32, T], F32)
            tmp = sb(nc, es2, "rtmp", [32, T], F32)
            ang2 = sb(nc, es2, "ang2", [32, T], F32)
            ivf = sb(nc, es2, "ivf", [32, 1], F32)
            pB = Buf("posi")
            src = bass.AP(self.pos.tensor, 0, [[0, 32], [1, T]])
            P.dma("sp", posi[:], src, writes=[pB])
            P.dma("sp", ivf[:], self.c_invf[0:32, :], writes=[pB])
            P.op("dve", f_copy(V, ang[:], posi[:]), [pB], [pB])
            P.op("dve", f_ts(V, ang[:], ang[:], ivf[:, 0:1], None, ALU.mult), [pB], [pB])
            pi = float(np.pi)
            for (dst, off) in ((S32, 0.0), (C32, 0.5 * pi)):
                P.op("dve", f_ts(V, ang2[:], ang[:], off, None, ALU.add), [pB], [pB])
                P.op("dve", f_ts(V, tmp[:], ang2[:], 1.0 / (2.0 * pi), None, ALU.mult), [pB], [pB])
                P.op("dve", f_copy(V, posi[:], tmp[:]), [pB], [pB])
                P.op("dve", f_copy(V, tmp[:], posi[:]), [pB], [pB])
                P.op("dve", f_stt(V, ang2[:], tmp[:], -2.0 * pi, ang2[:], ALU.mult, ALU.add), [pB], [pB])
                P.op("dve", f_ts(V, tmp[:], ang2[:], pi, 2.0 * pi, ALU.is_gt, ALU.mult), [pB], [pB])
                P.op("dve", f_tt(V, ang2[:], ang2[:], tmp[:], ALU.subtract), [pB], [pB])
                P.op("dve", f_ts(V, tmp[:], ang2[:], -pi, 2.0 * pi, ALU.is_lt, ALU.mult), [pB], [pB])
                P.op("dve", f_tt(V, ang2[:], ang2[:], tmp[:], ALU.add), [pB], [pB])
                P.op("act", f_act(nc, dst[:], ang2[:], AF.Sin), [pB], [rB, pB])
            P.flush()
        return C32, S32, rB

    def qk_pools(self, es):
        nc = self.nc
        return dict(sv=sb_rot(nc, es, "qsv", 2, [128, 512], F32), sq=sb_rot(nc, es, "qsq", 1, [128, 512], BF16),
                    rs=sb_rot(nc, es, "qrs", 1, [128, 512], F32), qb=sb_rot(nc, es, "qqb", 2, [128, 512], BF16),
                    t1=sb_rot(nc, es, "qt1", 1, [32, 512], F32), t2=sb_rot(nc, es, "qt2", 1, [32, 512], F32),
                    o=sb_rot(nc, es, "qo", 2, [128, 512], BF16))

    def qk_post(self, pl, ps, pb, gcol, gB, rot, t0, dst):
        nc, P = self.nc, self.P
        V, G = nc.vector, nc.gpsimd
        C32, S32, rB = rot
        sv, svB = pl["sv"].get()
        P.op("act", (lambda sv=sv, ps=ps: nc.scalar.copy(out=sv[:], in_=ps[:])), [pb], [svB])
        sq, sqB = pl["sq"].get()
        P.op("act", f_act(nc, sq[:], ps[:], AF.Square), [pb], [sqB])
        ps2, pb2 = self.bank(4, 6)
        P.op("pe", f_mm(nc, ps2[:], self.ones_b, sq[:], True, True), [sqB, self.cB], [pb2])
        rs, rsB = pl["rs"].get()
        self.rsqrt(rs[:], rsB, ps2[:], pb2, 128.0 * EPS)
        qb, qbB = pl["qb"].get()
        P.op("dve", f_stt(V, qb[:], sv[:], gcol, rs[:], ALU.mult, ALU.mult), [svB, rsB, gB], [qbB])
        ps3, pb3 = self.bank(6, 8)
        P.op("pe", f_mm(nc, ps3[0:32, :], self.rotm[:, 0:32], qb[:], True, True), [qbB, self.cB], [pb3])
        t1, t1B = pl["t1"].get()
        P.op("pool", f_tt(G, t1[:], qb[0:32, :], C32[:, t0:t0 + 512], ALU.mult), [qbB, rB], [t1B])
        t2, t2B = pl["t2"].get()
        P.op("dve", f_tt(V, t2[:], ps3[0:32, :], S32[:, t0:t0 + 512], ALU.mult), [pb3, rB], [t2B])
        o, oB = pl["o"].get()
        P.op("pool", f_copy(G, o[:], qb[:]), [qbB], [oB])
        P.op("pool", f_tt(G, o[0:32, :], t1[:], t2[:], ALU.add), [t1B, t2B, oB], [oB])
        P.dma("sp", dst, o[:], reads=[oB])

    def st_kv(self, x_in):
        nc, P = self.nc, self.P
        with ExitStack() as es:
            rot = self.rot_tables(es)
            hT = sb(nc, es, "hT", [128, 16, T], BF16)
            hB = [Buf("hT%d" % i) for i in range(8)]
            with ExitStack() as es2:
                pools = self.norm_pools(es2, 1, 256)
                for tt in range(16):
                    self.norm_tile(pools, x_in, tt * 256, hT[:, :, tt * 256:(tt + 1) * 256], hB[tt // 2], 8, 256)
                P.flush()
            with ExitStack() as es2:
                wvp = sb_rot(nc, es2, "wv", 1, [128, 16, 512], BF16)
                vo = sb_rot(nc, es2, "vo", 4, [128, 512], BF16)
                for gi, dil in enumerate((1, 4, 16)):
                    wt, wb = wvp.get()
                    P.dma("pool", wt[:], self.wkvv[gi].rearrange("p (c n) -> p c n", c=16), writes=[wb])
                    nbr = 32 // dil
                    for blk in range(32):
                        r, nb = blk // nbr, blk % nbr
                        st_ = r + dil * 128 * nb
                        ps, pb = self.bank(0, 4)
                        for kc in range(16):
                            P.op("pe", f_mm(nc, ps[:], hT[:, kc, st_:st_ + dil * 127 + 1:dil], wt[:, kc, :], kc == 0, kc == 15),
                                 hB + [wb], [pb])
                        v, vB = vo.get()
                        P.op("act", (lambda v=v, ps=ps: nc.scalar.copy(out=v[:], in_=ps[:])), [pb], [vB])
                        P.dma("sp", self.Vb_s[gi, blk], v[:], reads=[vB])
                P.flush()
            with ExitStack() as es2:
                pl = self.qk_pools(es2)
                gk = sb(nc, es2, "gk", [128, 3], F32)
                gB = Buf("gk")
                P.dma("sp", gk[:], self.knorm[:, :], writes=[gB])
                P.op("dve", f_ts(nc.vector, gk[:], gk[:], float(np.sqrt(128.0)), None, ALU.mult), [gB], [gB])
                wp = sb_rot(nc, es2, "wk", 3, [128, 16, 128], BF16)
                for c in range(12):
                    wt, wb = wp.get()
                    P.dma("pool", wt[:], self.wkvk[c].rearrange("p (c n) -> p c n", c=16), writes=[wb])
                    for tt in range(8):
                        ps, pb = self.bank(0, 4)
                        for kc in range(16):
                            P.op("pe", f_mm(nc, ps[:], wt[:, kc, :], hT[:, kc, tt * 512:(tt + 1) * 512], kc == 0, kc == 15),
                                 [wb, hB[tt]], [pb])
                        self.qk_post(pl, ps, pb, gk[:, c // 4:c // 4 + 1], gB, rot, tt * 512,
                                     self.KT_s[c, :, tt * 512:(tt + 1) * 512])
                P.flush()

    def st_q(self, l, x_in):
        nc, P = self.nc, self.P
        j = l - 2
        with ExitStack() as es:
            rot = self.rot_tables(es)
            hT = sb(nc, es, "hT", [128, 16, T], BF16)
            hB = [Buf("hT%d" % i) for i in range(8)]
            with ExitStack() as es2:
                pools = self.norm_pools(es2, 1, 256)
                for tt in range(16):
                    self.norm_tile(pools, x_in, tt * 256, hT[:, :, tt * 256:(tt + 1) * 256], hB[tt // 2], l * 2, 256)
                P.flush()
            with ExitStack() as es2:
                pl = self.qk_pools(es2)
                gq = sb(nc, es2, "gq", [128, 3], F32)
                gB = Buf("gq")
                P.dma("sp", gq[:], self.qnorm[j], writes=[gB])
                P.op("dve", f_ts(nc.vector, gq[:], gq[:], float(np.sqrt(128.0)), None, ALU.mult), [gB], [gB])
                wp = sb_rot(nc, es2, "wq", 3, [128, 16, 128], BF16)
                for c in range(48):
                    wt, wb = wp.get()
                    P.dma("pool", wt[:], self.wq[j, c].rearrange("p (c n) -> p c n", c=16), writes=[wb])
                    for tt in range(8):
                        ps, pb = self.bank(0, 4)
                        for kc in range(16):
                            P.op("pe", f_mm(nc, ps[:], wt[:, kc, :], hT[:, kc, tt * 512:(tt + 1) * 512], kc == 0, kc == 15),
                                 [wb, hB[tt]], [pb])
                        self.qk_post(pl, ps, pb, gq[:, c // 16:c // 16 + 1], gB, rot, tt * 512,
                                     self.QT_s[c, :, tt * 512:(tt + 1) * 512])
                P.flush()

    def st_attn(self):
        nc, P = self.nc, self.P
        V, G = nc.vector, nc.gpsimd
        HALF = 2048
        sc_scale = float(128.0 ** -0.5)
        with ExitStack() as es:
            accO = sb(nc, es, "accO", [128, 4, HALF], F32)
            accD = sb(nc, es, "accD", [128, 4, HALF], F32)
            aB, dB_ = Buf("accO"), Buf("accD")
            Qp = sb_rot(nc, es, "aQ", 2, [128, 4, HALF], BF16)
            Kp = sb_rot(nc, es, "aK", 2, [128, T], BF16)
            Vp = sb_rot(nc, es, "aV", 2, [128, 32, 128], BF16)
            PTp = sb_rot(nc, es, "aPT", 3, [128, 2, 4, 128], BF16)
            aop = sb_rot(nc, es, "aao", 1, [128, 4, HALF], BF16)
            cB = self.cB
            mask = self.maskPC.rearrange("p (b q) -> p b q", b=2)
            unit = 0
            for kvh in range(4):
                for a in range(2):
                    for gi, dil in enumerate((1, 4, 16)):
                        Q, QB = Qp.get()
                        for hq in range(4):
                            P.dma("sp", Q[:, hq, :], self.QT_s[gi * 16 + kvh * 4 + hq, :, a * HALF:(a + 1) * HALF], writes=[QB])
                        Kt, KB = Kp.get()
                        P.dma("sp", Kt[:], self.KT_s[gi * 4 + kvh], writes=[KB])
                        Vt, VB = Vp.get()
                        P.dma("sp", Vt[:], self.Vb_s[gi, :, :, kvh * 128:(kvh + 1) * 128].rearrange("b p d -> p b d"), writes=[VB])
                        nbr = 32 // dil
                        nbh = nbr // 2
                        for r in range(dil):
                            for nbi in range(nbh):
                                nb = a * nbh + nbi
                                blk = r * nbr + nb
                                ql = r + dil * 128 * nbi
                                qsl = slice(ql, ql + dil * 127 + 1, dil)
                                kc0 = r + dil * 128 * nb
                                ksl = slice(kc0, kc0 + dil * 127 + 1, dil)
                                has_prev = nb > 0
                                di = unit % 2
                                unit += 1
                                sc, scB = self.psd[di], self.psB[di * 2]
                                qr = Q[:, :, qsl]
                                if has_prev:
                                    kp0 = kc0 - dil * 128
                                    P.op("pe", f_mm(nc, sc[:, 0:512], Kt[:, kp0:kp0 + dil * 127 + 1:dil], qr, True, True), [KB, QB], [scB])
                                P.op("pe", f_mm(nc, sc[:, 512:1024], Kt[:, ksl], qr, True, True), [KB, QB], [scB])
                                PT, PB = PTp.get()
                                lo = 0 if has_prev else 1
                                P.op("act", f_act(nc, PT[:, lo:2].rearrange("p b h q -> p (b h q)"), sc[:, lo * 512:1024], AF.Exp, scale=sc_scale),
                                     [scB], [PB])
                                eng, en = (V, "dve") if unit % 2 == 0 else (G, "pool")
                                mk = mask[:, lo:2, :].unsqueeze(2).to_broadcast([128, 2 - lo, 4, 128])
                                P.op(en, f_tt(eng, PT[:, lo:2], PT[:, lo:2], mk, ALU.mult), [PB, cB], [PB])
                                ops_, opB = self.psb[4 + di * 2], self.psB[4 + di * 2]
                                dps_, dpB = self.psb[5 + di * 2], self.psB[5 + di * 2]
                                cur = PT[:, 1].rearrange("p h q -> p (h q)")
                                if has_prev:
                                    prv = PT[:, 0].rearrange("p h q -> p (h q)")
                                    P.op("pe", f_mm(nc, ops_[:], Vt[:, blk - 1, :], prv, True, False), [VB, PB], [opB])
                                    P.op("pe", f_mm(nc, ops_[:], Vt[:, blk, :], cur, False, True), [VB, PB], [opB])
                                    P.op("pe", f_mm(nc, dps_[:], self.ones_b, prv, True, False), [cB, PB], [dpB])
                                    P.op("pe", f_mm(nc, dps_[:], self.ones_b, cur, False, True), [cB, PB], [dpB])
                                else:
                                    P.op("pe", f_mm(nc, ops_[:], Vt[:, blk, :], cur, True, True), [VB, PB], [opB])
                                    P.op("pe", f_mm(nc, dps_[:], self.ones_b, cur, True, True), [cB, PB], [dpB])
                                ov = accO[:, :, qsl]
                                dv = accD[:, :, qsl]
                                o3 = ops_.rearrange("p (h q) -> p h q", h=4)
                                d3 = dps_.rearrange("p (h q) -> p h q", h=4)
                                if gi == 0:
                                    P.op("act", (lambda ov=ov, o3=o3: nc.scalar.copy(out=ov, in_=o3)), [opB], [aB])
                                    P.op("dve", f_copy(V, dv, d3), [dpB], [dB_])
                                else:
                                    P.op("dve", f_tt(V, ov, o3, ov, ALU.add), [opB, aB], [aB])
                                    P.op("dve", f_tt(V, dv, d3, dv, ALU.add), [dpB, dB_], [dB_])
                    P.op("dve", (lambda: nc.vector.reciprocal(out=accD[:], in_=accD[:])), [dB_], [dB_])
                    ao, aoB = aop.get()
                    P.op("pool", f_tt(G, ao[:], accO[:], accD[:], ALU.mult), [aB, dB_], [aoB])
                    for hq in range(4):
                        h = kvh * 4 + hq
                        P.dma("sp", self.aoT_s[h * 128:(h + 1) * 128, a * HALF:(a + 1) * HALF], ao[:, hq, :], reads=[aoB])
            P.flush()


def build(dbg=False, stages=None):
    k = K(dbg)
    k.declare()
    nc, P = k.nc, k.P
    allst = stages is None

    def want(s):
        return allst or s in stages

    with ExitStack() as es0:
        k.load_consts(es0)
        k.st_mods()
        for l in range(2):
            x_in = k.xT if l == 0 else k.yT
            if want("gdn%d" % l):
                with ExitStack() as esl:
                    bg = sb(nc, esl, "bg", [128, 32, 64], F32)
                    bgB = Buf("bg")
                    if want("gproj%d" % l):
                        k.st_gdn_proj(l, x_in, bg, bgB)
                    if want("gchunk%d" % l):
                        k.st_gdn_chunk(l, bg, bgB, n_chunks=(k.n_chunks if hasattr(k, "n_chunks") else 32))
            if want("gout%d" % l):
                k.st_proj_resid(k.ogT_s, 32, k.wout[l], k.gate(l, 0), x_in, k.yT)
            if want("mlp%d" % l):
                k.st_mlp(l, k.yT, k.yT)
        if (not allst) and "copyin" in stages:
            for c in range(16):
                P.dma("sp", k.yT[c * 128:(c + 1) * 128, :], k.xT[c * 128:(c + 1) * 128, :])
            P.flush()
        if want("kv"):
            k.st_kv(k.yT)
        for l in range(2, 4):
            if want("q%d" % l):
                k.st_q(l, k.yT)
            if want("attn%d" % l):
                k.st_attn()
            if want("aout%d" % l):
                k.st_proj_resid(k.aoT_s, 16, k.wo[l - 2], k.gate(l, 0), k.yT, k.yT)
            if want("mlp%d" % l):
                k.st_mlp(l, k.yT, k.yT)
        P.flush()
        P.barrier()
    return k


def tile_w(W, ncols):
    Kd, N = W.shape
    return np.ascontiguousarray(
        W.reshape(Kd // 128, 128, N // ncols, ncols).transpose(2, 1, 0, 3)).reshape(N // ncols, 128, (Kd // 128) * ncols)


def make_consts():
    import ml_dtypes
    i = np.arange(128)
    k_, m_ = i[:, None], i[None, :]
    ident = np.eye(128, dtype=np.float32)
    U = (k_ <= m_).astype(np.float32)
    SL = (k_ > m_).astype(np.float32)
    SU = (k_ < m_).astype(np.float32)
    BD32 = ((k_ // 32) == (m_ // 32)).astype(np.float32)
    rb, cb = k_ // 32, m_ // 32
    OFF1T = (((rb == 1) & (cb == 0)) | ((rb == 3) & (cb == 2))).astype(np.float32)
    OFF2T = ((rb >= 2) & (cb < 2)).astype(np.float32)
    ones = np.ones((128, 128), np.float32)
    z = np.zeros((128, 128), np.float32)
    c_f32 = np.concatenate([ident, U, SL, SU, BD32, OFF1T, OFF2T, ones, z, z], axis=1)
    rotm = np.zeros((128, 128), np.float32)
    for m in range(16):
        rotm[m + 16, m] = -1.0
        rotm[m, m + 16] = 1.0
    maskP = (k_ >= m_).astype(np.float32)
    maskC = (k_ <= m_).astype(np.float32)
    c_bf = np.concatenate([ident, ones, rotm, maskP, maskC], axis=1).astype(ml_dtypes.bfloat16)
    inv = (500000.0 ** (-np.arange(0, 32, 2, dtype=np.float32) / 32.0)).astype(np.float32)
    invf = np.zeros((128, 1), np.float32)
    invf[0:32, 0] = np.concatenate([inv, inv])
    return dict(c_f32=np.ascontiguousarray(c_f32), c_bf=np.ascontiguousarray(c_bf), c_invf=invf)


def prep_shared(inp):
    f = lambda a: np.ascontiguousarray(np.asarray(a, dtype=np.float32))
    sh = {}
    ada_w = f(inp["ada_w"])
    sh["ada_w_t"] = np.stack([tile_w(ada_w[l], 512) for l in range(4)])
    sh["ada_b_t"] = np.stack([f(inp["ada_b"][l]).reshape(96, 128).T for l in range(4)]).copy()
    sh["kvada_w_t"] = tile_w(f(inp["kv_ada_w"]), 512)
    sh["kvada_b_t"] = f(inp["kv_ada_b"]).reshape(32, 128).T.copy()
    ng = f(inp["norm_g"])
    sh["normg_t"] = np.ascontiguousarray(ng.reshape(4, 2, 16, 128).transpose(0, 1, 3, 2))
    sh["kvnormg_t"] = f(inp["kv_norm_g"]).reshape(16, 128).T.copy()
    w1, w2 = f(inp["mlp_w1"]), f(inp["mlp_w2"])
    sh["mlp_w1_t"] = np.stack([tile_w(w1[l], 128) for l in range(4)])
    sh["mlp_w2_t"] = np.stack([np.stack([tile_w(w2[l][h * 4096:(h + 1) * 4096], 128) for h in range(2)], axis=1)
                               for l in range(4)])
    win = f(inp["gdn_w_in"])
    sh["gdn_win_fm"] = np.stack([tile_w(win[l][:, 0:8192], 128) for l in range(2)])
    sh["gdn_wz_t"] = np.stack([tile_w(win[l][:, 8192:12288], 512) for l in range(2)])
    sh["gdn_wba_t"] = np.stack([tile_w(win[l][:, 12288:12352], 64)[0] for l in range(2)])
    cw = f(inp["gdn_conv_w"])
    sh["gdn_conv_t"] = np.ascontiguousarray(cw.reshape(2, 4, 64, 128).transpose(0, 3, 2, 1)).reshape(2, 128, 256)
    sh["gdn_alog_b"] = np.ascontiguousarray(np.broadcast_to(f(inp["gdn_a_log"])[:, None, :], (2, 128, 32)))
    sh["gdn_dtb_b"] = np.ascontiguousarray(np.broadcast_to(f(inp["gdn_dt_bias"])[:, None, :], (2, 128, 32)))
    sh["gdn_onorm_b"] = np.ascontiguousarray(np.broadcast_to(f(inp["gdn_onorm_g"])[:, None, :], (2, 128, 128)))
    wout = f(inp["gdn_w_out"])
    sh["gdn_wout_t"] = np.stack([tile_w(wout[l], 128) for l in range(2)])
    wkv = f(inp["w_kv"])
    sh["wkv_k_t"] = np.stack([tile_w(wkv[:, gi * 1024 + h * 128: gi * 1024 + (h + 1) * 128], 128)[0]
                              for gi in range(3) for h in range(4)])
    sh["wkv_v_t"] = np.stack([tile_w(wkv[:, gi * 1024 + 512: gi * 1024 + 1024], 512)[0] for gi in range(3)])
    sh["knorm_t"] = f(inp["k_norm_g"]).T.copy()
    sh["qnorm_t"] = np.ascontiguousarray(f(inp["q_norm_g"]).transpose(0, 2, 1))
    wq, wo = f(inp["attn_w_q"]), f(inp["attn_w_o"])
    sh["attn_wq_t"] = np.stack([tile_w(wq[j], 128) for j in range(2)])
    sh["attn_wo_t"] = np.stack([tile_w(wo[j], 128) for j in range(2)])
    sh.update(make_consts())
    return sh


def prep_core(inp, b):
    d = {}
    d["xT"] = np.ascontiguousarray(np.asarray(inp["x"][b], dtype=np.float32).T)
    d["ccol"] = np.ascontiguousarray(np.asarray(inp["c"][b], dtype=np.float32).reshape(16, 128).T)
    d["pos"] = np.ascontiguousarray(np.asarray(inp["positions"][b], dtype=np.int32)[None, :])
    return d


def kernel(**inputs):
    k = build()
    shared = prep_shared(inputs)
    in_maps = []
    for b in range(NCORES):
        m = dict(shared)
        m.update(prep_core(inputs, b))
        in_maps.append(m)
    res = run_bass_kernel_spmd(k.nc, in_maps, core_ids=list(range(NCORES)))
    out = np.stack([np.ascontiguousarray(np.asarray(r["yT"]).T) for r in res.results])
    return out.astype(np.float32)
```

```python
import numpy as np
from contextlib import ExitStack
import concourse.bass as bass
import concourse.mybir as mybir
from concourse.bass_utils import run_bass_kernel_spmd

F32 = mybir.dt.float32
BF16 = mybir.dt.bfloat16
I32 = mybir.dt.int32
AF = mybir.ActivationFunctionType
ALU = mybir.AluOpType
AX = mybir.AxisListType

T = 4096
D = 2048
KC = 16
NCORES = 8
EPS = 1e-6
SQD = float(np.sqrt(2048.0))


class Buf:
    __slots__ = ("name", "lw", "rd", "excl")

    def __init__(self, name="", excl=False):
        self.name = name
        self.lw = None
        self.rd = {}
        self.excl = excl


class Op:
    __slots__ = ("eng", "fn", "deps", "is_dma", "signal", "ticket", "dsem", "dval", "emitted")

    def __init__(self, eng, fn, is_dma):
        self.eng = eng
        self.fn = fn
        self.deps = []
        self.is_dma = is_dma
        self.signal = False
        self.ticket = None
        self.dsem = None
        self.dval = None
        self.emitted = False


class Prog:
    COMPUTE = ("pe", "act", "dve", "pool")
    ALLENG = ("pe", "act", "dve", "pool", "sp")

    def __init__(self, nc, n_dma_sems=32):
        self.nc = nc
        self.ops = []
        self.start = 0
        self.n_dma_sems = n_dma_sems
        self.eng_obj = {"pe": nc.tensor, "act": nc.scalar, "dve": nc.vector,
                        "pool": nc.gpsimd, "sp": nc.sync}
        self.sems = {e: nc.alloc_semaphore("s_" + e) for e in self.COMPUTE}
        self.cnt = {e: 0 for e in self.COMPUTE}
        self.dma_sems = [nc.alloc_semaphore("s_dma%d" % i) for i in range(n_dma_sems)]
        self.dma_val = [0] * n_dma_sems
        self.dma_rr = 0
        self.waited = {e: {} for e in self.ALLENG}
        self.n_inst = 0

    def op(self, eng, fn, reads=(), writes=(), dma=False):
        o = Op(eng, fn, dma)
        idx = len(self.ops)
        deps = {}
        ex = [b for b in reads if b.excl]
        if ex:
            writes = list(writes) + [b for b in ex if b not in writes]
            reads = [b for b in reads if not b.excl]
            for b in ex:
                if b.lw is not None:
                    deps[b.lw] = True
        for b in reads:
            if b.lw is not None:
                deps[b.lw] = True
        for b in writes:
            if b.lw is not None:
                deps.setdefault(b.lw, False)
            for r in b.rd.values():
                deps.setdefault(r, False)
        key = ("dma", idx) if dma else eng
        for b in reads:
            b.rd[key] = idx
        for b in writes:
            b.lw = idx
            b.rd = {}
        deps.pop(idx, None)
        o.deps = sorted(deps.items())
        self.ops.append(o)
        return idx

    def dma(self, q, out, in_, reads=(), writes=()):
        qe = self.eng_obj[q]
        return self.op(q, lambda: qe.dma_start(out=out, in_=in_), reads, writes, dma=True)

    def _wait(self, eng, key, sem, val):
        w = self.waited[eng]
        if w.get(key, 0) >= val:
            return
        self.eng_obj[eng].wait_ge(sem, val)
        self.n_inst += 1
        w[key] = val

    def flush(self, barrier=True):
        ops = self.ops
        new = ops[self.start:]
        for o in new:
            for d, raw in o.deps:
                p = ops[d]
                if p.is_dma or p.emitted:
                    continue
                if p.eng == o.eng and not raw and not o.is_dma:
                    continue
                p.signal = True
        last = {}
        for o in new:
            if not o.is_dma:
                last[o.eng] = o
        for o in last.values():
            o.signal = True
        pending_cover = {e: [] for e in self.COMPUTE}
        for o in new:
            e = o.eng
            for d, raw in o.deps:
                p = ops[d]
                if p.is_dma:
                    self._wait(e, ("d", p.dsem), self.dma_sems[p.dsem], p.dval)
                else:
                    if p.eng == e and not raw and not o.is_dma:
                        continue
                    assert p.ticket is not None, (p.eng, e)
                    self._wait(e, p.eng, self.sems[p.eng], p.ticket)
            if o.is_dma:
                i = self.dma_rr
                self.dma_rr = (self.dma_rr + 1) % self.n_dma_sems
                if self.dma_val[i] > 0:
                    self._wait(e, ("d", i), self.dma_sems[i], self.dma_val[i])
                ins = o.fn()
                self.dma_val[i] += 16
                ins.then_inc(self.dma_sems[i], 16)
                o.dsem = i
                o.dval = self.dma_val[i]
            else:
                ins = o.fn()
                if o.signal:
                    self.cnt[e] += 1
                    o.ticket = self.cnt[e]
                    ins.then_inc(self.sems[e], 1)
                    for q in pending_cover[e]:
                        q.ticket = o.ticket
                    pending_cover[e] = []
                else:
                    pending_cover[e].append(o)
            self.n_inst += 1
            o.emitted = True
            o.fn = None
        self.start = len(ops)
        if barrier:
            self.barrier()

    def barrier(self):
        for e in self.ALLENG:
            for f in self.COMPUTE:
                if f != e and self.cnt[f] > 0:
                    self._wait(e, f, self.sems[f], self.cnt[f])
            for i in range(self.n_dma_sems):
                if self.dma_val[i] > 0:
                    self._wait(e, ("d", i), self.dma_sems[i], self.dma_val[i])


class Rot:
    def __init__(self, tiles):
        self.tiles = tiles
        self.k = 0

    def get(self):
        t = self.tiles[self.k % len(self.tiles)]
        self.k += 1
        return t


_uid = [0]


def _nm(name):
    _uid[0] += 1
    return "%s_u%d" % (name, _uid[0])


def sb_rot(nc, es, name, n, shape, dtype):
    tiles = []
    for i in range(n):
        t = es.enter_context(nc.sbuf_tensor(_nm(name), shape, dtype))
        tiles.append((t, Buf("%s%d" % (name, i))))
    return Rot(tiles)


def sb(nc, es, name, shape, dtype):
    return es.enter_context(nc.sbuf_tensor(_nm(name), shape, dtype))


def f_mm(nc, out, lhsT, rhs, start, stop):
    return lambda: nc.tensor.matmul(out, lhsT, rhs, start=start, stop=stop)


def f_tt(eng, out, in0, in1, op):
    return lambda: eng.tensor_tensor(out=out, in0=in0, in1=in1, op=op)


def f_ts(eng, out, in0, s1, s2, op0, op1=None):
    if op1 is None:
        return lambda: eng.tensor_scalar(out=out, in0=in0, scalar1=s1, scalar2=None, op0=op0)
    return lambda: eng.tensor_scalar(out=out, in0=in0, scalar1=s1, scalar2=s2, op0=op0, op1=op1)


def f_stt(eng, out, in0, scalar, in1, op0, op1):
    return lambda: eng.scalar_tensor_tensor(out=out, in0=in0, scalar=scalar, in1=in1, op0=op0, op1=op1)


def f_act(nc, out, in_, func, bias=None, scale=None, accum_out=None):
    kw = {}
    if bias is not None:
        kw["bias"] = bias
    if scale is not None:
        kw["scale"] = scale
    if accum_out is not None:
        kw["accum_out"] = accum_out
    return lambda: nc.scalar.activation(out=out, in_=in_, func=func, **kw)


def f_copy(eng, out, in_):
    return lambda: eng.tensor_copy(out=out, in_=in_)


def f_memset(eng, ap, v):
    return lambda: eng.memset(ap, v)


def bc_t(col2d, n):
    return col2d.unsqueeze(2).to_broadcast([col2d.shape[0], col2d.shape[1], n])


def bc_c(row2d, c):
    return row2d.unsqueeze(1).to_broadcast([row2d.shape[0], c, row2d.shape[1]])


class K:
    def __init__(self, dbg=False):
        self.dbg = dbg
        nc = self.nc = bass.Bass("TRN2", target_bir_lowering=False)
        self.P = Prog(nc)
        self.inputs = {}
        self.psd = [nc.alloc_psum_tensor("psd%d" % i, [128, 1024], F32) for i in range(4)]
        self.psb = [self.psd[i // 2][:, (i % 2) * 512:(i % 2 + 1) * 512] for i in range(8)]
        self.psB = [Buf("psB%d" % i, excl=True) for i in range(8)]
        self.psq = Rot([(self.psb[i // 4][:, (i % 4) * 128:(i % 4 + 1) * 128], self.psB[i // 4])
                        for i in range(32)])
        self.bank_rr = 0

    def din(self, name, shape, dt=F32):
        t = self.nc.dram_tensor(name, list(shape), dt, kind="ExternalInput").ap()
        self.inputs[name] = (tuple(shape), dt)
        return t

    def dscr(self, name, shape, dt):
        isout = self.dbg is True or (isinstance(self.dbg, (set, list, tuple)) and any(name.startswith(n) for n in self.dbg))
        kind = "ExternalOutput" if isout else "Internal"
        return self.nc.dram_tensor(name, list(shape), dt, kind=kind).ap()

    def bank(self, lo=0, hi=8):
        i = lo + (self.bank_rr % (hi - lo))
        self.bank_rr += 1
        return self.psb[i], self.psB[i]

    def declare(self):
        d = self.din
        self.xT = d("xT", [D, T])
        self.ccol = d("ccol", [128, 16])
        self.pos = d("pos", [1, T], I32)
        self.ada_w = d("ada_w_t", [4, 24, 128, 16 * 512])
        self.ada_b = d("ada_b_t", [4, 128, 96])
        self.kvada_w = d("kvada_w_t", [8, 128, 16 * 512])
        self.kvada_b = d("kvada_b_t", [128, 32])
        self.normg = d("normg_t", [4, 2, 128, 16])
        self.kvnormg = d("kvnormg_t", [128, 16])
        self.w1 = d("mlp_w1_t", [4, 64, 128, 16 * 128])
        self.w2 = d("mlp_w2_t", [4, 16, 2, 128, 32 * 128])
        self.win_fm = d("gdn_win_fm", [2, 64, 128, 16 * 128])
        self.wz = d("gdn_wz_t", [2, 8, 128, 16 * 512])
        self.wba = d("gdn_wba_t", [2, 128, 16 * 64])
        self.convw = d("gdn_conv_t", [2, 128, 64 * 4])
        self.alog = d("gdn_alog_b", [2, 128, 32])
        self.dtb = d("gdn_dtb_b", [2, 128, 32])
        self.onorm = d("gdn_onorm_b", [2, 128, 128])
        self.wout = d("gdn_wout_t", [2, 16, 128, 32 * 128])
        self.wkvk = d("wkv_k_t", [12, 128, 16 * 128])
        self.wkvv = d("wkv_v_t", [3, 128, 16 * 512])
        self.knorm = d("knorm_t", [128, 3])
        self.qnorm = d("qnorm_t", [2, 128, 3])
        self.wq = d("attn_wq_t", [2, 48, 128, 16 * 128])
        self.wo = d("attn_wo_t", [2, 16, 128, 16 * 128])
        self.c_f32 = d("c_f32", [128, 10 * 128])
        self.c_bf = d("c_bf", [128, 5 * 128], BF16)
        self.c_invf = d("c_invf", [128, 1])
        self.yT = self.nc.dram_tensor("yT", [D, T], F32, kind="ExternalOutput").ap()
        s = self.dscr
        self.qT_s = s("qT_s", [16, 128, T], BF16)
        self.kT_s = s("kT_s", [16, 128, T], BF16)
        self.ktok_s = s("ktok_s", [16, T, 128], BF16)
        self.vtok_s = s("vtok_s", [32, T, 128], BF16)
        self.ztok_s = s("ztok_s", [T, 4096], BF16)
        self.ogT_s = s("ogT_s", [4096, T], BF16)
        self.KT_s = s("KT_s", [12, 128, T], BF16)
        self.Vb_s = s("Vb_s", [3, 32, 128, 512], BF16)
        self.QT_s = s("QT_s", [48, 128, T], BF16)
        self.aoT_s = s("aoT_s", [2048, T], BF16)

    def load_consts(self, es):
        nc, P = self.nc, self.P
        self.cf = sb(nc, es, "cf", [128, 10 * 128], F32)
        self.cb = sb(nc, es, "cbf", [128, 5 * 128], BF16)
        self.cB = Buf("consts")
        P.dma("sp", self.cf[:], self.c_f32[:, :], writes=[self.cB])
        P.dma("sp", self.cb[:], self.c_bf[:, :], writes=[self.cB])
        cf, cb = self.cf, self.cb
        sl = lambda i: slice(i * 128, (i + 1) * 128)
        self.ident = cf[:, sl(0)]
        self.U = cf[:, sl(1)]
        self.SL = cf[:, sl(2)]
        self.SU = cf[:, sl(3)]
        self.BD32 = cf[:, sl(4)]
        self.OFF1T = cf[:, sl(5)]
        self.OFF2T = cf[:, sl(6)]
        self.ones_f = cf[:, sl(7)]
        self.ident_b = cb[:, sl(0)]
        self.ones_b = cb[:, sl(1)]
        self.rotm = cb[:, sl(2)]
        self.maskPC = cb[:, 3 * 128:5 * 128]
        self.modT = sb(nc, es, "modT", [128, 5, 96], F32)
        self.modB = Buf("modT")
        self.cols = sb(nc, es, "cols", [128, 9, 2, 16], F32)
        self.colsB = Buf("cols")

    def st_mods(self):
        nc, P = self.nc, self.P
        with ExitStack() as es:
            cc = sb(nc, es, "cc", [128, 16], F32)
            ccb = sb(nc, es, "ccb", [128, 16], BF16)
            cB = Buf("cc")
            P.dma("sp", cc[:], self.ccol[:, :], writes=[cB])
            P.op("act", f_act(nc, ccb[:], cc[:], AF.Silu), [cB], [cB])
            wp = sb_rot(nc, es, "mw", 3, [128, 16, 512], BF16)
            ng = sb(nc, es, "ng", [128, 9, 16], F32)
            ab = sb(nc, es, "ab", [128, 5, 96], F32)
            ngB = Buf("ng")
            for l in range(4):
                for j in range(2):
                    P.dma("sp", ng[:, l * 2 + j, :], self.normg[l, j], writes=[ngB])
                P.dma("sp", ab[:, l, :], self.ada_b[l], writes=[ngB])
            P.dma("sp", ng[:, 8, :], self.kvnormg[:, :], writes=[ngB])
            P.dma("sp", ab[:, 4, 0:32], self.kvada_b[:, :], writes=[ngB])
            for l in range(5):
                nblk = 24 if l < 4 else 8
                ps, pb = self.bank()
                for blk in range(nblk):
                    wt, wb = wp.get()
                    src = self.ada_w[l, blk] if l < 4 else self.kvada_w[blk]
                    P.dma("pool", wt[:], src.rearrange("p (c n) -> p c n", c=16), writes=[wb])
                    for j in range(4):
                        col = blk * 4 + j
                        for kc in range(16):
                            P.op("pe", f_mm(nc, ps[:, col:col + 1], wt[:, kc, j * 128:(j + 1) * 128],
                                            ccb[:, kc:kc + 1], kc == 0, kc == 15), [wb, cB], [pb])
                nco = nblk * 4
                P.op("dve", f_tt(nc.vector, self.modT[:, l, 0:nco], ps[:, 0:nco], ab[:, l, 0:nco], ALU.add),
                     [pb, ngB], [self.modB])
            for st in range(9):
                if st < 8:
                    l, j = st // 2, st % 2
                    sh = self.modT[:, l, (0 + 48 * j):(16 + 48 * j)]
                    sc = self.modT[:, l, (16 + 48 * j):(32 + 48 * j)]
                else:
                    sh = self.modT[:, 4, 0:16]
                    sc = self.modT[:, 4, 16:32]
                P.op("dve", f_stt(nc.vector, self.cols[:, st, 0, :], sc, 1.0, ng[:, st, :], ALU.add, ALU.mult),
                     [self.modB, ngB], [self.colsB])
                P.op("dve", f_ts(nc.vector, self.cols[:, st, 0, :], self.cols[:, st, 0, :], SQD, None, ALU.mult),
                     [self.colsB], [self.colsB])
                P.op("dve", f_copy(nc.vector, self.cols[:, st, 1, :], sh), [self.modB], [self.colsB])
            P.flush()

    def rsqrt(self, dst, dstB, src, srcB, c):
        nc, P = self.nc, self.P
        P.op("act", f_act(nc, dst, src, AF.Sqrt, bias=float(c)), [srcB], [dstB])
        P.op("dve", (lambda: nc.vector.reciprocal(out=dst, in_=dst)), [dstB], [dstB])

    def gate(self, l, j):
        return self.modT[:, l, (32 + 48 * j):(48 + 48 * j)]

    def norm_tile(self, pools, x_src, t0, dst, dstB, st, W=512):
        nc, P = self.nc, self.P
        xp, sqp, rp = pools
        xv = x_src.rearrange("(c p) t -> p c t", p=128)
        xt, xb = xp.get()
        P.dma("sp", xt[:], xv[:, :, t0:t0 + W], writes=[xb])
        sq, sqb = sqp.get()
        P.op("act", f_act(nc, sq[:], xt[:], AF.Square), [xb], [sqb])
        ps, pb = self.bank(6, 8)
        for kc in range(16):
            P.op("pe", f_mm(nc, ps[:, 0:W], self.ones_b, sq[:, kc, :], kc == 0, kc == 15), [sqb, self.cB], [pb])
        r, rb = rp.get()
        self.rsqrt(r[:], rb, ps[:, 0:W], pb, 2048.0 * EPS)
        A = self.cols[:, st, 0, :]
        sh = self.cols[:, st, 1, :]
        P.op("dve", f_tt(nc.vector, xt[:], xt[:], bc_t(A, W), ALU.mult), [xb, self.colsB], [xb])
        P.op("pool", f_tt(nc.gpsimd, xt[:], xt[:], bc_c(r[:, :], 16), ALU.mult), [xb, rb], [xb])
        P.op("dve", f_tt(nc.vector, dst, xt[:], bc_t(sh, W), ALU.add), [xb, self.colsB], [dstB])

    def norm_pools(self, es, n=1, W=512):
        nc = self.nc
        return (sb_rot(nc, es, "nx", n, [128, 16, W], F32),
                sb_rot(nc, es, "nsq", n, [128, 16, W], BF16),
                sb_rot(nc, es, "nr", n, [128, W], F32))

    def st_mlp(self, l, x_in, x_out):
        nc, P = self.nc, self.P
        TT = 1024
        with ExitStack() as es:
            pools = self.norm_pools(es, 1)
            hp = sb_rot(nc, es, "mh", 1, [128, 16, TT], BF16)
            hid = sb(nc, es, "hid", [128, 32, TT], BF16)
            hidB = [[Buf("hid") for _ in range(2)] for _ in range(32)]
            w1p = sb_rot(nc, es, "w1", 3, [128, 16, 128], BF16)
            w2p = sb_rot(nc, es, "w2", 2, [128, 32, 128], BF16)
            rl = sb_rot(nc, es, "rl", 3, [128, 512], BF16)
            xo = sb_rot(nc, es, "xo", 3, [128, 512], F32)
            gt = self.gate(l, 1)
            xiv = x_in.rearrange("(c p) t -> p c t", p=128)
            xov = x_out.rearrange("(c p) t -> p c t", p=128)
            unit = 0
            for tb in range(T // TT):
                h, hB0 = hp.get()
                hB = [Buf("mhs") for _ in range(2)]
                for s in range(2):
                    self.norm_tile(pools, x_in, tb * TT + s * 512, h[:, :, s * 512:(s + 1) * 512], hB[s], l * 2 + 1)
                for half in range(2):
                    for fc in range(32):
                        wt, wb = w1p.get()
                        P.dma("pool", wt[:], self.w1[l, half * 32 + fc].rearrange("p (c n) -> p c n", c=16), writes=[wb])
                        for s in range(2):
                            ps, pb = self.bank(0, 4)
                            for kc in range(16):
                                P.op("pe", f_mm(nc, ps[:], wt[:, kc, :], h[:, kc, s * 512:(s + 1) * 512], kc == 0, kc == 15),
                                     [wb, hB[s]], [pb])
                            dst = hid[:, fc, s * 512:(s + 1) * 512]
                            r, rb = rl.get()
                            P.op("act", f_act(nc, r[:], ps[:], AF.Relu), [pb], [rb])
                            if unit % 2 == 0:
                                P.op("dve", f_tt(nc.vector, dst, r[:], r[:], ALU.mult), [rb], [hidB[fc][s]])
                            else:
                                P.op("pool", f_tt(nc.gpsimd, dst, r[:], r[:], ALU.mult), [rb], [hidB[fc][s]])
                            unit += 1
                    xsrc = xiv if half == 0 else xov
                    if half == 0:
                        xtok = [[Buf("xtok") for _ in range(2)] for _ in range(16)]
                    for fo in range(16):
                        wt, wb = w2p.get()
                        P.dma("pool", wt[:], self.w2[l, fo, half].rearrange("p (c n) -> p c n", c=32), writes=[wb])
                        for s in range(2):
                            t0 = tb * TT + s * 512
                            xt, xb = xo.get()
                            P.dma("sp", xt[:], xsrc[:, fo, t0:t0 + 512], reads=[xtok[fo][s]], writes=[xb])
                            ps, pb = self.bank(4, 6)
                            for kc in range(32):
                                P.op("pe", f_mm(nc, ps[:], wt[:, kc, :], hid[:, kc, s * 512:(s + 1) * 512], kc == 0, kc == 31),
                                     [wb, hidB[kc][s]], [pb])
                            P.op("dve", f_stt(nc.vector, xt[:], ps[:], gt[:, fo:fo + 1], xt[:], ALU.mult, ALU.add),
                                 [pb, xb, self.modB], [xb])
                            P.dma("sp", xov[:, fo, t0:t0 + 512], xt[:], reads=[xb], writes=[xtok[fo][s]])
            P.flush()

    def st_proj_resid(self, actT, kcn, w_t, gate, x_in, x_out):
        nc, P = self.nc, self.P
        TT = 1024
        with ExitStack() as es:
            ap_ = sb_rot(nc, es, "pa", 1, [128, kcn, TT], BF16)
            wp = sb_rot(nc, es, "pw", 3, [128, kcn, 128], BF16)
            xo = sb_rot(nc, es, "px", 4, [128, 512], F32)
            av = actT.rearrange("(c p) t -> p c t", p=128)
            xiv = x_in.rearrange("(c p) t -> p c t", p=128)
            xov = x_out.rearrange("(c p) t -> p c t", p=128)
            for tb in range(T // TT):
                a, aB = ap_.get()
                for c0 in range(0, kcn, 8):
                    P.dma("sp", a[:, c0:c0 + 8, :], av[:, c0:c0 + 8, tb * TT:(tb + 1) * TT], writes=[aB])
                for fo in range(16):
                    wt, wb = wp.get()
                    P.dma("pool", wt[:], w_t[fo].rearrange("p (c n) -> p c n", c=kcn), writes=[wb])
                    for s in range(2):
                        t0 = tb * TT + s * 512
                        xt, xb = xo.get()
                        P.dma("sp", xt[:], xiv[:, fo, t0:t0 + 512], writes=[xb])
                        ps, pb = self.bank(0, 4)
                        for kc in range(kcn):
                            P.op("pe", f_mm(nc, ps[:], wt[:, kc, :], a[:, kc, s * 512:(s + 1) * 512], kc == 0, kc == kcn - 1),
                                 [wb, aB], [pb])
                        P.op("dve", f_stt(nc.vector, xt[:], ps[:], gate[:, fo:fo + 1], xt[:], ALU.mult, ALU.add),
                             [pb, xb, self.modB], [xb])
                        P.dma("sp", xov[:, fo, t0:t0 + 512], xt[:], reads=[xb])
            P.flush()

    def st_gdn_proj(self, l, x_in, bg, bgB):
        nc, P = self.nc, self.P
        with ExitStack() as es:
            hT = sb(nc, es, "hT", [128, 16, T], BF16)
            hB = [Buf("hT%d" % i) for i in range(8)]
            with ExitStack() as es2:
                pools = self.norm_pools(es2, 1)
                for tt in range(8):
                    self.norm_tile(pools, x_in, tt * 512, hT[:, :, tt * 512:(tt + 1) * 512], hB[tt], l * 2)
                P.flush()
            with ExitStack() as es2:
                wba = sb(nc, es2, "wba", [128, 16, 64], BF16)
                wbB = Buf("wba")
                P.dma("pool", wba[:], self.wba[l].rearrange("p (c n) -> p c n", c=16), writes=[wbB])
                prm = sb(nc, es2, "prm", [128, 2, 32], F32)
                prB = Buf("prm")
                P.dma("sp", prm[:, 0, :], self.alog[l], writes=[prB])
                P.dma("sp", prm[:, 1, :], self.dtb[l], writes=[prB])
                tmp = sb(nc, es2, "bgtmp", [128, 32, 32], F32)
                tB = Buf("bgtmp")
                for ts in range(32):
                    ps, pb = self.bank(0, 4)
                    for kc in range(16):
                        P.op("pe", f_mm(nc, ps[:, 0:64], hT[:, kc, ts * 128:(ts + 1) * 128], wba[:, kc, :], kc == 0, kc == 15),
                             [hB[ts // 4], wbB], [pb])
                    P.op("act", (lambda ps=ps, ts=ts: nc.scalar.copy(out=bg[:, ts, :], in_=ps[:, 0:64])), [pb], [bgB])
                P.op("act", f_act(nc, bg[:, :, 0:32], bg[:, :, 0:32], AF.Sigmoid), [bgB], [bgB])
                P.op("dve", f_tt(nc.vector, tmp[:], bg[:, :, 32:64], bc_c(prm[:, 1, :], 32), ALU.add), [bgB, prB], [tB])
                P.op("act", f_act(nc, tmp[:], tmp[:], AF.Exp), [tB], [tB])
                P.op("act", f_act(nc, tmp[:], tmp[:], AF.Ln, bias=1.0), [tB], [tB])
                P.op("act", f_act(nc, prm[:, 0, :], prm[:, 0, :], AF.Exp), [prB], [prB])
                P.op("dve", f_stt(nc.vector, bg[:, :, 32:64], tmp[:], -1.0, bc_c(prm[:, 0, :], 32), ALU.mult, ALU.mult),
                     [tB, prB], [bgB])
                P.flush()
            with ExitStack() as es2:
                wzp = sb_rot(nc, es2, "wz", 2, [128, 16, 512], BF16)
                zo = sb_rot(nc, es2, "zo", 4, [128, 512], BF16)
                for zb in range(8):
                    wt, wb = wzp.get()
                    P.dma("pool", wt[:], self.wz[l, zb].rearrange("p (c n) -> p c n", c=16), writes=[wb])
                    for ts in range(32):
                        ps, pb = self.bank(0, 4)
                        for kc in range(16):
                            P.op("pe", f_mm(nc, ps[:], hT[:, kc, ts * 128:(ts + 1) * 128], wt[:, kc, :], kc == 0, kc == 15),
                                 [hB[ts // 4], wb], [pb])
                        z, zB = zo.get()
                        P.op("act", f_act(nc, z[:], ps[:], AF.Silu), [pb], [zB])
                        P.dma("sp", self.ztok_s[ts * 128:(ts + 1) * 128, zb * 512:(zb + 1) * 512], z[:], reads=[zB])
                P.flush()
            with ExitStack() as es2:
                cw = sb(nc, es2, "cw", [128, 64, 4], F32)
                cwB = Buf("cw")
                P.dma("sp", cw[:], self.convw[l].rearrange("p (c j) -> p c j", j=4), writes=[cwB])
                wp = sb_rot(nc, es2, "wi", 3, [128, 16, 128], BF16)
                pbuf = sb_rot(nc, es2, "pbuf", 3, [128, 515], F32)
                accp = sb_rot(nc, es2, "acc", 3, [128, 512], F32)
                svp = sb_rot(nc, es2, "sv", 3, [128, 512], F32)
                sqp = sb_rot(nc, es2, "sq", 3, [128, 512], BF16)
                rsp = sb_rot(nc, es2, "rs", 3, [128, 512], F32)
                qnp = sb_rot(nc, es2, "qn", 3, [128, 512], BF16)
                svbp = sb_rot(nc, es2, "svb", 3, [128, 512], BF16)
                tkp = sb_rot(nc, es2, "tk", 3, [128, 4, 128], BF16)
                ctp = None
                unit = 0
                for fc in range(64):
                    wt, wb = wp.get()
                    P.dma("pool", wt[:], self.win_fm[l, fc].rearrange("p (c n) -> p c n", c=16), writes=[wb])
                    prev = None
                    for tt in range(8):
                        ps, pb = self.bank(0, 4)
                        for kc in range(16):
                            P.op("pe", f_mm(nc, ps[:], wt[:, kc, :], hT[:, kc, tt * 512:(tt + 1) * 512], kc == 0, kc == 15),
                                 [wb, hB[tt]], [pb])
                        pbt, pbB = pbuf.get()
                        P.op("act", (lambda pbt=pbt, ps=ps: nc.scalar.copy(out=pbt[:, 3:515], in_=ps[:])), [pb], [pbB])
                        if prev is None:
                            P.op("pool", f_memset(nc.gpsimd, pbt[:, 0:3], 0.0), [], [pbB])
                        else:
                            P.op("pool", f_copy(nc.gpsimd, pbt[:, 0:3], prev[0][:, 512:515]), [prev[1]], [pbB])
                        prev = (pbt, pbB)
                        eng, en = (nc.vector, "dve")
                        unit += 1
                        acc, aB = accp.get()
                        P.op(en, f_ts(eng, acc[:], pbt[:, 0:512], cw[:, fc, 0:1], None, ALU.mult), [pbB, cwB], [aB])
                        for j in range(1, 4):
                            if en == "dve":
                                P.op(en, f_stt(eng, acc[:], pbt[:, j:j + 512], cw[:, fc, j:j + 1], acc[:], ALU.mult, ALU.add),
                                     [pbB, cwB, aB], [aB])
                            else:
                                ctmp, ctB = ctp.get()
                                P.op(en, f_ts(eng, ctmp[:], pbt[:, j:j + 512], cw[:, fc, j:j + 1], None, ALU.mult), [pbB, cwB], [ctB])
                                P.op(en, f_tt(eng, acc[:], acc[:], ctmp[:], ALU.add), [ctB, aB], [aB])
                        sv, sB = svp.get() if fc < 32 else svbp.get()
                        P.op("act", f_act(nc, sv[:], acc[:], AF.Silu), [aB], [sB])
                        if fc < 32:
                            sq, sqB = sqp.get()
                            isq = fc < 16
                            P.op("act", f_act(nc, sq[:], sv[:], AF.Square, scale=(float(np.sqrt(128.0)) if isq else 1.0)), [sB], [sqB])
                            ps2, pb2 = self.bank(4, 6)
                            P.op("pe", f_mm(nc, ps2[:], self.ones_b, sq[:], True, True), [sqB, self.cB], [pb2])
                            rs, rB = rsp.get()
                            self.rsqrt(rs[:], rB, ps2[:], pb2, (128.0 * EPS if isq else EPS))
                            qn, qB = qnp.get()
                            if fc < 16:
                                P.op("pool", f_tt(nc.gpsimd, qn[:], sv[:], rs[:], ALU.mult), [sB, rB], [qB])
                                P.dma("sp", self.qT_s[fc, :, tt * 512:(tt + 1) * 512], qn[:], reads=[qB])
                                src = None
                            else:
                                P.op("pool", f_tt(nc.gpsimd, qn[:], sv[:], rs[:], ALU.mult), [sB, rB], [qB])
                                P.dma("sp", self.kT_s[fc - 16, :, tt * 512:(tt + 1) * 512], qn[:], reads=[qB])
                                src, srcB = qn, qB
                                dstd = self.ktok_s[fc - 16]
                        else:
                            src, srcB = sv, sB
                            dstd = self.vtok_s[fc - 32]
                        if src is not None:
                            ps3, pb3 = self.bank(6, 8)
                            for j in range(4):
                                P.op("pe", f_mm(nc, ps3[:, j * 128:(j + 1) * 128], src[:, j * 128:(j + 1) * 128], self.ident_b, True, True),
                                     [srcB, self.cB], [pb3])
                            tk, tB = tkp.get()
                            P.op("act", (lambda tk=tk, ps3=ps3: nc.scalar.copy(out=tk[:].rearrange("p j d -> p (j d)"), in_=ps3[:])),
                                 [pb3], [tB])
                            P.dma("sp", dstd[tt * 512:(tt + 1) * 512, :].rearrange("(j p) d -> p j d", p=128), tk[:], reads=[tB])
                P.flush()
            if self.dbg:
                dbg_bg = self.dscr(_nm("dbg_bg"), [128, 32 * 64], F32)
                P.dma("sp", dbg_bg, bg[:].rearrange("p a b -> p (a b)"), reads=[bgB])
                dbg_mod = self.dscr(_nm("dbg_mod"), [128, 5 * 96], F32)
                P.dma("sp", dbg_mod, self.modT[:].rearrange("p a b -> p (a b)"), reads=[self.modB])
                P.flush()

    def st_gdn_chunk(self, l, bg, bgB, n_chunks=32):
        nc, P = self.nc, self.P
        V, G = nc.vector, nc.gpsimd
        W = 8
        SH = [128, W, 128]
        with ExitStack() as es:
            kT = sb(nc, es, "ckT", [128, 16, 128], BF16)
            qT = sb(nc, es, "cqT", [128, 16, 128], BF16)
            ktok = sb(nc, es, "cktok", [128, 16, 128], BF16)
            vtok = sb(nc, es, "cvtok", [128, 32, 128], BF16)
            zt = sb(nc, es, "czt", [128, 32, 128], BF16)
            kTB, qTB, ktB, vtB, ztB = Buf("kT"), Buf("qT"), Buf("ktok"), Buf("vtok"), Buf("zt")
            S = sb(nc, es, "S", [128, 32, 128], F32)
            Sb = sb(nc, es, "Sb", [128, 32, 128], BF16)
            SB = [Buf("S%d" % w) for w in range(4)]
            SbB = [Buf("Sb%d" % w) for w in range(4)]
            ogT = sb(nc, es, "ogT", [128, 32, 128], BF16)
            ogB = Buf("ogT")
            eAll = sb(nc, es, "eAll", [128, 96], F32)
            eB = Buf("eAll")
            onr = sb(nc, es, "onr", [128, 128], F32)
            onB = Buf("onr")
            P.dma("sp", onr[:], self.onorm[l], writes=[onB])
            P.op("dve", f_ts(V, onr[:], onr[:], float(np.sqrt(128.0)), None, ALU.mult), [onB], [onB])
            P.op("pool", f_memset(G, S[:], 0.0), [], SB)
            P.op("pool", f_memset(G, Sb[:], 0.0), [], SbB)
            cB = self.cB
            I_, U_, SL_, SU_, BD, O1T, O2T = self.ident, self.U, self.SL, self.SU, self.BD32, self.OFF1T, self.OFF2T
            Ib = self.ident_b

            def mk_slot(i):
                d = {}
                for nm in ("E", "DTs", "DTi", "P0", "P1", "kq", "Gm"):
                    d[nm] = (sb(nc, es, "s%d%s" % (i, nm), SH, F32), Buf(nm))
                for nm in ("A", "AT", "B0", "BT0", "B1", "BT1", "Ao1T", "Ao2T", "iT", "Pb0", "Pb1", "keg", "kd", "vn", "og"):
                    d[nm] = (sb(nc, es, "s%d%s" % (i, nm), SH, BF16), Buf(nm))
                d["ss"] = (sb(nc, es, "s%dss" % i, [128, 2, W], F32), Buf("ss"))
                return d

            slots = [mk_slot(0), mk_slot(1)]
            pd_rr = [0]
            pd_gen = [0, 0, 0, 0]

            def pd():
                i = pd_rr[0] % 4
                pd_rr[0] += 1
                pd_gen[i] += 1
                t = self.psd[i][:, :].rearrange("p (h d) -> p h d", h=W)
                return (t, [self.psB[2 * i], self.psB[2 * i + 1]], i, pd_gen[i])

            def chk(pt):
                assert pd_gen[pt[2]] == pt[3], "PSUM tile reused before its reader was emitted"
                return pt[0]

            def mm(pt, h, lhsT, rhs, rd, start=True, stop=True):
                P.op("pe", f_mm(nc, chk(pt)[:, h, :], lhsT, rhs, start, stop), rd, pt[1])

            def bc8(c):
                return bc_c(c, W)

            def pair(ap3):
                return ap3.rearrange("p (j t) d -> p j t d", t=2)

            def wave_gen(sl, n, w0):
                wv = w0 // W
                hk0 = w0 // 2
                T_ = lambda nm: sl[nm][0]
                B_ = lambda nm: sl[nm][1]
                beta = bc_t(bg[:, n, w0:w0 + W], 128)
                eG = bc_t(eAll[:, w0:w0 + W], 128)
                eGl = bc_t(eAll[:, 32 + w0:32 + w0 + W], 128)
                eGt = bc_t(eAll[:, 64 + w0:64 + w0 + W], 128)
                P.op("pool", f_tt(G, T_("Gm")[:], bc_t(bg[:, n, 32 + w0:32 + w0 + W], 128), bc8(SL_), ALU.mult), [bgB, cB], [B_("Gm")])
                pkq = pd()
                for j in range(4):
                    hk = hk0 + j
                    mm(pkq, 2 * j, kT[:, hk, :], kT[:, hk, :], [kTB])
                    mm(pkq, 2 * j + 1, kT[:, hk, :], qT[:, hk, :], [kTB, qTB])
                pdp = pd()
                for h in range(W):
                    mm(pdp, h, T_("Gm")[:, h, :], U_, [B_("Gm"), cB])
                yield
                P.op("dve", f_copy(V, T_("kq")[:], chk(pkq)), pkq[1], [B_("kq")])
                P.op("act", f_act(nc, T_("E")[:], chk(pdp), AF.Exp), pdp[1], [B_("E")])
                yield
                P.op("pool", f_tt(G, T_("DTs")[:], T_("E")[:], bc8(SU_), ALU.mult), [B_("E"), cB], [B_("DTs")])
                P.op("pool", f_tt(G, T_("DTi")[:], T_("E")[:], bc8(U_), ALU.mult), [B_("E"), cB], [B_("DTi")])
                kqv = pair(T_("kq")[:])
                KKb = kqv[:, :, 0, :].unsqueeze(2).to_broadcast([128, 4, 2, 128])
                QKb = kqv[:, :, 1, :].unsqueeze(2).to_broadcast([128, 4, 2, 128])
                P.op("dve", f_tt(V, pair(T_("DTs")[:]), pair(T_("DTs")[:]), KKb, ALU.mult), [B_("DTs"), B_("kq")], [B_("DTs")])
                P.op("dve", f_tt(V, T_("A")[:], T_("DTs")[:], beta, ALU.mult), [B_("DTs"), bgB], [B_("A")])
                P.op("dve", f_tt(V, pair(T_("iT")[:]), pair(T_("DTi")[:]), QKb, ALU.mult), [B_("DTi"), B_("kq")], [B_("iT")])
                yield
                pa = pd()
                for h in range(W):
                    mm(pa, h, T_("A")[:, h, :], Ib, [B_("A"), cB])
                yield
                P.op("act", (lambda d=T_("AT"), s_=chk(pa): nc.scalar.copy(out=d[:], in_=s_)), pa[1], [B_("AT")])
                P.op("pool", f_tt(G, T_("B0")[:], T_("A")[:], bc8(BD), ALU.mult), [B_("A"), cB], [B_("B0")])
                yield
                P.op("pool", f_tt(G, T_("BT0")[:], T_("AT")[:], bc8(BD), ALU.mult), [B_("AT"), cB], [B_("BT0")])
                P.op("pool", f_tt(G, T_("Ao1T")[:], T_("AT")[:], bc8(O1T), ALU.mult), [B_("AT"), cB], [B_("Ao1T")])
                P.op("dve", f_tt(V, T_("Ao2T")[:], T_("AT")[:], bc8(O2T), ALU.mult), [B_("AT"), cB], [B_("Ao2T")])
                P.op("dve", f_tt(V, T_("P0")[:], bc8(I_), T_("B0")[:], ALU.subtract), [B_("B0"), cB], [B_("P0")])
                P.op("act", (lambda d=T_("Pb0"), s_=T_("P0"): nc.scalar.copy(out=d[:], in_=s_[:])), [B_("P0")], [B_("Pb0")])
                yield
                cur, nxt = 0, 1
                for step in range(4):
                    last = (step == 3)
                    Bc, BTc = "B%d" % cur, "BT%d" % cur
                    Bn, BTn = "B%d" % nxt, "BT%d" % nxt
                    Pc, Pbc = "P%d" % cur, "Pb%d" % cur
                    Pn, Pbn = "P%d" % nxt, "Pb%d" % nxt
                    if not last:
                        p2 = pd()
                        for h in range(W):
                            mm(p2, h, T_(BTc)[:, h, :], T_(Bc)[:, h, :], [B_(BTc), B_(Bc)])
                    p2t = pd()
                    for h in range(W):
                        mm(p2t, h, T_(Bc)[:, h, :], T_(BTc)[:, h, :], [B_(BTc), B_(Bc)])
                    yield
                    if not last:
                        P.op("act", (lambda d=T_(Bn), s_=chk(p2): nc.scalar.copy(out=d[:], in_=s_)), p2[1], [B_(Bn)])
                    P.op("dve", f_copy(V, T_(BTn)[:], chk(p2t)), p2t[1], [B_(BTn)])
                    yield
                    pp = pd()
                    for h in range(W):
                        mm(pp, h, T_(BTn)[:, h, :], T_(Pbc)[:, h, :], [B_(BTn), B_(Pbc)])
                    yield
                    P.op("dve", f_tt(V, T_(Pn)[:], chk(pp), T_(Pc)[:], ALU.add), pp[1] + [B_(Pc)], [B_(Pn)])
                    P.op("pool", f_copy(G, T_(Pbn)[:], T_(Pn)[:]), [B_(Pn)], [B_(Pbn)])
                    cur, nxt = nxt, cur
                    yield
                for AoT in ("Ao1T", "Ao2T"):
                    Pc, Pbc = "P%d" % cur, "Pb%d" % cur
                    Pn, Pbn = "P%d" % nxt, "Pb%d" % nxt
                    px = pd()
                    for h in range(W):
                        mm(px, h, T_(AoT)[:, h, :], T_(Pbc)[:, h, :], [B_(AoT), B_(Pbc)])
                    pt_ = pd()
                    for h in range(W):
                        mm(pt_, h, T_(Pbc)[:, h, :], Ib, [B_(Pbc), cB])
                    yield
                    P.op("act", (lambda d=T_("A"), s_=chk(px): nc.scalar.copy(out=d[:], in_=s_)), px[1], [B_("A")])
                    P.op("dve", f_copy(V, T_("AT")[:], chk(pt_)), pt_[1], [B_("AT")])
                    yield
                    py = pd()
                    for h in range(W):
                        mm(py, h, T_("AT")[:, h, :], T_("A")[:, h, :], [B_("AT"), B_("A")])
                    yield
                    P.op("dve", f_tt(V, T_(Pn)[:], T_(Pc)[:], chk(py), ALU.subtract), py[1] + [B_(Pc)], [B_(Pn)])
                    P.op("pool", f_copy(G, T_(Pbn)[:], T_(Pn)[:]), [B_(Pn)], [B_(Pbn)])
                    cur, nxt = nxt, cur
                    yield
                TTb = "Pb%d" % cur
                ktb = ktok[:, hk0:hk0 + 4, :].unsqueeze(2).to_broadcast([128, 4, 2, 128])
                P.op("pool", f_tt(G, pair(T_("keg")[:]), ktb, pair(eG), ALU.mult), [ktB, eB], [B_("keg")])
                P.op("pool", f_tt(G, pair(T_("kd")[:]), ktb, pair(eGl), ALU.mult), [ktB, eB], [B_("kd")])
                yield
                pw = pd()
                for h in range(W):
                    mm(pw, h, T_("keg")[:, h, :], T_(TTb)[:, h, :], [B_("keg"), B_(TTb)])
                yield
                P.op("act", (lambda d=T_("A"), s_=chk(pw): nc.scalar.mul(out=d[:], in_=s_, mul=-1.0)), pw[1], [B_("A")])
                yield
                pv = pd()
                for h in range(W):
                    hv = w0 + h
                    mm(pv, h, T_(TTb)[:, h, :], vtok[:, hv, :], [B_(TTb), vtB], True, False)
                    mm(pv, h, T_("A")[:, h, :], Sb[:, hv, :], [B_("A"), SbB[wv]], False, True)
                po1 = pd()
                for h in range(W):
                    hv = w0 + h
                    mm(po1, h, qT[:, hk0 + h // 2, :], Sb[:, hv, :], [qTB, SbB[wv]])
                yield
                P.op("dve", f_tt(V, T_("vn")[:], chk(pv), beta, ALU.mult), pv[1] + [bgB], [B_("vn")])
                P.op("dve", f_tt(V, T_("E")[:], chk(po1), eG, ALU.mult), po1[1] + [eB], [B_("E")])
                yield
                po2 = pd()
                for h in range(W):
                    mm(po2, h, T_("iT")[:, h, :], T_("vn")[:, h, :], [B_("iT"), B_("vn")])
                psu = pd()
                for h in range(W):
                    mm(psu, h, T_("kd")[:, h, :], T_("vn")[:, h, :], [B_("kd"), B_("vn")])
                yield
                P.op("dve", f_tt(V, T_("DTs")[:], chk(po2), T_("E")[:], ALU.add), po2[1] + [B_("E")], [B_("DTs")])
                Sw = S[:, w0:w0 + W, :]
                P.op("pool", f_tt(G, Sw, Sw, eGt, ALU.mult), [SB[wv], eB], [SB[wv]])
                P.op("dve", f_tt(V, Sw, Sw, chk(psu), ALU.add), psu[1] + [SB[wv]], [SB[wv]])
                P.op("act", (lambda d=Sb[:, w0:w0 + W, :], s_=Sw: nc.scalar.copy(out=d, in_=s_)), [SB[wv]], [SbB[wv]])
                yield
                ss = T_("ss")
                P.op("pool", f_tt(G, T_("DTi")[:], T_("DTs")[:], T_("DTs")[:], ALU.mult), [B_("DTs")], [B_("DTi")])
                P.op("dve", (lambda d=ss[:, 0, :], s_=T_("DTi"): nc.vector.reduce_sum(out=d, in_=s_[:], axis=AX.X)), [B_("DTi")], [B_("ss")])
                self.rsqrt(ss[:, 1, :], B_("ss"), ss[:, 0, :], B_("ss"), 128.0 * EPS)
                yield
                P.op("dve", f_tt(V, T_("DTs")[:], T_("DTs")[:], bc_t(ss[:, 1, :], 128), ALU.mult), [B_("DTs"), B_("ss")], [B_("DTs")])
                P.op("pool", f_tt(G, T_("DTs")[:], T_("DTs")[:], bc8(onr[:, :]), ALU.mult), [B_("DTs"), onB], [B_("DTs")])
                P.op("dve", f_tt(V, T_("og")[:], T_("DTs")[:], zt[:, w0:w0 + W, :], ALU.mult), [B_("DTs"), ztB], [B_("og")])
                yield
                pg = pd()
                for h in range(W):
                    mm(pg, h, T_("og")[:, h, :], Ib, [B_("og"), cB])
                yield
                P.op("act", (lambda d=ogT[:, w0:w0 + W, :], s_=chk(pg): nc.scalar.copy(out=d, in_=s_)), pg[1], [ogB])

            for n in range(n_chunks):
                t0 = n * 128
                P.dma("sp", kT[:], self.kT_s[:, :, t0:t0 + 128].rearrange("h d t -> d h t"), writes=[kTB])
                P.dma("sp", qT[:], self.qT_s[:, :, t0:t0 + 128].rearrange("h d t -> d h t"), writes=[qTB])
                P.dma("sp", ktok[:], self.ktok_s[:, t0:t0 + 128, :].rearrange("h t d -> t h d"), writes=[ktB])
                P.dma("sp", vtok[:], self.vtok_s[:, t0:t0 + 128, :].rearrange("h t d -> t h d"), writes=[vtB])
                P.dma("sp", zt[:].rearrange("t h d -> t (h d)"), self.ztok_s[t0:t0 + 128, :], writes=[ztB])
                g_n = bg[:, n, 32:64]
                eps_, epB = self.psq.get()
                P.op("pe", f_mm(nc, eps_[:, 0:32], U_, g_n, True, True), [bgB, cB], [epB])
                P.op("pe", f_mm(nc, eps_[:, 32:64], SL_, g_n, True, True), [bgB, cB], [epB])
                P.op("pe", f_mm(nc, eps_[:, 64:96], self.ones_f, g_n, True, True), [bgB, cB], [epB])
                P.op("act", f_act(nc, eAll[:], eps_[:, 0:96], AF.Exp), [epB], [eB])
                for w0 in (0, 16):
                    alive = [wave_gen(slots[0], n, w0), wave_gen(slots[1], n, w0 + W)]
                    while alive:
                        nxt_ = []
                        for g in alive:
                            try:
                                next(g)
                                nxt_.append(g)
                            except StopIteration:
                                pass
                        alive = nxt_
                P.dma("sp", self.ogT_s[:, t0:t0 + 128].rearrange("(h p) t -> p h t", p=128), ogT[:], reads=[ogB])
            P.flush()

    def rot_tables(self, es):
        nc, P = self.nc, self.P
        V = nc.vector
        C32 = sb(nc, es, "C32", [32, T], F32)
        S32 = sb(nc, es, "S32", [32, T], F32)
        rB = Buf("rot")
        with ExitStack() as es2:
            posi = sb(nc, es2, "posi", [32, T], I32)
            ang = sb(nc, es2, "ang", [32, T], F32)
            tmp = sb(nc, es2, "rtmp", [32, T], F32)
            ang2 = sb(nc, es2, "ang2", [32, T], F32)
            ivf = sb(nc, es2, "ivf", [32, 1], F32)
            pB = Buf("posi")
            src = bass.AP(self.pos.tensor, 0, [[0, 32], [1, T]])
            P.dma("sp", posi[:], src, writes=[pB])
            P.dma("sp", ivf[:], self.c_invf[0:32, :], writes=[pB])
            P.op("dve", f_copy(V, ang[:], posi[:]), [pB], [pB])
            P.op("dve", f_ts(V, ang[:], ang[:], ivf[:, 0:1], None, ALU.mult), [pB], [pB])
            pi = float(np.pi)
            for (dst, off) in ((S32, 0.0), (C32, 0.5 * pi)):
                P.op("dve", f_ts(V, ang2[:], ang[:], off, None, ALU.add), [pB], [pB])
                P.op("dve", f_ts(V, tmp[:], ang2[:], 1.0 / (2.0 * pi), None, ALU.mult), [pB], [pB])
                P.op("dve", f_copy(V, posi[:], tmp[:]), [pB], [pB])
                P.op("dve", f_copy(V, tmp[:], posi[:]), [pB], [pB])
                P.op("dve", f_stt(V, ang2[:], tmp[:], -2.0 * pi, ang2[:], ALU.mult, ALU.add), [pB], [pB])
                P.op("dve", f_ts(V, tmp[:], ang2[:], pi, 2.0 * pi, ALU.is_gt, ALU.mult), [pB], [pB])
                P.op("dve", f_tt(V, ang2[:], ang2[:], tmp[:], ALU.subtract), [pB], [pB])
                P.op("dve", f_ts(V, tmp[:], ang2[:], -pi, 2.0 * pi, ALU.is_lt, ALU.mult), [pB], [pB])
                P.op("dve", f_tt(V, ang2[:], ang2[:], tmp[:], ALU.add), [pB], [pB])
                P.op("act", f_act(nc, dst[:], ang2[:], AF.Sin), [pB], [rB, pB])
            P.flush()
        return C32, S32, rB

    def qk_pools(self, es):
        nc = self.nc
        return dict(sv=sb_rot(nc, es, "qsv", 2, [128, 512], F32), sq=sb_rot(nc, es, "qsq", 2, [128, 512], BF16),
                    rs=sb_rot(nc, es, "qrs", 2, [128, 512], F32), qb=sb_rot(nc, es, "qqb", 2, [128, 512], BF16),
                    t1=sb_rot(nc, es, "qt1", 2, [32, 512], F32), t2=sb_rot(nc, es, "qt2", 2, [32, 512], F32),
                    o=sb_rot(nc, es, "qo", 2, [128, 512], BF16))

    def qk_post(self, pl, ps, pb, gcol, gB, rot, t0, dst):
        nc, P = self.nc, self.P
        V, G = nc.vector, nc.gpsimd
        C32, S32, rB = rot
        sv, svB = pl["sv"].get()
        P.op("act", (lambda sv=sv, ps=ps: nc.scalar.copy(out=sv[:], in_=ps[:])), [pb], [svB])
        sq, sqB = pl["sq"].get()
        P.op("act", f_act(nc, sq[:], ps[:], AF.Square), [pb], [sqB])
        ps2, pb2 = self.bank(4, 6)
        P.op("pe", f_mm(nc, ps2[:], self.ones_b, sq[:], True, True), [sqB, self.cB], [pb2])
        rs, rsB = pl["rs"].get()
        self.rsqrt(rs[:], rsB, ps2[:], pb2, 128.0 * EPS)
        qb, qbB = pl["qb"].get()
        P.op("dve", f_stt(V, qb[:], sv[:], gcol, rs[:], ALU.mult, ALU.mult), [svB, rsB, gB], [qbB])
        ps3, pb3 = self.bank(6, 8)
        P.op("pe", f_mm(nc, ps3[0:32, :], self.rotm[:, 0:32], qb[:], True, True), [qbB, self.cB], [pb3])
        t1, t1B = pl["t1"].get()
        P.op("pool", f_tt(G, t1[:], qb[0:32, :], C32[:, t0:t0 + 512], ALU.mult), [qbB, rB], [t1B])
        t2, t2B = pl["t2"].get()
        P.op("dve", f_tt(V, t2[:], ps3[0:32, :], S32[:, t0:t0 + 512], ALU.mult), [pb3, rB], [t2B])
        o, oB = pl["o"].get()
        P.op("pool", f_copy(G, o[:], qb[:]), [qbB], [oB])
        P.op("pool", f_tt(G, o[0:32, :], t1[:], t2[:], ALU.add), [t1B, t2B, oB], [oB])
        P.dma("sp", dst, o[:], reads=[oB])

    def st_kv(self, x_in):
        nc, P = self.nc, self.P
        with ExitStack() as es:
            rot = self.rot_tables(es)
            hT = sb(nc, es, "hT", [128, 16, T], BF16)
            hB = [Buf("hT%d" % i) for i in range(8)]
            with ExitStack() as es2:
                pools = self.norm_pools(es2, 1, 256)
                for tt in range(16):
                    self.norm_tile(pools, x_in, tt * 256, hT[:, :, tt * 256:(tt + 1) * 256], hB[tt // 2], 8, 256)
                P.flush()
            with ExitStack() as es2:
                wvp = sb_rot(nc, es2, "wv", 1, [128, 16, 512], BF16)
                vo = sb_rot(nc, es2, "vo", 4, [128, 512], BF16)
                for gi, dil in enumerate((1, 4, 16)):
                    wt, wb = wvp.get()
                    P.dma("pool", wt[:], self.wkvv[gi].rearrange("p (c n) -> p c n", c=16), writes=[wb])
                    nbr = 32 // dil
                    for blk in range(32):
                        r, nb = blk // nbr, blk % nbr
                        st_ = r + dil * 128 * nb
                        ps, pb = self.bank(0, 4)
                        for kc in range(16):
                            P.op("pe", f_mm(nc, ps[:], hT[:, kc, st_:st_ + dil * 127 + 1:dil], wt[:, kc, :], kc == 0, kc == 15),
                                 hB + [wb], [pb])
                        v, vB = vo.get()
                        P.op("act", (lambda v=v, ps=ps: nc.scalar.copy(out=v[:], in_=ps[:])), [pb], [vB])
                        P.dma("sp", self.Vb_s[gi, blk], v[:], reads=[vB])
                P.flush()
            with ExitStack() as es2:
                pl = self.qk_pools(es2)
                gk = sb(nc, es2, "gk", [128, 3], F32)
                gB = Buf("gk")
                P.dma("sp", gk[:], self.knorm[:, :], writes=[gB])
                P.op("dve", f_ts(nc.vector, gk[:], gk[:], float(np.sqrt(128.0)), None, ALU.mult), [gB], [gB])
                wp = sb_rot(nc, es2, "wk", 3, [128, 16, 128], BF16)
                for c in range(12):
                    wt, wb = wp.get()
                    P.dma("pool", wt[:], self.wkvk[c].rearrange("p (c n) -> p c n", c=16), writes=[wb])
                    for tt in range(8):
                        ps, pb = self.bank(0, 4)
                        for kc in range(16):
                            P.op("pe", f_mm(nc, ps[:], wt[:, kc, :], hT[:, kc, tt * 512:(tt + 1) * 512], kc == 0, kc == 15),
                                 [wb, hB[tt]], [pb])
                        self.qk_post(pl, ps, pb, gk[:, c // 4:c // 4 + 1], gB, rot, tt * 512,
                                     self.KT_s[c, :, tt * 512:(tt + 1) * 512])
                P.flush()

    def st_q(self, l, x_in):
        nc, P = self.nc, self.P
        j = l - 2
        with ExitStack() as es:
            rot = self.rot_tables(es)
            hT = sb(nc, es, "hT", [128, 16, T], BF16)
            hB = [Buf("hT%d" % i) for i in range(8)]
            with ExitStack() as es2:
                pools = self.norm_pools(es2, 1, 256)
                for tt in range(16):
                    self.norm_tile(pools, x_in, tt * 256, hT[:, :, tt * 256:(tt + 1) * 256], hB[tt // 2], l * 2, 256)
                P.flush()
            with ExitStack() as es2:
                pl = self.qk_pools(es2)
                gq = sb(nc, es2, "gq", [128, 3], F32)
                gB = Buf("gq")
                P.dma("sp", gq[:], self.qnorm[j], writes=[gB])
                P.op("dve", f_ts(nc.vector, gq[:], gq[:], float(np.sqrt(128.0)), None, ALU.mult), [gB], [gB])
                wp = sb_rot(nc, es2, "wq", 3, [128, 16, 128], BF16)
                for c in range(48):
                    wt, wb = wp.get()
                    P.dma("pool", wt[:], self.wq[j, c].rearrange("p (c n) -> p c n", c=16), writes=[wb])
                    for tt in range(8):
                        ps, pb = self.bank(0, 4)
                        for kc in range(16):
                            P.op("pe", f_mm(nc, ps[:], wt[:, kc, :], hT[:, kc, tt * 512:(tt + 1) * 512], kc == 0, kc == 15),
                                 [wb, hB[tt]], [pb])
                        self.qk_post(pl, ps, pb, gq[:, c // 16:c // 16 + 1], gB, rot, tt * 512,
                                     self.QT_s[c, :, tt * 512:(tt + 1) * 512])
                P.flush()

    def st_attn(self):
        nc, P = self.nc, self.P
        V, G = nc.vector, nc.gpsimd
        HALF = 2048
        sc_scale = float(128.0 ** -0.5)
        with ExitStack() as es:
            accO = sb(nc, es, "accO", [128, 4, HALF], F32)
            accD = sb(nc, es, "accD", [128, 4, HALF], F32)
            aB, dB_ = Buf("accO"), Buf("accD")
            Qp = sb_rot(nc, es, "aQ", 2, [128, 4, HALF], BF16)
            Kp = sb_rot(nc, es, "aK", 2, [128, T], BF16)
            Vp = sb_rot(nc, es, "aV", 2, [128, 32, 128], BF16)
            PTp = sb_rot(nc, es, "aPT", 3, [128, 2, 4, 128], BF16)
            aop = sb_rot(nc, es, "aao", 1, [128, 4, HALF], BF16)
            cB = self.cB
            mask = self.maskPC.rearrange("p (b q) -> p b q", b=2)
            unit = 0
            for kvh in range(4):
                for a in range(2):
                    for gi, dil in enumerate((1, 4, 16)):
                        Q, QB = Qp.get()
                        for hq in range(4):
                            P.dma("sp", Q[:, hq, :], self.QT_s[gi * 16 + kvh * 4 + hq, :, a * HALF:(a + 1) * HALF], writes=[QB])
                        Kt, KB = Kp.get()
                        P.dma("sp", Kt[:], self.KT_s[gi * 4 + kvh], writes=[KB])
                        Vt, VB = Vp.get()
                        P.dma("sp", Vt[:], self.Vb_s[gi, :, :, kvh * 128:(kvh + 1) * 128].rearrange("b p d -> p b d"), writes=[VB])
                        nbr = 32 // dil
                        nbh = nbr // 2
                        for r in range(dil):
                            for nbi in range(nbh):
                                nb = a * nbh + nbi
                                blk = r * nbr + nb
                                ql = r + dil * 128 * nbi
                                qsl = slice(ql, ql + dil * 127 + 1, dil)
                                kc0 = r + dil * 128 * nb
                                ksl = slice(kc0, kc0 + dil * 127 + 1, dil)
                                has_prev = nb > 0
                                di = unit % 2
                                unit += 1
                                sc, scB = self.psd[di], self.psB[di * 2]
                                qr = Q[:, :, qsl]
                                if has_prev:
                                    kp0 = kc0 - dil * 128
                                    P.op("pe", f_mm(nc, sc[:, 0:512], Kt[:, kp0:kp0 + dil * 127 + 1:dil], qr, True, True), [KB, QB], [scB])
                                P.op("pe", f_mm(nc, sc[:, 512:1024], Kt[:, ksl], qr, True, True), [KB, QB], [scB])
                                PT, PB = PTp.get()
                                lo = 0 if has_prev else 1
                                P.op("act", f_act(nc, PT[:, lo:2].rearrange("p b h q -> p (b h q)"), sc[:, lo * 512:1024], AF.Exp, scale=sc_scale),
                                     [scB], [PB])
                                eng, en = (V, "dve") if unit % 2 == 0 else (G, "pool")
                                mk = mask[:, lo:2, :].unsqueeze(2).to_broadcast([128, 2 - lo, 4, 128])
                                P.op(en, f_tt(eng, PT[:, lo:2], PT[:, lo:2], mk, ALU.mult), [PB, cB], [PB])
                                ops_, opB = self.psb[4 + di * 2], self.psB[4 + di * 2]
                                dps_, dpB = self.psb[5 + di * 2], self.psB[5 + di * 2]
                                cur = PT[:, 1].rearrange("p h q -> p (h q)")
                                if has_prev:
                                    prv = PT[:, 0].rearrange("p h q -> p (h q)")
                                    P.op("pe", f_mm(nc, ops_[:], Vt[:, blk - 1, :], prv, True, False), [VB, PB], [opB])
                                    P.op("pe", f_mm(nc, ops_[:], Vt[:, blk, :], cur, False, True), [VB, PB], [opB])
                                    P.op("pe", f_mm(nc, dps_[:], self.ones_b, prv, True, False), [cB, PB], [dpB])
                                    P.op("pe", f_mm(nc, dps_[:], self.ones_b, cur, False, True), [cB, PB], [dpB])
                                else:
                                    P.op("pe", f_mm(nc, ops_[:], Vt[:, blk, :], cur, True, True), [VB, PB], [opB])
                                    P.op("pe", f_mm(nc, dps_[:], self.ones_b, cur, True, True), [cB, PB], [dpB])
                                ov = accO[:, :, qsl]
                                dv = accD[:, :, qsl]
                                o3 = ops_.rearrange("p (h q) -> p h q", h=4)
                                d3 = dps_.rearrange("p (h q) -> p h q", h=4)
                                if gi == 0:
                                    P.op("act", (lambda ov=ov, o3=o3: nc.scalar.copy(out=ov, in_=o3)), [opB], [aB])
                                    P.op("dve", f_copy(V, dv, d3), [dpB], [dB_])
                                else:
                                    P.op("dve", f_tt(V, ov, o3, ov, ALU.add), [opB, aB], [aB])
                                    P.op("dve", f_tt(V, dv, d3, dv, ALU.add), [dpB, dB_], [dB_])
                    P.op("dve", (lambda: nc.vector.reciprocal(out=accD[:], in_=accD[:])), [dB_], [dB_])
                    ao, aoB = aop.get()
                    P.op("pool", f_tt(G, ao[:], accO[:], accD[:], ALU.mult), [aB, dB_], [aoB])
                    for hq in range(4):
                        h = kvh * 4 + hq
                        P.dma("sp", self.aoT_s[h * 128:(h + 1) * 128, a * HALF:(a + 1) * HALF], ao[:, hq, :], reads=[aoB])
            P.flush()


def build(dbg=False, stages=None):
    k = K(dbg)
    k.declare()
    nc, P = k.nc, k.P
    allst = stages is None

    def want(s):
        return allst or s in stages

    with ExitStack() as es0:
        k.load_consts(es0)
        k.st_mods()
        for l in range(2):
            x_in = k.xT if l == 0 else k.yT
            if want("gdn%d" % l):
                with ExitStack() as esl:
                    bg = sb(nc, esl, "bg", [128, 32, 64], F32)
                    bgB = Buf("bg")
                    if want("gproj%d" % l):
                        k.st_gdn_proj(l, x_in, bg, bgB)
                    if want("gchunk%d" % l):
                        k.st_gdn_chunk(l, bg, bgB, n_chunks=(k.n_chunks if hasattr(k, "n_chunks") else 32))
            if want("gout%d" % l):
                k.st_proj_resid(k.ogT_s, 32, k.wout[l], k.gate(l, 0), x_in, k.yT)
            if want("mlp%d" % l):
                k.st_mlp(l, k.yT, k.yT)
        if (not allst) and "copyin" in stages:
            for c in range(16):
                P.dma("sp", k.yT[c * 128:(c + 1) * 128, :], k.xT[c * 128:(c + 1) * 128, :])
            P.flush()
        if want("kv"):
            k.st_kv(k.yT)
        for l in range(2, 4):
            if want("q%d" % l):
                k.st_q(l, k.yT)
            if want("attn%d" % l):
                k.st_attn()
            if want("aout%d" % l):
                k.st_proj_resid(k.aoT_s, 16, k.wo[l - 2], k.gate(l, 0), k.yT, k.yT)
            if want("mlp%d" % l):
                k.st_mlp(l, k.yT, k.yT)
        P.flush()
        P.barrier()
    return k


def tile_w(W, ncols):
    Kd, N = W.shape
    return np.ascontiguousarray(
        W.reshape(Kd // 128, 128, N // ncols, ncols).transpose(2, 1, 0, 3)).reshape(N // ncols, 128, (Kd // 128) * ncols)


def make_consts():
    import ml_dtypes
    i = np.arange(128)
    k_, m_ = i[:, None], i[None, :]
    ident = np.eye(128, dtype=np.float32)
    U = (k_ <= m_).astype(np.float32)
    SL = (k_ > m_).astype(np.float32)
    SU = (k_ < m_).astype(np.float32)
    BD32 = ((k_ // 32) == (m_ // 32)).astype(np.float32)
    rb, cb = k_ // 32, m_ // 32
    OFF1T = (((rb == 1) & (cb == 0)) | ((rb == 3) & (cb == 2))).astype(np.float32)
    OFF2T = ((rb >= 2) & (cb < 2)).astype(np.float32)
    ones = np.ones((128, 128), np.float32)
    z = np.zeros((128, 128), np.float32)
    c_f32 = np.concatenate([ident, U, SL, SU, BD32, OFF1T, OFF2T, ones, z, z], axis=1)
    rotm = np.zeros((128, 128), np.float32)
    for m in range(16):
        rotm[m + 16, m] = -1.0
        rotm[m, m + 16] = 1.0
    maskP = (k_ >= m_).astype(np.float32)
    maskC = (k_ <= m_).astype(np.float32)
    c_bf = np.concatenate([ident, ones, rotm, maskP, maskC], axis=1).astype(ml_dtypes.bfloat16)
    inv = (500000.0 ** (-np.arange(0, 32, 2, dtype=np.float32) / 32.0)).astype(np.float32)
    invf = np.zeros((128, 1), np.float32)
    invf[0:32, 0] = np.concatenate([inv, inv])
    return dict(c_f32=np.ascontiguousarray(c_f32), c_bf=np.ascontiguousarray(c_bf), c_invf=invf)


def prep_shared(inp):
    f = lambda a: np.ascontiguousarray(np.asarray(a, dtype=np.float32))
    sh = {}
    ada_w = f(inp["ada_w"])
    sh["ada_w_t"] = np.stack([tile_w(ada_w[l], 512) for l in range(4)])
    sh["ada_b_t"] = np.stack([f(inp["ada_b"][l]).reshape(96, 128).T for l in range(4)]).copy()
    sh["kvada_w_t"] = tile_w(f(inp["kv_ada_w"]), 512)
    sh["kvada_b_t"] = f(inp["kv_ada_b"]).reshape(32, 128).T.copy()
    ng = f(inp["norm_g"])
    sh["normg_t"] = np.ascontiguousarray(ng.reshape(4, 2, 16, 128).transpose(0, 1, 3, 2))
    sh["kvnormg_t"] = f(inp["kv_norm_g"]).reshape(16, 128).T.copy()
    w1, w2 = f(inp["mlp_w1"]), f(inp["mlp_w2"])
    sh["mlp_w1_t"] = np.stack([tile_w(w1[l], 128) for l in range(4)])
    sh["mlp_w2_t"] = np.stack([np.stack([tile_w(w2[l][h * 4096:(h + 1) * 4096], 128) for h in range(2)], axis=1)
                               for l in range(4)])
    win = f(inp["gdn_w_in"])
    sh["gdn_win_fm"] = np.stack([tile_w(win[l][:, 0:8192], 128) for l in range(2)])
    sh["gdn_wz_t"] = np.stack([tile_w(win[l][:, 8192:12288], 512) for l in range(2)])
    sh["gdn_wba_t"] = np.stack([tile_w(win[l][:, 12288:12352], 64)[0] for l in range(2)])
    cw = f(inp["gdn_conv_w"])
    sh["gdn_conv_t"] = np.ascontiguousarray(cw.reshape(2, 4, 64, 128).transpose(0, 3, 2, 1)).reshape(2, 128, 256)
    sh["gdn_alog_b"] = np.ascontiguousarray(np.broadcast_to(f(inp["gdn_a_log"])[:, None, :], (2, 128, 32)))
    sh["gdn_dtb_b"] = np.ascontiguousarray(np.broadcast_to(f(inp["gdn_dt_bias"])[:, None, :], (2, 128, 32)))
    sh["gdn_onorm_b"] = np.ascontiguousarray(np.broadcast_to(f(inp["gdn_onorm_g"])[:, None, :], (2, 128, 128)))
    wout = f(inp["gdn_w_out"])
    sh["gdn_wout_t"] = np.stack([tile_w(wout[l], 128) for l in range(2)])
    wkv = f(inp["w_kv"])
    sh["wkv_k_t"] = np.stack([tile_w(wkv[:, gi * 1024 + h * 128: gi * 1024 + (h + 1) * 128], 128)[0]
                              for gi in range(3) for h in range(4)])
    sh["wkv_v_t"] = np.stack([tile_w(wkv[:, gi * 1024 + 512: gi * 1024 + 1024], 512)[0] for gi in range(3)])
    sh["knorm_t"] = f(inp["k_norm_g"]).T.copy()
    sh["qnorm_t"] = np.ascontiguousarray(f(inp["q_norm_g"]).transpose(0, 2, 1))
    wq, wo = f(inp["attn_w_q"]), f(inp["attn_w_o"])
    sh["attn_wq_t"] = np.stack([tile_w(wq[j], 128) for j in range(2)])
    sh["attn_wo_t"] = np.stack([tile_w(wo[j], 128) for j in range(2)])
    sh.update(make_consts())
    return sh


def prep_core(inp, b):
    d = {}
    d["xT"] = np.ascontiguousarray(np.asarray(inp["x"][b], dtype=np.float32).T)
    d["ccol"] = np.ascontiguousarray(np.asarray(inp["c"][b], dtype=np.float32).reshape(16, 128).T)
    d["pos"] = np.ascontiguousarray(np.asarray(inp["positions"][b], dtype=np.int32)[None, :])
    return d


def kernel(**inputs):
    k = build()
    shared = prep_shared(inputs)
    in_maps = []
    for b in range(NCORES):
        m = dict(shared)
        m.update(prep_core(inputs, b))
        in_maps.append(m)
    res = run_bass_kernel_spmd(k.nc, in_maps, core_ids=list(range(NCORES)))
    out = np.stack([np.ascontiguousarray(np.asarray(r["yT"]).T) for r in res.results])
    return out.astype(np.float32)
```
